# Optimizing a Trainium2 kernel written in Bass

```python
import math
import jax, jax.numpy as jnp
from jax import lax
import numpy as np

D_MODEL = 1024
BATCH = 8
SEQ = 8192
DEPTH = 2

N_A = DEPTH // 2
N_B = DEPTH - N_A
D_FF = 2816
N_NORMS = 6
D_RNN = D_MODEL
N_RNN_BLOCKS = 16
RNN_BW = D_RNN // N_RNN_BLOCKS
CONV_W = 4
C_RGLRU = 8.0
HEAD_DIM = 64
N_HEADS = D_MODEL // HEAD_DIM
N_KV_HEADS = 4
GROUP = N_HEADS // N_KV_HEADS
WINDOW = 128
BLK = 128
EPS = 1e-6
NEG = -1e30

kernel_name = "yoco_rglru_swa_sinks_macaron"


def rms_norm(x, g):
    xf = x.astype(jnp.float32)
    y = xf * lax.rsqrt(jnp.mean(xf * xf, axis=-1, keepdims=True) + EPS)
    return (y * g.astype(jnp.float32)).astype(x.dtype)


def swiglu(x, w_gate, w_up, w_down):
    return (jax.nn.silu(x @ w_gate) * (x @ w_up)) @ w_down


def causal_depthwise_conv(x, w, b):
    s = x.shape[1]
    xp = jnp.pad(x, ((0, 0), (CONV_W - 1, 0), (0, 0)))
    y = b
    for k in range(CONV_W):
        y = y + xp[:, k:k + s] * w[k]
    return y


def block_diag_linear(x, w, b):
    bsz, s, _ = x.shape
    xb = x.reshape(bsz, s, N_RNN_BLOCKS, RNN_BW)
    return jnp.einsum('bshi,hij->bshj', xb, w).reshape(bsz, s, D_RNN) + b


def _lin_rec_combine(c1, c2):
    a1, b1 = c1
    a2, b2 = c2
    return a1 * a2, a2 * b1 + b2


def rg_lru(x, w_ga, b_ga, w_gx, b_gx, lam):
    r = jax.nn.sigmoid(block_diag_linear(x, w_ga, b_ga).astype(jnp.float32))
    i = jax.nn.sigmoid(block_diag_linear(x, w_gx, b_gx).astype(jnp.float32))
    log_a = -C_RGLRU * r * jax.nn.softplus(-lam.astype(jnp.float32))
    a = jnp.exp(log_a)
    mult = jnp.sqrt(-jnp.expm1(2.0 * log_a))
    u = mult * i * x.astype(jnp.float32)
    _, h = lax.associative_scan(_lin_rec_combine, (a, u), axis=1)
    return h.astype(x.dtype)


def recurrent_block(x, w_in, conv_w, conv_b, w_ga, b_ga, w_gx, b_gx, lam, w_out):
    u = x @ w_in
    xr, gate = u[..., :D_RNN], u[..., D_RNN:]
    xr = causal_depthwise_conv(xr, conv_w, conv_b)
    h = rg_lru(xr, w_ga, b_ga, w_gx, b_gx, lam)
    return (h * jax.nn.gelu(gate)) @ w_out


def shared_kv(x, kv_norm, w_kv):
    bsz, s, _ = x.shape
    kv = rms_norm(x, kv_norm) @ w_kv
    k = kv[..., :N_KV_HEADS * HEAD_DIM].reshape(bsz, s, N_KV_HEADS, HEAD_DIM)
    v = kv[..., N_KV_HEADS * HEAD_DIM:].reshape(bsz, s, N_KV_HEADS, HEAD_DIM)
    return k, v


def _band(t):
    bsz, s = t.shape[0], t.shape[1]
    nb = s // BLK
    tp = jnp.pad(t, ((0, 0), (BLK, 0), (0, 0), (0, 0)))
    tb = tp.reshape(bsz, nb + 1, BLK, N_KV_HEADS, HEAD_DIM)
    return jnp.concatenate([tb[:, :-1], tb[:, 1:]], axis=2)


def swa_sinks_attention(x, w_q, sinks, k, v, w_o):
    bsz, s, _ = x.shape
    nb = s // BLK
    q = (x @ w_q).reshape(bsz, nb, BLK, N_KV_HEADS, GROUP, HEAD_DIM)
    kb, vb = _band(k), _band(v)
    scores = jnp.einsum('bnqkgd,bnjkd->bnkgqj', q, kb).astype(jnp.float32) * (HEAD_DIM ** -0.5)
    qi = jnp.arange(BLK)[:, None]
    kj = jnp.arange(2 * BLK)[None, :]
    delta = qi + BLK - kj
    blk = jnp.arange(nb)[:, None, None]
    valid = (delta >= 0) & (delta < WINDOW) & (blk * BLK + kj - BLK >= 0)
    scores = jnp.where(valid[None, :, None, None], scores, NEG)
    sink = sinks.astype(jnp.float32).reshape(1, 1, N_KV_HEADS, GROUP, 1, 1)
    m = jnp.maximum(jnp.max(scores, axis=-1, keepdims=True), sink)
    p = jnp.exp(scores - m)
    probs = p / (jnp.sum(p, axis=-1, keepdims=True) + jnp.exp(sink - m))
    o = jnp.einsum('bnkgqj,bnjkd->bnqkgd', probs.astype(vb.dtype), vb)
    return o.reshape(bsz, s, N_HEADS * HEAD_DIM) @ w_o


def setup_inputs(seed: int = 0) -> dict:
    key = jax.random.key(seed)
    ks = jax.random.split(key, 24)
    f32 = jnp.float32

    def nrm(k, shape, fan_in):
        return jax.random.normal(k, shape, f32) * (fan_in ** -0.5)

    x = jax.random.normal(ks[0], (BATCH, SEQ, D_MODEL), f32)
    norms = 1.0 + 0.05 * jax.random.normal(ks[1], (DEPTH, N_NORMS, D_MODEL), f32)
    ffn_w_gate = nrm(ks[2], (DEPTH, 2, D_MODEL, D_FF), D_MODEL)
    ffn_w_up = nrm(ks[3], (DEPTH, 2, D_MODEL, D_FF), D_MODEL)
    ffn_w_down = nrm(ks[4], (DEPTH, 2, D_FF, D_MODEL), D_FF)
    a_w_in = nrm(ks[5], (N_A, D_MODEL, 2 * D_RNN), D_MODEL)
    a_conv_w = nrm(ks[6], (N_A, CONV_W, D_RNN), CONV_W)
    a_conv_b = 0.02 * jax.random.normal(ks[7], (N_A, D_RNN), f32)
    a_gate_a_w = nrm(ks[8], (N_A, N_RNN_BLOCKS, RNN_BW, RNN_BW), RNN_BW)
    a_gate_a_b = 0.02 * jax.random.normal(ks[9], (N_A, D_RNN), f32)
    a_gate_x_w = nrm(ks[10], (N_A, N_RNN_BLOCKS, RNN_BW, RNN_BW), RNN_BW)
    a_gate_x_b = 0.02 * jax.random.normal(ks[11], (N_A, D_RNN), f32)
    a_base = jax.random.uniform(ks[12], (N_A, D_RNN), f32, 0.9, 0.999)
    a_lambda = jnp.log(a_base) - jnp.log1p(-a_base)
    a_w_out = nrm(ks[13], (N_A, D_RNN, D_MODEL), D_RNN)
    kv_norm = 1.0 + 0.05 * jax.random.normal(ks[14], (D_MODEL,), f32)
    w_kv = nrm(ks[15], (D_MODEL, 2 * N_KV_HEADS * HEAD_DIM), D_MODEL)
    b_w_q = nrm(ks[16], (N_B, D_MODEL, N_HEADS * HEAD_DIM), D_MODEL)
    b_sinks = jax.random.normal(ks[17], (N_B, N_HEADS), f32)
    b_w_o = nrm(ks[18], (N_B, N_HEADS * HEAD_DIM, D_MODEL), N_HEADS * HEAD_DIM)
    return {"x": x, "norms": norms, "ffn_w_gate": ffn_w_gate, "ffn_w_up": ffn_w_up,
            "ffn_w_down": ffn_w_down, "a_w_in": a_w_in, "a_conv_w": a_conv_w,
            "a_conv_b": a_conv_b, "a_gate_a_w": a_gate_a_w, "a_gate_a_b": a_gate_a_b,
            "a_gate_x_w": a_gate_x_w, "a_gate_x_b": a_gate_x_b, "a_lambda": a_lambda,
            "a_w_out": a_w_out, "kv_norm": kv_norm, "w_kv": w_kv, "b_w_q": b_w_q,
            "b_sinks": b_sinks, "b_w_o": b_w_o}


def reference(x, norms, ffn_w_gate, ffn_w_up, ffn_w_down, a_w_in, a_conv_w, a_conv_b,
              a_gate_a_w, a_gate_a_b, a_gate_x_w, a_gate_x_b, a_lambda, a_w_out,
              kv_norm, w_kv, b_w_q, b_sinks, b_w_o):
    k_sh = v_sh = None
    for l in range(DEPTH):
        g = norms[l]
        f = swiglu(rms_norm(x, g[0]), ffn_w_gate[l, 0], ffn_w_up[l, 0], ffn_w_down[l, 0])
        x = x + 0.5 * rms_norm(f, g[1])
        h = rms_norm(x, g[2])
        if l < N_A:
            mix = recurrent_block(h, a_w_in[l], a_conv_w[l], a_conv_b[l],
                                  a_gate_a_w[l], a_gate_a_b[l], a_gate_x_w[l], a_gate_x_b[l],
                                  a_lambda[l], a_w_out[l])
        else:
            j = l - N_A
            mix = swa_sinks_attention(h, b_w_q[j], b_sinks[j], k_sh, v_sh, b_w_o[j])
        x = x + rms_norm(mix, g[3])
        f = swiglu(rms_norm(x, g[4]), ffn_w_gate[l, 1], ffn_w_up[l, 1], ffn_w_down[l, 1])
        x = x + 0.5 * rms_norm(f, g[5])
        if l == N_A - 1:
            k_sh, v_sh = shared_kv(x, kv_norm, w_kv)
    return x
```

```python
import contextlib
import os
KDBG = set(os.environ.get('KDBG', '').split(','))
import numpy as np
import concourse.bass as bass
import concourse.mybir as mybir
from concourse.bass_utils import run_bass_kernel_spmd

F32 = mybir.dt.float32
BF16 = mybir.dt.bfloat16
ALU = mybir.AluOpType
AF = mybir.ActivationFunctionType
AX = mybir.AxisListType

D = 1024
KC = 8
DFF = 2816
FC = 22
T = 1024
NB = 8
GT = 512
NG = 2
EPS = 1e-6
NCORES = 8
SLOT = [0, 2, 1, 3]
SEQ = 8192


class Tok:
    __slots__ = ("sem", "val")

    def __init__(self, sem, val):
        self.sem = sem
        self.val = val


class Buf:
    __slots__ = ("name", "w", "r", "excl")

    def __init__(self, name, excl=False):
        self.name = name
        self.w = None
        self.r = []
        self.excl = excl


class DSem:
    __slots__ = ("sem", "cnt")

    def __init__(self, sem):
        self.sem = sem
        self.cnt = 0


class Eng:
    def __init__(self, name):
        self.name = name
        self.ops = []
        self.sem = None
        self.cnt = 0
        self.waited = {}

    def set_sem(self, sem):
        self.sem = sem
        self.cnt = 0


class Prog:
    def __init__(self, nc):
        self.nc = nc
        self.pe = Eng("tensor")
        self.act = Eng("scalar")
        self.dve = Eng("vector")
        self.pool = Eng("gpsimd")
        self.sp = Eng("sync")
        self.engs = [self.pe, self.act, self.dve, self.pool, self.sp]

    def _deps(self, eng, reads, writes, extra=()):
        best = {}

        def add(t):
            k = id(t.sem)
            if k not in best or best[k].val < t.val:
                best[k] = t
        for b in reads:
            if b.w is not None:
                add(b.w)
            if b.excl:
                for t in b.r:
                    if t.sem is not eng.sem:
                        add(t)
        for b in writes:
            if b.w is not None:
                add(b.w)
            for t in b.r:
                add(t)
        for t in extra:
            add(t)
        for k, t in best.items():
            if eng.waited.get(k, 0) < t.val:
                eng.ops.append(("wait", t.sem, t.val))
                eng.waited[k] = t.val

    @staticmethod
    def _mark(tok, reads, writes):
        for b in reads:
            b.r = [t for t in b.r if t.sem is not tok.sem] + [tok]
        for b in writes:
            b.w = tok
            b.r = []

    def op(self, eng, fn, reads=(), writes=(), extra=()):
        self._deps(eng, reads, writes, extra)
        eng.cnt += 1
        tok = Tok(eng.sem, eng.cnt)
        eng.ops.append(("op", fn, eng.sem, 1))
        self._mark(tok, reads, writes)
        return tok

    def dma(self, eng, fn, dsem, reads=(), writes=(), extra=(), deps=True):
        if deps:
            self._deps(eng, reads, writes, extra)
        dsem.cnt += 16
        tok = Tok(dsem.sem, dsem.cnt)
        eng.ops.append(("op", fn, dsem.sem, 16))
        self._mark(tok, reads, writes)
        return tok

    def wait(self, eng, toks):
        self._deps(eng, (), (), toks)

    def finish(self, block):
        def run(e, ops):
            for o in ops:
                if o[0] == "wait":
                    e.wait_ge(o[1], o[2])
                else:
                    ins = o[1](e)
                    ins.then_inc(o[2], o[3])

        pe, act, dve, pool, sp = self.pe, self.act, self.dve, self.pool, self.sp

        @block.tensor
        def _(e):
            run(e, pe.ops)

        @block.scalar
        def _(e):
            run(e, act.ops)

        @block.vector
        def _(e):
            run(e, dve.ops)

        @block.gpsimd
        def _(e):
            run(e, pool.ops)

        @block.sync
        def _(e):
            run(e, sp.ops)


def build_program(S, stop_after=None):
    npass = S // T
    nc = bass.Bass("TRN2", target_bir_lowering=False)

    def din(name, shape, dt=F32):
        return nc.dram_tensor(name, list(shape), dt, kind="ExternalInput").ap()

    def dscr(name, shape, dt=BF16):
        return nc.dram_tensor(name, list(shape), dt, kind="Internal").ap()

    x_d = din("x", [S, D])
    out_d = nc.dram_tensor("out", [S, D], F32, kind="ExternalOutput").ap()
    wg_d = din("ffn_w_gate", [2, 2, D, DFF])
    wu_d = din("ffn_w_up", [2, 2, D, DFF])
    wd_d = din("ffn_w_down", [2, 2, DFF, D])
    win_d = din("a_w_in", [D, 2 * D])
    wout_d = din("a_w_out", [D, D])
    wkv_d = din("w_kv", [D, 512])
    wq_d = din("b_w_q", [D, D])
    wo_d = din("b_w_o", [D, D])
    gaw_d = din("a_gate_a_w", [2, 64, 8, 64])
    gxw_d = din("a_gate_x_w", [2, 64, 8, 64])
    prefm_d = din("prenorm_fm", [128, 7 * 8])
    postn_d = din("postnorm", [6, D])
    recfm_d = din("rec_fm", [128, 8 * 8])
    sinks_d = din("sinks", [16])
    ident_d = din("ident", [128, 128])
    mask_d = din("mask", [128, 2 * 256])

    WgS = [[dscr(f"WgS{l}{i}", [FC, 128, 2, KC, 128]) for i in range(2)] for l in range(2)]
    WdS = [[dscr(f"WdS{l}{i}", [DFF, D]) for i in range(2)] for l in range(2)]
    WinS = dscr("WinS", [8, 128, 2, KC, 128])
    WqS = dscr("WqS", [4, 128, 2, KC, 128])
    WkS = dscr("WkS", [2, 128, 2, KC, 128])
    WoutS = dscr("WoutS", [D, D])
    WoS = dscr("WoS", [D, D])
    WvS = dscr("WvS", [D, 256])

    with contextlib.ExitStack() as es:
        def sb(name, shape, dt):
            return es.enter_context(nc.sbuf_tensor("sb_" + name, list(shape), dt))

        def psum(name, shape, dt):
            return es.enter_context(nc.psum_tensor(name, list(shape), dt))

        def sem(name):
            return es.enter_context(nc.semaphore(name))

        X = sb("X", [128, NB, D], F32)
        xnT = sb("xnT", [128, KC, T], BF16)
        hT = sb("hT", [128, FC, T], BF16)
        BT = sb("BT", [128, FC, D], BF16)
        AR = sb("AR", [128, 3, 2, KC, 128], BF16)
        gain = sb("gain", [128, D], F32)
        kT = sb("kT", [128, 4, T + 128], BF16)
        vpad = sb("vpad", [128, NB + 1, 4, 192], BF16)
        wv = sb("wv", [128, KC, 256], BF16)
        bd = sb("bd", [128, 2, KC, 128], BF16)
        ident = sb("ident", [128, 128], BF16)
        mask = sb("mask", [128, 2, 256], F32)
        mask_bf = sb("mask_bf", [128, 2, 256], BF16)
        prefm = sb("prefm", [128, 7, 8], F32)
        recfm = sb("recfm", [128, 8, 8], F32)
        der = sb("der", [128, 6, 8], F32)
        sinks = sb("sinks", [128, 16], F32)
        half_c = sb("half_c", [128, GT], F32)
        mhalf_c = sb("mhalf_c", [128, 1], F32)
        hstate = sb("hstate", [128, 8], F32)
        cstate = sb("cstate", [128, 8, 3], F32)
        xr_sb = sb("xr_sb", [128, 4, GT + 3], F32)
        xs_bf = sb("xs_bf", [128, 2, D], BF16)
        junk = sb("junk", [128, D], BF16)
        ffn_t = sb("ffn_t", [128, 2, GT], F32)
        stat = sb("stat", [128, 4, 16], F32)
        astat = sb("astat", [128, 4, 8, 4], F32)

        PS = [psum(f"ps{i}", [128, 2, GT], F32) for i in range(4)]

        def bank(i):
            return PS[i // 2][:, i % 2, :]

        def bank_bf(i):
            return PS[i // 2][:, i % 2, :].bitcast(BF16)

        P = Prog(nc)
        bX = [Buf(f"X{b}") for b in range(NB)]
        bxn = [Buf(f"xn{g}") for g in range(NG)]
        bh = [Buf(f"h{r}") for r in range(FC)]
        bB = [Buf(f"B{r}") for r in range(FC)]
        bA = [Buf(f"A{r}") for r in range(3)]
        bgain = Buf("gain")
        bkT = [Buf(f"kT{h}") for h in range(4)]
        bvp = [Buf(f"vpad{i}") for i in range(NB + 1)]
        bwv = Buf("wv")
        bbd = Buf("bd")
        bconst = Buf("const")
        bpb = [Buf(f"pb{i}", excl=True) for i in range(8)]
        bxr = [Buf("xr0"), Buf("xr1"), Buf("xr2"), Buf("xr3")]
        bxs = [Buf("xs0"), Buf("xs1")]
        bjunk = Buf("junk")
        bft = [Buf("ft0"), Buf("ft1")]
        bstat = [Buf(f"st{i}") for i in range(16)]
        bastat = [Buf("as0"), Buf("as1"), Buf("as2"), Buf("as3")]
        bhst = Buf("hstate")
        bcst = Buf("cstate")

        def new_engine_sems(tag):
            for e in P.engs:
                e.set_sem(sem(f"e_{e.name}_{tag}"))
        new_engine_sems("pro")
        dA = [DSem(sem(f"dA{i}")) for i in range(3)]
        dB = [DSem(sem(f"dB{i}")) for i in range(3)]
        dgain = DSem(sem("dgain"))
        dXl = [DSem(sem(f"dXl{b}")) for b in range(NB)]
        dXs = [DSem(sem(f"dXs{b}")) for b in range(NB)]
        dconst = DSem(sem("dconst"))
        dwv = DSem(sem("dwv"))

        P.dma(P.sp, lambda e: e.dma_start(out=mask[:].rearrange("p a b -> p (a b)"), in_=mask_d), dconst, writes=[bconst])
        P.dma(P.sp, lambda e: e.dma_start(out=prefm[:].rearrange("p a b -> p (a b)"), in_=prefm_d), dconst, writes=[bconst], deps=False)
        P.dma(P.sp, lambda e: e.dma_start(out=recfm[:].rearrange("p a b -> p (a b)"), in_=recfm_d), dconst, writes=[bconst], deps=False)
        P.dma(P.sp, lambda e: e.dma_start(out=sinks[:], in_=sinks_d.partition_broadcast(128)), dconst, writes=[bconst], deps=False)
        dident = DSem(sem("dident"))
        bident = Buf("ident")
        P.dma(P.pool, lambda e: e.dma_start(out=ident[:], in_=ident_d), dident, writes=[bident], deps=False)
        P.op(P.dve, lambda e: e.tensor_copy(out=mask_bf[:].rearrange("p a b -> p (a b)"), in_=mask[:].rearrange("p a b -> p (a b)")),
             reads=[bconst], writes=[bconst])
        P.op(P.dve, lambda e: e.memset(vpad[:].rearrange("p a b c -> p (a b c)"), 0.0), writes=bvp)
        P.op(P.dve, lambda e: e.memset(kT[:].rearrange("p a b -> p (a b)"), 0.0), writes=bkT)
        P.op(P.dve, lambda e: e.memset(bd[:].rearrange("p a b c -> p (a b c)"), 0.0), writes=[bbd])
        P.op(P.dve, lambda e: e.memset(hstate[:], 0.0), writes=[bhst])
        P.op(P.dve, lambda e: e.memset(cstate[:].rearrange("p a b -> p (a b)"), 0.0), writes=[bcst])
        bexp = Buf("expc")
        P.op(P.dve, lambda e: e.memset(half_c[:], 0.5), writes=[bexp])
        P.op(P.dve, lambda e: e.memset(mhalf_c[:], -0.5), writes=[bexp])
        dbd = DSem(sem("dbd"))
        for gi, src in enumerate((gaw_d, gxw_d)):
            for half in range(2):
                P.dma(P.pool, lambda e, gi=gi, src=src, half=half: e.dma_start(
                    out=bd[half * 64:(half + 1) * 64, gi, :, half * 64:(half + 1) * 64], in_=src[half]),
                    dbd, writes=[bbd], deps=(gi == 0 and half == 0))
        bder = Buf("der")
        P.op(P.dve, lambda e: e.tensor_scalar(out=der[:, 0, :], in0=recfm[:, 5, :], scalar1=0.5, scalar2=None, op0=ALU.mult),
             reads=[bconst], writes=[bder])
        P.op(P.dve, lambda e: e.tensor_scalar(out=der[:, 1, :], in0=recfm[:, 6, :], scalar1=0.5, scalar2=None, op0=ALU.mult),
             reads=[bconst], writes=[bder])
        P.op(P.act, lambda e: e.activation(out=der[:, 4, :], in_=recfm[:, 7, :], func=AF.Exp, scale=-1.0),
             reads=[bconst], writes=[bder])
        P.op(P.act, lambda e: e.activation(out=der[:, 5, :], in_=der[:, 4, :], func=AF.Ln, bias=1.0),
             reads=[bder], writes=[bder])
        P.op(P.dve, lambda e: e.tensor_scalar(out=der[:, 2, :], in0=der[:, 5, :], scalar1=-8.0, scalar2=None, op0=ALU.mult),
             reads=[bder], writes=[bder])
        P.op(P.dve, lambda e: e.tensor_scalar(out=der[:, 3, :], in0=der[:, 5, :], scalar1=-4.0, scalar2=None, op0=ALU.mult),
             reads=[bder], writes=[bder])

        def cast_group(name, items):
            ds = DSem(sem("dc_" + name))
            b = Buf("scr_" + name)
            for (o, i) in items:
                if 'nocast' in KDBG:
                    continue
                P.dma(P.pool, lambda e, o=o, i=i: e.dma_start(out=o, in_=i), ds, writes=[b], deps=False)
            return b

        def a_items(dst, src, nchunk, col_of):
            it = []
            for c in range(nchunk):
                for h in range(2):
                    c0 = col_of(c, h)
                    it.append((dst[c, :, h], src[:, c0:c0 + 128].rearrange("(kc p) f -> p kc f", p=128)))
            return it

        def plain_items(dst, src, rows, piece):
            return [(dst[r0:r0 + piece, :], src[r0:r0 + piece, :]) for r0 in range(0, rows, piece)]

        scr = {}
        wg00_buf = {}

        def ffn_cast(l, i):
            it = []
            for c in range(FC):
                it.append((WgS[l][i][c, :, 0], wg_d[l, i][:, c * 128:(c + 1) * 128].rearrange("(kc p) f -> p kc f", p=128)))
                it.append((WgS[l][i][c, :, 1], wu_d[l, i][:, c * 128:(c + 1) * 128].rearrange("(kc p) f -> p kc f", p=128)))
            if (l, i) == (0, 0):
                bounds = [0, 3, 8, 15, FC]
                for gi in range(4):
                    bgrp = cast_group(f"Wg00_{gi}", it[2 * bounds[gi]:2 * bounds[gi + 1]])
                    for c in range(bounds[gi], bounds[gi + 1]):
                        wg00_buf[c] = bgrp
                scr["Wg00"] = bgrp
            else:
                scr[f"Wg{l}{i}"] = cast_group(f"Wg{l}{i}", it)
            scr[f"Wd{l}{i}"] = cast_group(f"Wd{l}{i}", plain_items(WdS[l][i], wd_d[l, i], DFF, 352))

        ffn_cast(0, 0)
        scr["Win"] = cast_group("Win", a_items(WinS, win_d, 8, lambda c, h: h * D + c * 128))
        scr["Wout"] = cast_group("Wout", plain_items(WoutS, wout_d, D, 256))
        ffn_cast(0, 1)
        kit = []
        for j in range(2):
            for h in range(2):
                for d2 in range(2):
                    c0 = (2 * j + h) * 64
                    kit.append((WkS[j, :, h, :, d2 * 64:(d2 + 1) * 64],
                                wkv_d[:, c0:c0 + 64].rearrange("(kc p) f -> p kc f", p=128)))
        scr["Wk"] = cast_group("Wk", kit)
        scr["Wv"] = cast_group("Wv", plain_items(WvS, wkv_d[:, 256:512], D, 256))
        ffn_cast(1, 0)
        scr["Wq"] = cast_group("Wq", a_items(WqS, wq_d, 4, lambda c, h: (2 * c + h) * 128))
        scr["Wo"] = cast_group("Wo", plain_items(WoS, wo_d, D, 256))
        ffn_cast(1, 1)


        a_list = []
        ns_ = 9 if stop_after is None else stop_after
        for p in range(npass):
            if ns_ >= 2:
                for c in range(FC):
                    a_list.append((WgS[0][0][c], wg00_buf[c]))
            if ns_ >= 3:
                for c in range(8):
                    a_list.append((WinS[c], scr["Win"]))
            if ns_ >= 4:
                for c in range(FC):
                    a_list.append((WgS[0][1][c], scr["Wg01"]))
            if ns_ >= 5:
                for c in range(2):
                    a_list.append((WkS[c], scr["Wk"]))
            if ns_ >= 7:
                for c in range(FC):
                    a_list.append((WgS[1][0][c], scr["Wg10"]))
            if ns_ >= 8:
                for c in range(4):
                    a_list.append((WqS[c], scr["Wq"]))
            if ns_ >= 9:
                for c in range(FC):
                    a_list.append((WgS[1][1][c], scr["Wg11"]))
        a_state = {"issued": 0, "next": 0}

        def a_issue_upto(j):
            while a_state["issued"] <= min(j, len(a_list) - 1):
                k = a_state["issued"]
                src, sbuf_ = a_list[k]
                slot = k % 3
                P.dma(P.sp, lambda e, src=src, slot=slot: e.dma_start(out=AR[:, slot], in_=src),
                      dA[slot], reads=[sbuf_], writes=[bA[slot]])
                a_state["issued"] += 1

        def a_next(pref=2):
            j = a_state["next"]
            a_issue_upto(j + pref)
            a_state["next"] += 1
            return j % 3

        def b_load(src, scrbuf, nrows_chunks):
            for g0 in range(0, nrows_chunks, 8):
                n = min(8, nrows_chunks - g0)
                gi = g0 // 8
                P.dma(P.sp, lambda e, g0=g0, n=n: e.dma_start(
                    out=BT[:, g0:g0 + n, :],
                    in_=src[g0 * 128:(g0 + n) * 128, :].rearrange("(fc p) m -> p fc m", p=128)),
                    dB[gi], reads=[scrbuf], writes=bB[g0:g0 + n])

        def gain_load(n):
            P.dma(P.sp, lambda e: e.dma_start(out=gain[:], in_=postn_d[n].partition_broadcast(128)),
                  dgain, writes=[bgain])

        ctr = {"stat": 0, "xs": 0, "tp": 0, "ft": 0, "fp": 0, "ev": 0}

        def stat_slot():
            s = ctr["stat"] % 16
            ctr["stat"] += 1
            return s

        def prenorm_a(b):
            s = stat_slot()
            par = ctr["xs"] % 2
            ctr["xs"] += 1
            P.op(P.act, lambda e: e.activation(out=junk[:], in_=X[:, b, :], func=AF.Square, accum_out=stat[:, 0, s:s + 1]),
                 reads=[bX[b]], writes=[bjunk, bstat[s]])
            P.op(P.act, lambda e: e.activation(out=stat[:, 1, s:s + 1], in_=stat[:, 0, s:s + 1], func=AF.Ln,
                                               scale=1.0 / D, bias=EPS), reads=[bstat[s]], writes=[bstat[s]])
            P.op(P.act, lambda e: e.activation(out=stat[:, 2, s:s + 1], in_=stat[:, 1, s:s + 1], func=AF.Exp, scale=-0.5),
                 reads=[bstat[s]], writes=[bstat[s]])
            P.op(P.act, lambda e: e.activation(out=xs_bf[:, par, :], in_=X[:, b, :], func=AF.Copy, scale=stat[:, 2, s:s + 1]),
                 reads=[bX[b], bstat[s]], writes=[bxs[par]])
            return par

        def prenorm_b(b, par, n):
            if 'nopre_b' in KDBG:
                return
            bk = ctr["tp"] % 4
            ctr["tp"] += 1
            tpv = bank_bf(bk).rearrange("p (k t) -> p k t", k=KC)

            def tr(e):
                for k in range(KC):
                    i = e.transpose(out=tpv[:, k, :], in_=xs_bf[:, par, k * 128:(k + 1) * 128], identity=ident[:])
                return i
            P.op(P.pe, tr, reads=[bxs[par], bident], writes=[bpb[bk]])
            g = b // 4
            P.op(P.dve, lambda e: e.tensor_tensor(out=xnT[:, :, b * 128:(b + 1) * 128], in0=tpv,
                                                  in1=prefm[:, n, :].unsqueeze(2).broadcast_to([128, KC, 128]), op=ALU.mult),
                 reads=[bpb[bk], bconst], writes=[bxn[g]])

        def prenorm_full(n):
            pend = None
            for b in range(NB):
                par = prenorm_a(b)
                if pend is not None:
                    prenorm_b(pend[0], pend[1], n)
                pend = (b, par)
            prenorm_b(pend[0], pend[1], n)

        def project_postnorm(lhs_of, lhs_bufs, nk, rhs_of, rhs_bufs, coef, next_n, final_store):
            pend = None
            for b in range(NB):
                fp = ctr["fp"] % 2
                ctr["fp"] += 1
                pst = PS[2 + fp]
                banks = [bpb[4 + 2 * fp], bpb[5 + 2 * fp]]

                def mm(e, b=b, pst=pst):
                    for half in range(2):
                        for k in range(nk):
                            i = e.matmul(pst[:, half, :], lhsT=lhs_of(k, b), rhs=rhs_of(k, half),
                                         start=(k == 0), stop=(k == nk - 1))
                    return i
                P.op(P.pe, mm, reads=list(lhs_bufs) + list(rhs_bufs), writes=banks)
                s = stat_slot()
                fv = pst[:].rearrange("p a b -> p (a b)")
                P.op(P.act, lambda e, fv=fv, s=s: e.activation(out=junk[:], in_=fv, func=AF.Square, accum_out=stat[:, 0, s:s + 1]),
                     reads=banks, writes=[bjunk, bstat[s]])
                c2 = 1.0 / (coef * coef)
                P.op(P.act, lambda e, s=s, c2=c2: e.activation(out=stat[:, 1, s:s + 1], in_=stat[:, 0, s:s + 1], func=AF.Ln,
                                                               scale=c2 / D, bias=EPS * c2), reads=[bstat[s]], writes=[bstat[s]])
                P.op(P.act, lambda e, s=s: e.activation(out=stat[:, 2, s:s + 1], in_=stat[:, 1, s:s + 1], func=AF.Exp, scale=-0.5),
                     reads=[bstat[s]], writes=[bstat[s]])
                P.op(P.dve, lambda e, fv=fv: e.tensor_tensor(out=fv, in0=fv, in1=gain[:], op=ALU.mult),
                     reads=banks + [bgain], writes=banks)
                P.op(P.dve, lambda e, fv=fv, s=s, b=b: e.scalar_tensor_tensor(out=X[:, b, :], in0=fv, scalar=stat[:, 2, s:s + 1],
                                                                              in1=X[:, b, :], op0=ALU.mult, op1=ALU.add),
                     reads=banks + [bstat[s], bX[b]], writes=[bX[b]])
                if final_store is not None:
                    final_store(b)
                if next_n is not None:
                    par = prenorm_a(b)
                    if pend is not None:
                        prenorm_b(pend[0], pend[1], next_n)
                    pend = (b, par)
            if pend is not None:
                pending_tr.append((pend[0], pend[1], next_n))

        pending_tr = []

        def flush_pending():
            while pending_tr:
                b_, par_, n_ = pending_tr.pop(0)
                prenorm_b(b_, par_, n_)

        def ffn(l, i, gain_n, next_n, final_store=None):
            b_load(WdS[l][i], scr[f"Wd{l}{i}"], FC)
            gain_load(gain_n)
            slot_of = {}

            def group(c, g):
                if c not in slot_of:
                    slot_of[c] = a_next(1 if c == 1 else 2)
                slot = slot_of[c]
                pp = ctr["ft"] % 2
                ctr["ft"] += 1
                pst = PS[pp]
                banks = [bpb[2 * pp], bpb[2 * pp + 1]]

                def mm(e):
                    for h in range(2):
                        for k in range(KC):
                            i_ = e.matmul(pst[:, h, :], lhsT=AR[:, slot, h, k, :], rhs=xnT[:, k, g * GT:(g + 1) * GT],
                                          start=(k == 0), stop=(k == KC - 1))
                    return i_
                P.op(P.pe, mm, reads=[bA[slot], bxn[g]], writes=banks)
                P.op(P.act, lambda e: e.activation(out=ffn_t[:, pp, :], in_=pst[:, 0, :], func=AF.Silu),
                     reads=[banks[0]], writes=[bft[pp]])
                P.op(P.dve, lambda e: e.tensor_tensor(
                    out=hT[:, c, g * GT:(g + 1) * GT], in0=ffn_t[:, pp, :], in1=pst[:, 1, :], op=ALU.mult),
                    reads=[bft[pp], banks[1]], writes=[bh[c]])

            group(0, 0)
            group(1, 0)
            flush_pending()
            group(0, 1)
            group(1, 1)
            for c in range(2, FC):
                for g in range(NG):
                    group(c, g)
            project_postnorm(lambda k, b: hT[:, k, b * 128:(b + 1) * 128], bh, FC,
                             lambda k, half: BT[:, k, half * GT:(half + 1) * GT], bB, 0.5, next_n, final_store)

        def row_ap(r):
            return hT[:, r, :] if r < FC else BT[:, 16 + (r - FC), :]

        def row_buf(r):
            return bh[r] if r < FC else bB[16 + (r - FC)]

        def hrow_f32(r):
            return row_ap(r).bitcast(F32)

        def run_pipelined(unit_lists, depth=2):
            items = []
            maxlen = max(len(o) for o in unit_lists)
            step = -(-maxlen // depth)
            for u, ops in enumerate(unit_lists):
                for k, th in enumerate(ops):
                    items.append((u * step + k, u, k, th))
            items.sort(key=lambda t: (t[0], t[1], t[2]))
            for _, _, _, th in items:
                th()

        def recurrent(gain_n, next_n, p=1):
            peng = P.dve if p == 0 else P.pool
            b_load(WoutS, scr["Wout"], 8)
            gain_load(gain_n)
            slots = {}

            def unit_ops(c, g, u):
                par = u % 2
                rs = u % 4
                prs = (u - 1) % 4
                R = [rs * 7 + r for r in range(7)]
                r_xc, r_xcb, r_tr, r_a2, r_ti, r_gt, r_gs = R
                r_a = r_tr
                r_h = r_ti
                xc, tr_, a2, ti, gt, gs = (hrow_f32(r) for r in (r_xc, r_tr, r_a2, r_ti, r_gt, r_gs))
                a_ = tr_
                h_ = ti
                xcb = row_ap(r_xcb)[:, 0:GT]
                pst = PS[par]
                bk = [bpb[2 * par], bpb[2 * par + 1]]
                pst2 = PS[2 + par]
                bk2 = [bpb[4 + 2 * par], bpb[5 + 2 * par]]
                xr = xr_sb[:, rs, :]
                bxr_c = bxr[rs]
                bxr_p = bxr[prs]
                tsl = slice(g * GT, (g + 1) * GT)
                ops = []
                if g == 0:
                    ops.append(lambda: slots.__setitem__(c, a_next()))

                def mm(e):
                    slot = slots[c]
                    for h in range(2):
                        for k in range(KC):
                            i_ = e.matmul(pst[:, h, :], lhsT=AR[:, slot, h, k, :], rhs=xnT[:, k, tsl],
                                          start=(k == 0), stop=(k == KC - 1))
                    return i_
                ops.append(lambda: P.op(P.pe, mm, reads=[bA[slots[c]], bxn[g]], writes=bk))
                ops.append(lambda: P.op(P.act, lambda e: e.activation(out=xr[:, 3:GT + 3], in_=pst[:, 0, :], func=AF.Copy),
                                        reads=[bk[0]], writes=[bxr_c]))
                ops.append(lambda: P.op(P.dve, lambda e: e.tensor_copy(out=gs, in_=pst[:, 1, :]),
                                        reads=[bk[1]], writes=[row_buf(r_gs)]))
                if u == 0:
                    ops.append(flush_pending)
                if g == 0:
                    ops.append(lambda: P.op(P.dve, lambda e: e.tensor_copy(out=xr[:, 0:3], in_=cstate[:, c, :]),
                                            reads=[bcst], writes=[bxr_c]))
                else:
                    def halo():
                        P.op(P.dve, lambda e: e.tensor_copy(out=xr[:, 0:3], in_=xr_sb[:, prs, GT:GT + 3]),
                             reads=[bxr_p], writes=[bxr_c])
                        P.op(P.dve, lambda e: e.tensor_copy(out=cstate[:, c, :], in_=xr[:, GT:GT + 3]),
                             reads=[bxr_c], writes=[bcst])
                    ops.append(halo)
                ops.append(lambda: P.op(P.dve, lambda e: e.tensor_scalar(
                    out=xc, in0=xr[:, 0:GT], scalar1=recfm[:, 0, c:c + 1], scalar2=recfm[:, 4, c:c + 1],
                    op0=ALU.mult, op1=ALU.add), reads=[bxr_c, bconst], writes=[row_buf(r_xc)]))
                for k in range(1, 4):
                    ops.append(lambda k=k: P.op(P.dve, lambda e: e.scalar_tensor_tensor(
                        out=xc, in0=xr[:, k:k + GT], scalar=recfm[:, k, c:c + 1], in1=xc, op0=ALU.mult, op1=ALU.add),
                        reads=[bxr_c, bconst, row_buf(r_xc)], writes=[row_buf(r_xc)]))
                ops.append(lambda: P.op(P.act, lambda e: e.activation(out=xcb, in_=xc, func=AF.Copy),
                                        reads=[row_buf(r_xc)], writes=[row_buf(r_xcb)]))

                def mmg(e):
                    e.matmul(pst2[:, 0, :], lhsT=bd[:, 0, c, :], rhs=xcb, start=True, stop=True)
                    return e.matmul(pst2[:, 1, :], lhsT=bd[:, 1, c, :], rhs=xcb, start=True, stop=True)
                ops.append(lambda: P.op(P.pe, mmg, reads=[bbd, row_buf(r_xcb)], writes=bk2))
                ops.append(lambda: P.op(P.act, lambda e: e.activation(
                    out=tr_, in_=pst2[:, 0, :], func=AF.Tanh, scale=0.5, bias=der[:, 0, c:c + 1]),
                    reads=[bk2[0], bder], writes=[row_buf(r_tr)]))
                ops.append(lambda: P.op(P.act, lambda e: e.activation(
                    out=ti, in_=pst2[:, 1, :], func=AF.Tanh, scale=0.5, bias=der[:, 1, c:c + 1]),
                    reads=[bk2[1], bder], writes=[row_buf(r_ti)]))
                ops.append(lambda: P.op(peng, lambda e: e.tensor_tensor(out=gt, in0=gs, in1=gs, op=ALU.mult),
                                        reads=[row_buf(r_gs)], writes=[row_buf(r_gt)]))
                ops.append(lambda: P.op(peng, lambda e: e.tensor_scalar(out=gt, in0=gt, scalar1=0.044715, scalar2=1.0,
                                                                          op0=ALU.mult, op1=ALU.add),
                                        reads=[row_buf(r_gt)], writes=[row_buf(r_gt)]))
                ops.append(lambda: P.op(peng, lambda e: e.tensor_tensor(out=gt, in0=gt, in1=gs, op=ALU.mult),
                                        reads=[row_buf(r_gt), row_buf(r_gs)], writes=[row_buf(r_gt)]))
                ops.append(lambda: P.op(P.act, lambda e: e.activation(out=gt, in_=gt, func=AF.Tanh, scale=0.7978845608028654),
                                        reads=[row_buf(r_gt)], writes=[row_buf(r_gt)]))
                ops.append(lambda: P.op(P.act, lambda e: e.activation(
                    out=a2, in_=tr_, func=AF.Exp, scale=der[:, 2, c:c + 1], bias=der[:, 2, c:c + 1]),
                    reads=[row_buf(r_tr), bder], writes=[row_buf(r_a2)]))
                ops.append(lambda: P.op(P.act, lambda e: e.activation(
                    out=a_, in_=tr_, func=AF.Exp, scale=der[:, 3, c:c + 1], bias=der[:, 3, c:c + 1]),
                    reads=[row_buf(r_tr), bder], writes=[row_buf(r_a)]))
                ops.append(lambda: P.op(P.act, lambda e: e.activation(out=a2, in_=a2, func=AF.Ln, scale=-1.0, bias=1.000001),
                                        reads=[row_buf(r_a2)], writes=[row_buf(r_a2)]))
                ops.append(lambda: P.op(P.act, lambda e: e.activation(out=a2, in_=a2, func=AF.Exp, scale=0.5),
                                        reads=[row_buf(r_a2)], writes=[row_buf(r_a2)]))
                ops.append(lambda: P.op(P.dve, lambda e: e.scalar_tensor_tensor(out=ti, in0=ti, scalar=1.0, in1=xc,
                                                                                op0=ALU.add, op1=ALU.mult),
                                        reads=[row_buf(r_ti), row_buf(r_xc)], writes=[row_buf(r_ti)]))
                ops.append(lambda: P.op(P.dve, lambda e: e.tensor_tensor(out=a2, in0=a2, in1=ti, op=ALU.mult),
                                        reads=[row_buf(r_a2), row_buf(r_ti)], writes=[row_buf(r_a2)]))
                if g == 0:
                    init = hstate[:, c:c + 1]
                    init_b = bhst
                else:
                    init = hrow_f32(prs * 7 + 4)[:, GT - 1:GT]
                    init_b = row_buf(prs * 7 + 4)
                ops.append(lambda: P.op(P.dve, lambda e: e.tensor_tensor_scan(
                    out=h_, data0=a_, data1=a2, initial=init, op0=ALU.mult, op1=ALU.add),
                    reads=[row_buf(r_a), row_buf(r_a2), init_b], writes=[row_buf(r_h)]))
                if g == 1:
                    ops.append(lambda: P.op(P.dve, lambda e: e.tensor_copy(out=hstate[:, c:c + 1], in_=h_[:, GT - 1:GT]),
                                            reads=[row_buf(r_h)], writes=[bhst]))
                ops.append(lambda: P.op(P.dve, lambda e: e.scalar_tensor_tensor(out=gt, in0=gt, scalar=1.0, in1=gs,
                                                                                op0=ALU.add, op1=ALU.mult),
                                        reads=[row_buf(r_gt), row_buf(r_gs)], writes=[row_buf(r_gt)]))
                ops.append(lambda: P.op(P.dve, lambda e: e.scalar_tensor_tensor(
                    out=BT[:, 8 + c, tsl], in0=h_, scalar=0.25, in1=gt, op0=ALU.mult, op1=ALU.mult),
                    reads=[row_buf(r_h), row_buf(r_gt)], writes=[bB[8 + c]]))
                return ops

            units = []
            u = 0
            for c in range(8):
                for g in range(NG):
                    units.append(unit_ops(c, g, u))
                    u += 1
            run_pipelined(units, depth=4)
            project_postnorm(lambda k, b: BT[:, 8 + k, b * 128:(b + 1) * 128], bB[8:16], 8,
                             lambda k, half: BT[:, k, half * GT:(half + 1) * GT], bB[0:8], 1.0, next_n, None)

        wv_loaded = [False]

        def kv_stage():
            ev = 0
            if not wv_loaded[0]:
                P.dma(P.sp, lambda e: e.dma_start(out=wv[:], in_=WvS.rearrange("(kc p) n -> p kc n", p=128)), dwv,
                      reads=[scr["Wv"]], writes=[bwv])
                wv_loaded[0] = True
            for j in range(2):
                slot = a_next()
                for h in range(2):
                    kvh = 2 * j + h
                    for g in range(NG):
                        bk = ctr["tp"] % 4
                        ctr["tp"] += 1

                        def mm(e, slot=slot, h=h, g=g, bk=bk):
                            for k in range(KC):
                                i_ = e.matmul(bank(bk), lhsT=AR[:, slot, h, k, :], rhs=xnT[:, k, g * GT:(g + 1) * GT],
                                              start=(k == 0), stop=(k == KC - 1))
                            return i_
                        P.op(P.pe, mm, reads=[bA[slot], bxn[g]], writes=[bpb[bk]])
                        flush_pending()
                        dst = kT[:, kvh, 128 + g * GT:128 + (g + 1) * GT]
                        if ev % 2 == 0:
                            P.op(P.act, lambda e, dst=dst, bk=bk: e.activation(out=dst, in_=bank(bk), func=AF.Copy),
                                 reads=[bpb[bk]], writes=[bkT[kvh]])
                        else:
                            P.op(P.dve, lambda e, dst=dst, bk=bk: e.tensor_copy(out=dst, in_=bank(bk)),
                                 reads=[bpb[bk]], writes=[bkT[kvh]])
                        ev += 1
            for b in range(NB):
                bk = 4 + (b % 4)

                def mmv(e, b=b, bk=bk):
                    for k in range(KC):
                        i_ = e.matmul(bank(bk)[:, 0:256], lhsT=xnT[:, k, b * 128:(b + 1) * 128], rhs=wv[:, k, :],
                                      start=(k == 0), stop=(k == KC - 1))
                    return i_
                P.op(P.pe, mmv, reads=[bxn[b // 4], bwv], writes=[bpb[bk]])
                src = bank(bk)[:, 0:256].rearrange("p (h d) -> p h d", h=4)
                if b % 2 == 0:
                    P.op(P.act, lambda e, b=b, src=src: e.activation(out=vpad[:, 1 + b, :, 0:64], in_=src, func=AF.Copy),
                         reads=[bpb[bk]], writes=[bvp[1 + b]])
                    P.op(P.act, lambda e, b=b, src=src: e.activation(out=vpad[:, 1 + b, :, 128:192], in_=src, func=AF.Copy),
                         reads=[bpb[bk]], writes=[bvp[1 + b]])
                else:
                    P.op(P.dve, lambda e, b=b, src=src: e.tensor_copy(out=vpad[:, 1 + b, :, 0:64], in_=src),
                         reads=[bpb[bk]], writes=[bvp[1 + b]])
                    P.op(P.dve, lambda e, b=b, src=src: e.tensor_copy(out=vpad[:, 1 + b, :, 128:192], in_=src),
                         reads=[bpb[bk]], writes=[bvp[1 + b]])

        def attention(p, gain_n, next_n):
            b_load(WoS, scr["Wo"], 8)
            gain_load(gain_n)
            ev = 0
            for j in range(4):
                slot = a_next()
                for h in range(2):
                    ch = 2 * j + h
                    for g in range(NG):
                        bk = ctr["tp"] % 4
                        ctr["tp"] += 1

                        def mm(e, slot=slot, h=h, g=g, bk=bk):
                            for k in range(KC):
                                i_ = e.matmul(bank(bk), lhsT=AR[:, slot, h, k, :], rhs=xnT[:, k, g * GT:(g + 1) * GT],
                                              start=(k == 0), stop=(k == KC - 1))
                            return i_
                        P.op(P.pe, mm, reads=[bA[slot], bxn[g]], writes=[bpb[bk]])
                        flush_pending()
                        dst = hT[:, ch, g * GT:(g + 1) * GT]
                        if ev % 2 == 0:
                            P.op(P.act, lambda e, dst=dst, bk=bk: e.activation(out=dst, in_=bank(bk), func=AF.Copy),
                                 reads=[bpb[bk]], writes=[bh[ch]])
                        else:
                            P.op(P.dve, lambda e, dst=dst, bk=bk: e.tensor_copy(out=dst, in_=bank(bk)),
                                 reads=[bpb[bk]], writes=[bh[ch]])
                        ev += 1

            otp = PS[3]
            otv = otp[:].rearrange("p a b -> p (a b)").rearrange("p (c t) -> p c t", c=8)

            def quad_ops(b, kvh, q):
                par = q % 2
                rs = q % 3
                gb = p * NB + b
                mi = 1 if gb == 0 else 0
                sp_ = PS[par]
                sbk = [bpb[2 * par], bpb[2 * par + 1]]
                sv = sp_[:].rearrange("p a b -> p (a b)").rearrange("p (j k) -> p j k", j=4)
                r0 = 8 + rs * 4
                sm = hT[:, r0:r0 + 2, :].rearrange("p a b -> p (a b)").bitcast(F32).rearrange("p (j k) -> p j k", j=4)
                pe_ = sm
                pn = row_ap(r0 + 2).rearrange("p (j k) -> p j k", j=4)
                pts = row_ap(r0 + 3)
                b_sm = [bh[r0], bh[r0 + 1]]
                b_pe = b_sm
                b_pn = [row_buf(r0 + 2)]
                b_pt = [row_buf(r0 + 3)]
                st = astat[:, rs]
                bst = bastat[rs]
                ptb = 4 + par
                ptv = bank_bf(ptb)
                ops = []

                def mms(e):
                    for jj in range(4):
                        ch = 2 * kvh + jj // 2
                        ph = jj % 2
                        i_ = e.matmul(sv[:, SLOT[jj], :], lhsT=hT[ph * 64:(ph + 1) * 64, ch, b * 128:(b + 1) * 128],
                                      rhs=kT[ph * 64:(ph + 1) * 64, kvh, b * 128:b * 128 + 256], start=True, stop=True)
                    return i_
                ops.append(lambda: P.op(P.pe, mms, reads=[bh[2 * kvh], bh[2 * kvh + 1], bkT[kvh]], writes=sbk))
                ops.append(lambda: P.op(P.dve, lambda e: e.scalar_tensor_tensor(
                    out=sm, in0=sv, scalar=0.125, in1=mask[:, mi, :].unsqueeze(1).broadcast_to([128, 4, 256]),
                    op0=ALU.mult, op1=ALU.add), reads=sbk + [bconst], writes=b_sm))
                ops.append(lambda: P.op(P.dve, lambda e: e.tensor_reduce(out=st[:, 0, :], in_=sm, axis=AX.X, op=ALU.max),
                                        reads=b_sm, writes=[bst]))
                ops.append(lambda: P.op(P.dve, lambda e: e.tensor_tensor(out=st[:, 1, :], in0=st[:, 0, :],
                                                                         in1=sinks[:, 4 * kvh:4 * kvh + 4], op=ALU.max),
                                        reads=[bst, bconst], writes=[bst]))
                ops.append(lambda: P.op(P.dve, lambda e: e.tensor_scalar(out=st[:, 2, :], in0=st[:, 1, :], scalar1=-1.0,
                                                                         scalar2=None, op0=ALU.mult), reads=[bst], writes=[bst]))
                ops.append(lambda: P.op(P.dve, lambda e: e.tensor_tensor(out=st[:, 3, :], in0=sinks[:, 4 * kvh:4 * kvh + 4],
                                                                         in1=st[:, 1, :], op=ALU.subtract),
                                        reads=[bst, bconst], writes=[bst]))
                for jj in range(4):
                    ops.append(lambda jj=jj: P.op(P.act, lambda e: e.activation(
                        out=pe_[:, jj, :], in_=sm[:, jj, :], func=AF.Exp, bias=st[:, 2, jj:jj + 1],
                        accum_out=st[:, 4, jj:jj + 1]), reads=b_sm + [bst], writes=b_pe + [bst]))
                ops.append(lambda: P.op(P.act, lambda e: e.activation(out=st[:, 5, :], in_=st[:, 3, :], func=AF.Exp),
                                        reads=[bst], writes=[bst]))
                ops.append(lambda: P.op(P.dve, lambda e: e.tensor_tensor(out=st[:, 6, :], in0=st[:, 4, :], in1=st[:, 5, :],
                                                                         op=ALU.add), reads=[bst], writes=[bst]))
                ops.append(lambda: P.op(P.dve, lambda e: e.reciprocal(out=st[:, 7, :], in_=st[:, 6, :]),
                                        reads=[bst], writes=[bst]))
                ops.append(lambda: P.op(P.dve, lambda e: e.tensor_tensor(
                    out=pn, in0=pe_, in1=st[:, 7, :].unsqueeze(2).broadcast_to([128, 4, 256]), op=ALU.mult),
                    reads=b_pe + [bst], writes=b_pn))

                def trp(e):
                    for jj in range(4):
                        for kb in range(2):
                            idx = jj * 2 + kb
                            i_ = e.transpose(out=ptv[:, idx * 128:(idx + 1) * 128], in_=pn[:, jj, kb * 128:(kb + 1) * 128],
                                             identity=ident[:])
                    return i_
                ops.append(lambda: P.op(P.pe, trp, reads=b_pn + [bident], writes=[bpb[ptb]]))
                ops.append(lambda: P.op(P.act, lambda e: e.activation(out=pts, in_=ptv, func=AF.Copy),
                                        reads=[bpb[ptb]], writes=b_pt))

                def mmo(e):
                    for cc in range(2):
                        ch = 2 * kvh + cc
                        n = 0
                        for hl in range(2):
                            jj = 2 * cc + hl
                            for kb in range(2):
                                idx = SLOT[jj] * 2 + kb
                                i_ = e.matmul(otv[:, ch, :], lhsT=vpad[:, b + kb, kvh, hl * 64:hl * 64 + 128],
                                              rhs=pts[:, idx * 128:(idx + 1) * 128], start=(n == 0), stop=(n == 3))
                                n += 1
                    return i_
                ops.append(lambda: P.op(P.pe, mmo, reads=b_pt + [bvp[b], bvp[b + 1]], writes=[bpb[6], bpb[7]]))
                if kvh == 3:
                    ops.append(lambda: P.op(P.dve, lambda e: e.tensor_copy(out=BT[:, 8:16, b * 128:(b + 1) * 128], in_=otv),
                                            reads=[bpb[6], bpb[7]], writes=bB[8:16]))
                return ops

            quads = []
            q = 0
            for b in range(NB):
                for kvh in range(4):
                    quads.append(quad_ops(b, kvh, q))
                    q += 1
            run_pipelined(quads, depth=3)
            return

        def attention_tail():
            P.op(P.act, lambda e: e.activation(out=kT[:, :, 0:128], in_=kT[:, :, T:T + 128], func=AF.Copy),
                 reads=bkT, writes=bkT)
            P.op(P.act, lambda e: e.activation(out=vpad[:, 0], in_=vpad[:, NB], func=AF.Copy),
                 reads=[bvp[NB]], writes=[bvp[0]])

        def load_x(p, b):
            P.dma(P.sp, lambda e: e.dma_start(out=X[:, b, :], in_=x_d[p * T + b * 128:p * T + (b + 1) * 128, :]),
                  dXl[b], writes=[bX[b]])

        for b in range(NB):
            load_x(0, b)
        P.wait(P.pool, [b_.w for b_ in list(scr.values()) + list(wg00_buf.values()) if b_.w is not None] + [bbd.w, bident.w])

        last_store = [None] * NB
        for p in range(npass):
            new_engine_sems(f"p{p}")

            def store(b, p=p):
                last_store[b] = P.dma(P.sp, lambda e: e.dma_start(
                    out=out_d[p * T + b * 128:p * T + (b + 1) * 128, :], in_=X[:, b, :]), dXs[b], reads=[bX[b]])
                if p + 1 < npass and b >= 1:
                    load_x(p + 1, b - 1)
            ns = 9 if stop_after is None else stop_after
            if ns >= 1:
                prenorm_full(0)
            if ns >= 2:
                ffn(0, 0, 0, 1)
            if ns >= 3:
                recurrent(1, 2, p)
            if ns >= 4:
                ffn(0, 1, 2, 6)
            if ns >= 5:
                kv_stage()
            if ns >= 6:
                prenorm_full(3)
            if ns >= 7:
                ffn(1, 0, 3, 4)
            if ns >= 8:
                attention(p, 4, 5)
                project_postnorm(lambda k, b: BT[:, 8 + k, b * 128:(b + 1) * 128], bB[8:16], 8,
                                 lambda k, half: BT[:, k, half * GT:(half + 1) * GT], bB[0:8], 1.0, 5, None)
                attention_tail()
            if ns >= 9:
                ffn(1, 1, 5, None, final_store=store)
            else:
                for b in range(NB):
                    store(b)
            if p + 1 < npass:
                load_x(p + 1, NB - 1)
        P.wait(P.sp, [t for t in last_store if t is not None])
        P.wait(P.pool, [t for t in last_store if t is not None])
        with nc.Block() as block:
            P.finish(block)
    return nc


def _fm(v):
    return np.ascontiguousarray(np.asarray(v, np.float32).reshape(8, 128).T)


def make_consts():
    ident = np.eye(128, dtype=np.float32)
    qi = np.arange(128)[:, None]
    kj = np.arange(256)[None, :]
    delta = qi + 128 - kj
    valid = (delta >= 0) & (delta < 128)
    m0 = np.where(valid, 0.0, -1e30).astype(np.float32)
    valid1 = valid & (kj >= 128)
    m1 = np.where(valid1, 0.0, -1e30).astype(np.float32)
    mask = np.concatenate([m0, m1], axis=1)
    return ident, np.ascontiguousarray(mask)


def shared_inputs(norms, ffn_w_gate, ffn_w_up, ffn_w_down, a_w_in, a_conv_w, a_conv_b, a_gate_a_w, a_gate_a_b,
                  a_gate_x_w, a_gate_x_b, a_lambda, a_w_out, kv_norm, w_kv, b_w_q, b_sinks, b_w_o):
    f = lambda a: np.ascontiguousarray(np.asarray(a, np.float32))
    norms = f(norms)
    pre = [norms[0, 0], norms[0, 2], norms[0, 4], norms[1, 0], norms[1, 2], norms[1, 4], f(kv_norm)]
    prenorm_fm = np.concatenate([_fm(v) for v in pre], axis=1)
    postnorm = np.stack([norms[0, 1], norms[0, 3], norms[0, 5], norms[1, 1], norms[1, 3], norms[1, 5]])
    cw = f(a_conv_w)[0]
    rec = [cw[0], cw[1], cw[2], cw[3], f(a_conv_b)[0], f(a_gate_a_b)[0], f(a_gate_x_b)[0], f(a_lambda)[0]]
    rec_fm = np.concatenate([_fm(v) for v in rec], axis=1)

    def gate_layout(w):
        w = f(w)[0]
        w = w.reshape(8, 2, 64, 64)
        return np.ascontiguousarray(w.transpose(1, 2, 0, 3))
    ident, mask = make_consts()
    return {
        "ffn_w_gate": f(ffn_w_gate), "ffn_w_up": f(ffn_w_up), "ffn_w_down": f(ffn_w_down),
        "a_w_in": f(a_w_in)[0], "a_w_out": f(a_w_out)[0], "w_kv": f(w_kv), "b_w_q": f(b_w_q)[0], "b_w_o": f(b_w_o)[0],
        "a_gate_a_w": gate_layout(a_gate_a_w), "a_gate_x_w": gate_layout(a_gate_x_w),
        "prenorm_fm": np.ascontiguousarray(prenorm_fm), "postnorm": np.ascontiguousarray(postnorm),
        "rec_fm": np.ascontiguousarray(rec_fm),
        "sinks": np.ascontiguousarray(f(b_sinks)[0].reshape(4, 4)[:, [0, 2, 1, 3]].reshape(16)), "ident": ident, "mask": mask,
    }


_NC_CACHE = {}


def kernel(x, **params):
    x = np.asarray(x, np.float32)
    bsz, seq, _ = x.shape
    shared = shared_inputs(**params)
    if seq not in _NC_CACHE:
        _NC_CACHE[seq] = build_program(seq)
    nc = _NC_CACHE[seq]
    in_maps = []
    for c in range(bsz):
        m = dict(shared)
        m["x"] = np.ascontiguousarray(x[c])
        in_maps.append(m)
    res = run_bass_kernel_spmd(nc, in_maps, core_ids=list(range(bsz)))
    return np.stack([np.asarray(r["out"], np.float32) for r in res.results], axis=0)
```

```python
import contextlib
import os
KDBG = set(os.environ.get('KDBG', '').split(','))
import numpy as np
import concourse.bass as bass
import concourse.mybir as mybir
from concourse.bass_utils import run_bass_kernel_spmd

F32 = mybir.dt.float32
BF16 = mybir.dt.bfloat16
ALU = mybir.AluOpType
AF = mybir.ActivationFunctionType
AX = mybir.AxisListType

D = 1024
KC = 8
DFF = 2816
FC = 22
T = 1024
NB = 8
GT = 512
NG = 2
EPS = 1e-6
NCORES = 8
SLOT = [0, 2, 1, 3]
SEQ = 8192


class Tok:
    __slots__ = ("sem", "val")

    def __init__(self, sem, val):
        self.sem = sem
        self.val = val


class Buf:
    __slots__ = ("name", "w", "r", "excl")

    def __init__(self, name, excl=False):
        self.name = name
        self.w = None
        self.r = []
        self.excl = excl


class DSem:
    __slots__ = ("sem", "cnt")

    def __init__(self, sem):
        self.sem = sem
        self.cnt = 0


class Eng:
    def __init__(self, name):
        self.name = name
        self.ops = []
        self.sem = None
        self.cnt = 0
        self.waited = {}

    def set_sem(self, sem):
        self.sem = sem
        self.cnt = 0


class Prog:
    def __init__(self, nc):
        self.nc = nc
        self.pe = Eng("tensor")
        self.act = Eng("scalar")
        self.dve = Eng("vector")
        self.pool = Eng("gpsimd")
        self.sp = Eng("sync")
        self.engs = [self.pe, self.act, self.dve, self.pool, self.sp]

    def _deps(self, eng, reads, writes, extra=()):
        best = {}

        def add(t):
            k = id(t.sem)
            if k not in best or best[k].val < t.val:
                best[k] = t
        for b in reads:
            if b.w is not None:
                add(b.w)
            if b.excl:
                for t in b.r:
                    if t.sem is not eng.sem:
                        add(t)
        for b in writes:
            if b.w is not None:
                add(b.w)
            for t in b.r:
                add(t)
        for t in extra:
            add(t)
        for k, t in best.items():
            if eng.waited.get(k, 0) < t.val:
                eng.ops.append(("wait", t.sem, t.val))
                eng.waited[k] = t.val

    @staticmethod
    def _mark(tok, reads, writes):
        for b in reads:
            b.r = [t for t in b.r if t.sem is not tok.sem] + [tok]
        for b in writes:
            b.w = tok
            b.r = []

    def op(self, eng, fn, reads=(), writes=(), extra=()):
        self._deps(eng, reads, writes, extra)
        eng.cnt += 1
        tok = Tok(eng.sem, eng.cnt)
        eng.ops.append(("op", fn, eng.sem, 1))
        self._mark(tok, reads, writes)
        return tok

    def dma(self, eng, fn, dsem, reads=(), writes=(), extra=(), deps=True):
        if deps:
            self._deps(eng, reads, writes, extra)
        dsem.cnt += 16
        tok = Tok(dsem.sem, dsem.cnt)
        eng.ops.append(("op", fn, dsem.sem, 16))
        self._mark(tok, reads, writes)
        return tok

    def wait(self, eng, toks):
        self._deps(eng, (), (), toks)

    def finish(self, block):
        def run(e, ops):
            for o in ops:
                if o[0] == "wait":
                    e.wait_ge(o[1], o[2])
                else:
                    ins = o[1](e)
                    ins.then_inc(o[2], o[3])

        pe, act, dve, pool, sp = self.pe, self.act, self.dve, self.pool, self.sp

        @block.tensor
        def _(e):
            run(e, pe.ops)

        @block.scalar
        def _(e):
            run(e, act.ops)

        @block.vector
        def _(e):
            run(e, dve.ops)

        @block.gpsimd
        def _(e):
            run(e, pool.ops)

        @block.sync
        def _(e):
            run(e, sp.ops)


def build_program(S, stop_after=None):
    npass = S // T
    nc = bass.Bass("TRN2", target_bir_lowering=False)

    def din(name, shape, dt=F32):
        return nc.dram_tensor(name, list(shape), dt, kind="ExternalInput").ap()

    def dscr(name, shape, dt=BF16):
        return nc.dram_tensor(name, list(shape), dt, kind="Internal").ap()

    x_d = din("x", [S, D])
    out_d = nc.dram_tensor("out", [S, D], F32, kind="ExternalOutput").ap()
    wg_d = din("ffn_w_gate", [2, 2, D, DFF])
    wu_d = din("ffn_w_up", [2, 2, D, DFF])
    wd_d = din("ffn_w_down", [2, 2, DFF, D])
    win_d = din("a_w_in", [D, 2 * D])
    wout_d = din("a_w_out", [D, D])
    wkv_d = din("w_kv", [D, 512])
    wq_d = din("b_w_q", [D, D])
    wo_d = din("b_w_o", [D, D])
    gaw_d = din("a_gate_a_w", [2, 64, 8, 64])
    gxw_d = din("a_gate_x_w", [2, 64, 8, 64])
    prefm_d = din("prenorm_fm", [128, 7 * 8])
    postn_d = din("postnorm", [6, D])
    recfm_d = din("rec_fm", [128, 8 * 8])
    sinks_d = din("sinks", [16])
    ident_d = din("ident", [128, 128])
    mask_d = din("mask", [128, 2 * 256])

    WgS = [[dscr(f"WgS{l}{i}", [FC, 128, 2, KC, 128]) for i in range(2)] for l in range(2)]
    WdS = [[dscr(f"WdS{l}{i}", [DFF, D]) for i in range(2)] for l in range(2)]
    WinS = dscr("WinS", [8, 128, 2, KC, 128])
    WqS = dscr("WqS", [4, 128, 2, KC, 128])
    WkS = dscr("WkS", [2, 128, 2, KC, 128])
    WoutS = dscr("WoutS", [D, D])
    WoS = dscr("WoS", [D, D])
    WvS = dscr("WvS", [D, 256])

    with contextlib.ExitStack() as es:
        def sb(name, shape, dt):
            return es.enter_context(nc.sbuf_tensor("sb_" + name, list(shape), dt))

        def psum(name, shape, dt):
            return es.enter_context(nc.psum_tensor(name, list(shape), dt))

        def sem(name):
            return es.enter_context(nc.semaphore(name))

        X = sb("X", [128, NB, D], F32)
        xnT = sb("xnT", [128, KC, T], BF16)
        hT = sb("hT", [128, FC, T], BF16)
        BT = sb("BT", [128, FC, D], BF16)
        AR = sb("AR", [128, 3, 2, KC, 128], BF16)
        gain = sb("gain", [128, D], F32)
        kT = sb("kT", [128, 4, T + 128], BF16)
        vpad = sb("vpad", [128, NB + 1, 4, 192], BF16)
        wv = sb("wv", [128, KC, 256], BF16)
        bd = sb("bd", [128, 2, KC, 128], BF16)
        ident = sb("ident", [128, 128], BF16)
        mask = sb("mask", [128, 2, 256], F32)
        mask_bf = sb("mask_bf", [128, 2, 256], BF16)
        prefm = sb("prefm", [128, 7, 8], F32)
        recfm = sb("recfm", [128, 8, 8], F32)
        der = sb("der", [128, 6, 8], F32)
        sinks = sb("sinks", [128, 16], F32)
        half_c = sb("half_c", [128, GT], F32)
        mhalf_c = sb("mhalf_c", [128, 1], F32)
        hstate = sb("hstate", [128, 8], F32)
        cstate = sb("cstate", [128, 8, 3], F32)
        xr_sb = sb("xr_sb", [128, 4, GT + 3], F32)
        xs_bf = sb("xs_bf", [128, 2, D], BF16)
        junk = sb("junk", [128, D], BF16)
        ffn_t = sb("ffn_t", [128, 2, GT], F32)
        stat = sb("stat", [128, 4, 16], F32)
        astat = sb("astat", [128, 4, 8, 4], F32)

        PS = [psum(f"ps{i}", [128, 2, GT], F32) for i in range(4)]

        def bank(i):
            return PS[i // 2][:, i % 2, :]

        def bank_bf(i):
            return PS[i // 2][:, i % 2, :].bitcast(BF16)

        P = Prog(nc)
        bX = [Buf(f"X{b}") for b in range(NB)]
        bxn = [Buf(f"xn{g}") for g in range(NG)]
        bh = [Buf(f"h{r}") for r in range(FC)]
        bB = [Buf(f"B{r}") for r in range(FC)]
        bA = [Buf(f"A{r}") for r in range(3)]
        bgain = Buf("gain")
        bkT = [Buf(f"kT{h}") for h in range(4)]
        bvp = [Buf(f"vpad{i}") for i in range(NB + 1)]
        bwv = Buf("wv")
        bbd = Buf("bd")
        bconst = Buf("const")
        bpb = [Buf(f"pb{i}", excl=True) for i in range(8)]
        bxr = [Buf("xr0"), Buf("xr1"), Buf("xr2"), Buf("xr3")]
        bxs = [Buf("xs0"), Buf("xs1")]
        bjunk = Buf("junk")
        bft = [Buf("ft0"), Buf("ft1")]
        bstat = [Buf(f"st{i}") for i in range(16)]
        bastat = [Buf("as0"), Buf("as1"), Buf("as2"), Buf("as3")]
        bhst = Buf("hstate")
        bcst = Buf("cstate")

        def new_engine_sems(tag):
            for e in P.engs:
                e.set_sem(sem(f"e_{e.name}_{tag}"))
        new_engine_sems("pro")
        dA = [DSem(sem(f"dA{i}")) for i in range(3)]
        dB = [DSem(sem(f"dB{i}")) for i in range(3)]
        dgain = DSem(sem("dgain"))
        dXl = [DSem(sem(f"dXl{b}")) for b in range(NB)]
        dXs = [DSem(sem(f"dXs{b}")) for b in range(NB)]
        dconst = DSem(sem("dconst"))
        dwv = DSem(sem("dwv"))

        P.dma(P.sp, lambda e: e.dma_start(out=mask[:].rearrange("p a b -> p (a b)"), in_=mask_d), dconst, writes=[bconst])
        P.dma(P.sp, lambda e: e.dma_start(out=prefm[:].rearrange("p a b -> p (a b)"), in_=prefm_d), dconst, writes=[bconst], deps=False)
        P.dma(P.sp, lambda e: e.dma_start(out=recfm[:].rearrange("p a b -> p (a b)"), in_=recfm_d), dconst, writes=[bconst], deps=False)
        P.dma(P.sp, lambda e: e.dma_start(out=sinks[:], in_=sinks_d.partition_broadcast(128)), dconst, writes=[bconst], deps=False)
        dident = DSem(sem("dident"))
        bident = Buf("ident")
        P.dma(P.pool, lambda e: e.dma_start(out=ident[:], in_=ident_d), dident, writes=[bident], deps=False)
        P.op(P.dve, lambda e: e.tensor_copy(out=mask_bf[:].rearrange("p a b -> p (a b)"), in_=mask[:].rearrange("p a b -> p (a b)")),
             reads=[bconst], writes=[bconst])
        P.op(P.dve, lambda e: e.memset(vpad[:].rearrange("p a b c -> p (a b c)"), 0.0), writes=bvp)
        P.op(P.dve, lambda e: e.memset(kT[:].rearrange("p a b -> p (a b)"), 0.0), writes=bkT)
        P.op(P.dve, lambda e: e.memset(bd[:].rearrange("p a b c -> p (a b c)"), 0.0), writes=[bbd])
        P.op(P.dve, lambda e: e.memset(hstate[:], 0.0), writes=[bhst])
        P.op(P.dve, lambda e: e.memset(cstate[:].rearrange("p a b -> p (a b)"), 0.0), writes=[bcst])
        bexp = Buf("expc")
        P.op(P.dve, lambda e: e.memset(half_c[:], 0.5), writes=[bexp])
        P.op(P.dve, lambda e: e.memset(mhalf_c[:], -0.5), writes=[bexp])
        dbd = DSem(sem("dbd"))
        for gi, src in enumerate((gaw_d, gxw_d)):
            for half in range(2):
                P.dma(P.pool, lambda e, gi=gi, src=src, half=half: e.dma_start(
                    out=bd[half * 64:(half + 1) * 64, gi, :, half * 64:(half + 1) * 64], in_=src[half]),
                    dbd, writes=[bbd], deps=(gi == 0 and half == 0))
        bder = Buf("der")
        P.op(P.dve, lambda e: e.tensor_scalar(out=der[:, 0, :], in0=recfm[:, 5, :], scalar1=0.5, scalar2=None, op0=ALU.mult),
             reads=[bconst], writes=[bder])
        P.op(P.dve, lambda e: e.tensor_scalar(out=der[:, 1, :], in0=recfm[:, 6, :], scalar1=0.5, scalar2=None, op0=ALU.mult),
             reads=[bconst], writes=[bder])
        P.op(P.act, lambda e: e.activation(out=der[:, 4, :], in_=recfm[:, 7, :], func=AF.Exp, scale=-1.0),
             reads=[bconst], writes=[bder])
        P.op(P.act, lambda e: e.activation(out=der[:, 5, :], in_=der[:, 4, :], func=AF.Ln, bias=1.0),
             reads=[bder], writes=[bder])
        P.op(P.dve, lambda e: e.tensor_scalar(out=der[:, 2, :], in0=der[:, 5, :], scalar1=-8.0, scalar2=None, op0=ALU.mult),
             reads=[bder], writes=[bder])
        P.op(P.dve, lambda e: e.tensor_scalar(out=der[:, 3, :], in0=der[:, 5, :], scalar1=-4.0, scalar2=None, op0=ALU.mult),
             reads=[bder], writes=[bder])

        def cast_group(name, items):
            ds = DSem(sem("dc_" + name))
            b = Buf("scr_" + name)
            for (o, i) in items:
                if 'nocast' in KDBG:
                    continue
                P.dma(P.pool, lambda e, o=o, i=i: e.dma_start(out=o, in_=i), ds, writes=[b], deps=False)
            return b

        def a_items(dst, src, nchunk, col_of):
            it = []
            for c in range(nchunk):
                for h in range(2):
                    c0 = col_of(c, h)
                    it.append((dst[c, :, h], src[:, c0:c0 + 128].rearrange("(kc p) f -> p kc f", p=128)))
            return it

        def plain_items(dst, src, rows, piece):
            return [(dst[r0:r0 + piece, :], src[r0:r0 + piece, :]) for r0 in range(0, rows, piece)]

        scr = {}
        wg00_buf = {}

        def ffn_cast(l, i):
            it = []
            for c in range(FC):
                it.append((WgS[l][i][c, :, 0], wg_d[l, i][:, c * 128:(c + 1) * 128].rearrange("(kc p) f -> p kc f", p=128)))
                it.append((WgS[l][i][c, :, 1], wu_d[l, i][:, c * 128:(c + 1) * 128].rearrange("(kc p) f -> p kc f", p=128)))
            if (l, i) == (0, 0):
                bounds = [0, 3, 8, 15, FC]
                for gi in range(4):
                    bgrp = cast_group(f"Wg00_{gi}", it[2 * bounds[gi]:2 * bounds[gi + 1]])
                    for c in range(bounds[gi], bounds[gi + 1]):
                        wg00_buf[c] = bgrp
                    if gi == 0:
                        scr["Wd00"] = cast_group("Wd00", plain_items(WdS[l][i], wd_d[l, i], DFF, 352))
                scr["Wg00"] = bgrp
                return
            scr[f"Wg{l}{i}"] = cast_group(f"Wg{l}{i}", it)
            scr[f"Wd{l}{i}"] = cast_group(f"Wd{l}{i}", plain_items(WdS[l][i], wd_d[l, i], DFF, 352))

        ffn_cast(0, 0)
        scr["Win"] = cast_group("Win", a_items(WinS, win_d, 8, lambda c, h: h * D + c * 128))
        scr["Wout"] = cast_group("Wout", plain_items(WoutS, wout_d, D, 256))
        ffn_cast(0, 1)
        kit = []
        for j in range(2):
            for h in range(2):
                for d2 in range(2):
                    c0 = (2 * j + h) * 64
                    kit.append((WkS[j, :, h, :, d2 * 64:(d2 + 1) * 64],
                                wkv_d[:, c0:c0 + 64].rearrange("(kc p) f -> p kc f", p=128)))
        scr["Wk"] = cast_group("Wk", kit)
        scr["Wv"] = cast_group("Wv", plain_items(WvS, wkv_d[:, 256:512], D, 256))
        ffn_cast(1, 0)
        scr["Wq"] = cast_group("Wq", a_items(WqS, wq_d, 4, lambda c, h: (2 * c + h) * 128))
        scr["Wo"] = cast_group("Wo", plain_items(WoS, wo_d, D, 256))
        ffn_cast(1, 1)


        a_list = []
        ns_ = 9 if stop_after is None else stop_after
        for p in range(npass):
            if ns_ >= 2:
                for c in range(FC):
                    a_list.append((WgS[0][0][c], wg00_buf[c]))
            if ns_ >= 3:
                for c in range(8):
                    a_list.append((WinS[c], scr["Win"]))
            if ns_ >= 4:
                for c in range(FC):
                    a_list.append((WgS[0][1][c], scr["Wg01"]))
            if ns_ >= 5:
                for c in range(2):
                    a_list.append((WkS[c], scr["Wk"]))
            if ns_ >= 7:
                for c in range(FC):
                    a_list.append((WgS[1][0][c], scr["Wg10"]))
            if ns_ >= 8:
                for c in range(4):
                    a_list.append((WqS[c], scr["Wq"]))
            if ns_ >= 9:
                for c in range(FC):
                    a_list.append((WgS[1][1][c], scr["Wg11"]))
        a_state = {"issued": 0, "next": 0}

        def a_issue_upto(j):
            while a_state["issued"] <= min(j, len(a_list) - 1):
                k = a_state["issued"]
                src, sbuf_ = a_list[k]
                slot = k % 3
                P.dma(P.sp, lambda e, src=src, slot=slot: e.dma_start(out=AR[:, slot], in_=src),
                      dA[slot], reads=[sbuf_], writes=[bA[slot]])
                a_state["issued"] += 1

        def a_next(pref=2):
            j = a_state["next"]
            a_issue_upto(j + pref)
            a_state["next"] += 1
            return j % 3

        def b_load(src, scrbuf, nrows_chunks):
            for g0 in range(0, nrows_chunks, 8):
                n = min(8, nrows_chunks - g0)
                gi = g0 // 8
                P.dma(P.sp, lambda e, g0=g0, n=n: e.dma_start(
                    out=BT[:, g0:g0 + n, :],
                    in_=src[g0 * 128:(g0 + n) * 128, :].rearrange("(fc p) m -> p fc m", p=128)),
                    dB[gi], reads=[scrbuf], writes=bB[g0:g0 + n])

        def gain_load(n):
            P.dma(P.sp, lambda e: e.dma_start(out=gain[:], in_=postn_d[n].partition_broadcast(128)),
                  dgain, writes=[bgain])

        ctr = {"stat": 0, "xs": 0, "tp": 0, "ft": 0, "fp": 0, "ev": 0}

        def stat_slot():
            s = ctr["stat"] % 16
            ctr["stat"] += 1
            return s

        def prenorm_a(b):
            s = stat_slot()
            par = ctr["xs"] % 2
            ctr["xs"] += 1
            P.op(P.act, lambda e: e.activation(out=junk[:], in_=X[:, b, :], func=AF.Square, accum_out=stat[:, 0, s:s + 1]),
                 reads=[bX[b]], writes=[bjunk, bstat[s]])
            P.op(P.act, lambda e: e.activation(out=stat[:, 1, s:s + 1], in_=stat[:, 0, s:s + 1], func=AF.Ln,
                                               scale=1.0 / D, bias=EPS), reads=[bstat[s]], writes=[bstat[s]])
            P.op(P.act, lambda e: e.activation(out=stat[:, 2, s:s + 1], in_=stat[:, 1, s:s + 1], func=AF.Exp, scale=-0.5),
                 reads=[bstat[s]], writes=[bstat[s]])
            P.op(P.act, lambda e: e.activation(out=xs_bf[:, par, :], in_=X[:, b, :], func=AF.Copy, scale=stat[:, 2, s:s + 1]),
                 reads=[bX[b], bstat[s]], writes=[bxs[par]])
            return par

        def prenorm_b(b, par, n):
            if 'nopre_b' in KDBG:
                return
            bk = ctr["tp"] % 4
            ctr["tp"] += 1
            tpv = bank_bf(bk).rearrange("p (k t) -> p k t", k=KC)

            def tr(e):
                for k in range(KC):
                    i = e.transpose(out=tpv[:, k, :], in_=xs_bf[:, par, k * 128:(k + 1) * 128], identity=ident[:])
                return i
            P.op(P.pe, tr, reads=[bxs[par], bident], writes=[bpb[bk]])
            g = b // 4
            P.op(P.dve, lambda e: e.tensor_tensor(out=xnT[:, :, b * 128:(b + 1) * 128], in0=tpv,
                                                  in1=prefm[:, n, :].unsqueeze(2).broadcast_to([128, KC, 128]), op=ALU.mult),
                 reads=[bpb[bk], bconst], writes=[bxn[g]])

        def prenorm_full(n):
            pend = None
            for b in range(NB):
                par = prenorm_a(b)
                if pend is not None:
                    prenorm_b(pend[0], pend[1], n)
                pend = (b, par)
            prenorm_b(pend[0], pend[1], n)

        def project_postnorm(lhs_of, lhs_bufs, nk, rhs_of, rhs_bufs, coef, next_n, final_store):
            pend = None
            for b in range(NB):
                fp = ctr["fp"] % 2
                ctr["fp"] += 1
                pst = PS[2 + fp]
                banks = [bpb[4 + 2 * fp], bpb[5 + 2 * fp]]

                def mm(e, b=b, pst=pst):
                    for half in range(2):
                        for k in range(nk):
                            i = e.matmul(pst[:, half, :], lhsT=lhs_of(k, b), rhs=rhs_of(k, half),
                                         start=(k == 0), stop=(k == nk - 1))
                    return i
                P.op(P.pe, mm, reads=list(lhs_bufs) + list(rhs_bufs), writes=banks)
                s = stat_slot()
                fv = pst[:].rearrange("p a b -> p (a b)")
                P.op(P.act, lambda e, fv=fv, s=s: e.activation(out=junk[:], in_=fv, func=AF.Square, accum_out=stat[:, 0, s:s + 1]),
                     reads=banks, writes=[bjunk, bstat[s]])
                c2 = 1.0 / (coef * coef)
                P.op(P.act, lambda e, s=s, c2=c2: e.activation(out=stat[:, 1, s:s + 1], in_=stat[:, 0, s:s + 1], func=AF.Ln,
                                                               scale=c2 / D, bias=EPS * c2), reads=[bstat[s]], writes=[bstat[s]])
                P.op(P.act, lambda e, s=s: e.activation(out=stat[:, 2, s:s + 1], in_=stat[:, 1, s:s + 1], func=AF.Exp, scale=-0.5),
                     reads=[bstat[s]], writes=[bstat[s]])
                P.op(P.dve, lambda e, fv=fv: e.tensor_tensor(out=fv, in0=fv, in1=gain[:], op=ALU.mult),
                     reads=banks + [bgain], writes=banks)
                P.op(P.dve, lambda e, fv=fv, s=s, b=b: e.scalar_tensor_tensor(out=X[:, b, :], in0=fv, scalar=stat[:, 2, s:s + 1],
                                                                              in1=X[:, b, :], op0=ALU.mult, op1=ALU.add),
                     reads=banks + [bstat[s], bX[b]], writes=[bX[b]])
                if final_store is not None:
                    final_store(b)
                if next_n is not None:
                    par = prenorm_a(b)
                    if pend is not None:
                        prenorm_b(pend[0], pend[1], next_n)
                    pend = (b, par)
            if pend is not None:
                pending_tr.append((pend[0], pend[1], next_n))

        pending_tr = []

        def flush_pending():
            while pending_tr:
                b_, par_, n_ = pending_tr.pop(0)
                prenorm_b(b_, par_, n_)

        def ffn(l, i, gain_n, next_n, final_store=None):
            b_load(WdS[l][i], scr[f"Wd{l}{i}"], FC)
            gain_load(gain_n)
            slot_of = {}

            def group(c, g):
                if c not in slot_of:
                    slot_of[c] = a_next(1 if c == 1 else 2)
                slot = slot_of[c]
                pp = ctr["ft"] % 2
                ctr["ft"] += 1
                pst = PS[pp]
                banks = [bpb[2 * pp], bpb[2 * pp + 1]]

                def mm(e):
                    for h in range(2):
                        for k in range(KC):
                            i_ = e.matmul(pst[:, h, :], lhsT=AR[:, slot, h, k, :], rhs=xnT[:, k, g * GT:(g + 1) * GT],
                                          start=(k == 0), stop=(k == KC - 1))
                    return i_
                P.op(P.pe, mm, reads=[bA[slot], bxn[g]], writes=banks)
                P.op(P.act, lambda e: e.activation(out=ffn_t[:, pp, :], in_=pst[:, 0, :], func=AF.Silu),
                     reads=[banks[0]], writes=[bft[pp]])
                P.op(P.dve, lambda e: e.tensor_tensor(
                    out=hT[:, c, g * GT:(g + 1) * GT], in0=ffn_t[:, pp, :], in1=pst[:, 1, :], op=ALU.mult),
                    reads=[bft[pp], banks[1]], writes=[bh[c]])

            group(0, 0)
            group(1, 0)
            flush_pending()
            group(0, 1)
            group(1, 1)
            for c in range(2, FC):
                for g in range(NG):
                    group(c, g)
            project_postnorm(lambda k, b: hT[:, k, b * 128:(b + 1) * 128], bh, FC,
                             lambda k, half: BT[:, k, half * GT:(half + 1) * GT], bB, 0.5, next_n, final_store)

        def row_ap(r):
            return hT[:, r, :] if r < FC else BT[:, 16 + (r - FC), :]

        def row_buf(r):
            return bh[r] if r < FC else bB[16 + (r - FC)]

        def hrow_f32(r):
            return row_ap(r).bitcast(F32)

        def run_pipelined(unit_lists, depth=2):
            items = []
            maxlen = max(len(o) for o in unit_lists)
            step = -(-maxlen // depth)
            for u, ops in enumerate(unit_lists):
                for k, th in enumerate(ops):
                    items.append((u * step + k, u, k, th))
            items.sort(key=lambda t: (t[0], t[1], t[2]))
            for _, _, _, th in items:
                th()

        def recurrent(gain_n, next_n, p=1):
            peng = P.dve if p == 0 else P.pool
            b_load(WoutS, scr["Wout"], 8)
            gain_load(gain_n)
            slots = {}

            def unit_ops(c, g, u):
                par = u % 2
                rs = u % 4
                prs = (u - 1) % 4
                R = [rs * 7 + r for r in range(7)]
                r_xc, r_xcb, r_tr, r_a2, r_ti, r_gt, r_gs = R
                r_a = r_tr
                r_h = r_ti
                xc, tr_, a2, ti, gt, gs = (hrow_f32(r) for r in (r_xc, r_tr, r_a2, r_ti, r_gt, r_gs))
                a_ = tr_
                h_ = ti
                xcb = row_ap(r_xcb)[:, 0:GT]
                pst = PS[par]
                bk = [bpb[2 * par], bpb[2 * par + 1]]
                pst2 = PS[2 + par]
                bk2 = [bpb[4 + 2 * par], bpb[5 + 2 * par]]
                xr = xr_sb[:, rs, :]
                bxr_c = bxr[rs]
                bxr_p = bxr[prs]
                tsl = slice(g * GT, (g + 1) * GT)
                ops = []
                if g == 0:
                    ops.append(lambda: slots.__setitem__(c, a_next()))

                def mm(e):
                    slot = slots[c]
                    for h in range(2):
                        for k in range(KC):
                            i_ = e.matmul(pst[:, h, :], lhsT=AR[:, slot, h, k, :], rhs=xnT[:, k, tsl],
                                          start=(k == 0), stop=(k == KC - 1))
                    return i_
                ops.append(lambda: P.op(P.pe, mm, reads=[bA[slots[c]], bxn[g]], writes=bk))
                ops.append(lambda: P.op(P.act, lambda e: e.activation(out=xr[:, 3:GT + 3], in_=pst[:, 0, :], func=AF.Copy),
                                        reads=[bk[0]], writes=[bxr_c]))
                ops.append(lambda: P.op(P.dve, lambda e: e.tensor_copy(out=gs, in_=pst[:, 1, :]),
                                        reads=[bk[1]], writes=[row_buf(r_gs)]))
                if u == 0:
                    ops.append(flush_pending)
                if g == 0:
                    ops.append(lambda: P.op(P.dve, lambda e: e.tensor_copy(out=xr[:, 0:3], in_=cstate[:, c, :]),
                                            reads=[bcst], writes=[bxr_c]))
                else:
                    def halo():
                        P.op(P.dve, lambda e: e.tensor_copy(out=xr[:, 0:3], in_=xr_sb[:, prs, GT:GT + 3]),
                             reads=[bxr_p], writes=[bxr_c])
                        P.op(P.dve, lambda e: e.tensor_copy(out=cstate[:, c, :], in_=xr[:, GT:GT + 3]),
                             reads=[bxr_c], writes=[bcst])
                    ops.append(halo)
                ops.append(lambda: P.op(P.dve, lambda e: e.tensor_scalar(
                    out=xc, in0=xr[:, 0:GT], scalar1=recfm[:, 0, c:c + 1], scalar2=recfm[:, 4, c:c + 1],
                    op0=ALU.mult, op1=ALU.add), reads=[bxr_c, bconst], writes=[row_buf(r_xc)]))
                for k in range(1, 4):
                    ops.append(lambda k=k: P.op(P.dve, lambda e: e.scalar_tensor_tensor(
                        out=xc, in0=xr[:, k:k + GT], scalar=recfm[:, k, c:c + 1], in1=xc, op0=ALU.mult, op1=ALU.add),
                        reads=[bxr_c, bconst, row_buf(r_xc)], writes=[row_buf(r_xc)]))
                ops.append(lambda: P.op(P.act, lambda e: e.activation(out=xcb, in_=xc, func=AF.Copy),
                                        reads=[row_buf(r_xc)], writes=[row_buf(r_xcb)]))

                def mmg(e):
                    e.matmul(pst2[:, 0, :], lhsT=bd[:, 0, c, :], rhs=xcb, start=True, stop=True)
                    return e.matmul(pst2[:, 1, :], lhsT=bd[:, 1, c, :], rhs=xcb, start=True, stop=True)
                ops.append(lambda: P.op(P.pe, mmg, reads=[bbd, row_buf(r_xcb)], writes=bk2))
                ops.append(lambda: P.op(P.act, lambda e: e.activation(
                    out=tr_, in_=pst2[:, 0, :], func=AF.Tanh, scale=0.5, bias=der[:, 0, c:c + 1]),
                    reads=[bk2[0], bder], writes=[row_buf(r_tr)]))
                ops.append(lambda: P.op(P.act, lambda e: e.activation(
                    out=ti, in_=pst2[:, 1, :], func=AF.Tanh, scale=0.5, bias=der[:, 1, c:c + 1]),
                    reads=[bk2[1], bder], writes=[row_buf(r_ti)]))
                ops.append(lambda: P.op(peng, lambda e: e.tensor_tensor(out=gt, in0=gs, in1=gs, op=ALU.mult),
                                        reads=[row_buf(r_gs)], writes=[row_buf(r_gt)]))
                ops.append(lambda: P.op(peng, lambda e: e.tensor_scalar(out=gt, in0=gt, scalar1=0.044715, scalar2=1.0,
                                                                          op0=ALU.mult, op1=ALU.add),
                                        reads=[row_buf(r_gt)], writes=[row_buf(r_gt)]))
                ops.append(lambda: P.op(peng, lambda e: e.tensor_tensor(out=gt, in0=gt, in1=gs, op=ALU.mult),
                                        reads=[row_buf(r_gt), row_buf(r_gs)], writes=[row_buf(r_gt)]))
                ops.append(lambda: P.op(P.act, lambda e: e.activation(out=gt, in_=gt, func=AF.Tanh, scale=0.7978845608028654),
                                        reads=[row_buf(r_gt)], writes=[row_buf(r_gt)]))
                ops.append(lambda: P.op(P.act, lambda e: e.activation(
                    out=a2, in_=tr_, func=AF.Exp, scale=der[:, 2, c:c + 1], bias=der[:, 2, c:c + 1]),
                    reads=[row_buf(r_tr), bder], writes=[row_buf(r_a2)]))
                ops.append(lambda: P.op(P.act, lambda e: e.activation(
                    out=a_, in_=tr_, func=AF.Exp, scale=der[:, 3, c:c + 1], bias=der[:, 3, c:c + 1]),
                    reads=[row_buf(r_tr), bder], writes=[row_buf(r_a)]))
                ops.append(lambda: P.op(P.act, lambda e: e.activation(out=a2, in_=a2, func=AF.Ln, scale=-1.0, bias=1.000001),
                                        reads=[row_buf(r_a2)], writes=[row_buf(r_a2)]))
                ops.append(lambda: P.op(P.act, lambda e: e.activation(out=a2, in_=a2, func=AF.Exp, scale=0.5),
                                        reads=[row_buf(r_a2)], writes=[row_buf(r_a2)]))
                ops.append(lambda: P.op(P.dve, lambda e: e.scalar_tensor_tensor(out=ti, in0=ti, scalar=1.0, in1=xc,
                                                                                op0=ALU.add, op1=ALU.mult),
                                        reads=[row_buf(r_ti), row_buf(r_xc)], writes=[row_buf(r_ti)]))
                ops.append(lambda: P.op(P.dve, lambda e: e.tensor_tensor(out=a2, in0=a2, in1=ti, op=ALU.mult),
                                        reads=[row_buf(r_a2), row_buf(r_ti)], writes=[row_buf(r_a2)]))
                if g == 0:
                    init = hstate[:, c:c + 1]
                    init_b = bhst
                else:
                    init = hrow_f32(prs * 7 + 4)[:, GT - 1:GT]
                    init_b = row_buf(prs * 7 + 4)
                ops.append(lambda: P.op(P.dve, lambda e: e.tensor_tensor_scan(
                    out=h_, data0=a_, data1=a2, initial=init, op0=ALU.mult, op1=ALU.add),
                    reads=[row_buf(r_a), row_buf(r_a2), init_b], writes=[row_buf(r_h)]))
                if g == 1:
                    ops.append(lambda: P.op(P.dve, lambda e: e.tensor_copy(out=hstate[:, c:c + 1], in_=h_[:, GT - 1:GT]),
                                            reads=[row_buf(r_h)], writes=[bhst]))
                ops.append(lambda: P.op(P.dve, lambda e: e.scalar_tensor_tensor(out=gt, in0=gt, scalar=1.0, in1=gs,
                                                                                op0=ALU.add, op1=ALU.mult),
                                        reads=[row_buf(r_gt), row_buf(r_gs)], writes=[row_buf(r_gt)]))
                ops.append(lambda: P.op(P.dve, lambda e: e.scalar_tensor_tensor(
                    out=BT[:, 8 + c, tsl], in0=h_, scalar=0.25, in1=gt, op0=ALU.mult, op1=ALU.mult),
                    reads=[row_buf(r_h), row_buf(r_gt)], writes=[bB[8 + c]]))
                return ops

            units = []
            u = 0
            for c in range(8):
                for g in range(NG):
                    units.append(unit_ops(c, g, u))
                    u += 1
            run_pipelined(units, depth=4)
            project_postnorm(lambda k, b: BT[:, 8 + k, b * 128:(b + 1) * 128], bB[8:16], 8,
                             lambda k, half: BT[:, k, half * GT:(half + 1) * GT], bB[0:8], 1.0, next_n, None)

        wv_loaded = [False]

        def kv_stage():
            ev = 0
            if not wv_loaded[0]:
                P.dma(P.sp, lambda e: e.dma_start(out=wv[:], in_=WvS.rearrange("(kc p) n -> p kc n", p=128)), dwv,
                      reads=[scr["Wv"]], writes=[bwv])
                wv_loaded[0] = True
            for j in range(2):
                slot = a_next()
                for h in range(2):
                    kvh = 2 * j + h
                    for g in range(NG):
                        bk = ctr["tp"] % 4
                        ctr["tp"] += 1

                        def mm(e, slot=slot, h=h, g=g, bk=bk):
                            for k in range(KC):
                                i_ = e.matmul(bank(bk), lhsT=AR[:, slot, h, k, :], rhs=xnT[:, k, g * GT:(g + 1) * GT],
                                              start=(k == 0), stop=(k == KC - 1))
                            return i_
                        P.op(P.pe, mm, reads=[bA[slot], bxn[g]], writes=[bpb[bk]])
                        flush_pending()
                        dst = kT[:, kvh, 128 + g * GT:128 + (g + 1) * GT]
                        if ev % 2 == 0:
                            P.op(P.act, lambda e, dst=dst, bk=bk: e.activation(out=dst, in_=bank(bk), func=AF.Copy),
                                 reads=[bpb[bk]], writes=[bkT[kvh]])
                        else:
                            P.op(P.dve, lambda e, dst=dst, bk=bk: e.tensor_copy(out=dst, in_=bank(bk)),
                                 reads=[bpb[bk]], writes=[bkT[kvh]])
                        ev += 1
            for b in range(NB):
                bk = 4 + (b % 4)

                def mmv(e, b=b, bk=bk):
                    for k in range(KC):
                        i_ = e.matmul(bank(bk)[:, 0:256], lhsT=xnT[:, k, b * 128:(b + 1) * 128], rhs=wv[:, k, :],
                                      start=(k == 0), stop=(k == KC - 1))
                    return i_
                P.op(P.pe, mmv, reads=[bxn[b // 4], bwv], writes=[bpb[bk]])
                src = bank(bk)[:, 0:256].rearrange("p (h d) -> p h d", h=4)
                if b % 2 == 0:
                    P.op(P.act, lambda e, b=b, src=src: e.activation(out=vpad[:, 1 + b, :, 0:64], in_=src, func=AF.Copy),
                         reads=[bpb[bk]], writes=[bvp[1 + b]])
                    P.op(P.act, lambda e, b=b, src=src: e.activation(out=vpad[:, 1 + b, :, 128:192], in_=src, func=AF.Copy),
                         reads=[bpb[bk]], writes=[bvp[1 + b]])
                else:
                    P.op(P.dve, lambda e, b=b, src=src: e.tensor_copy(out=vpad[:, 1 + b, :, 0:64], in_=src),
                         reads=[bpb[bk]], writes=[bvp[1 + b]])
                    P.op(P.dve, lambda e, b=b, src=src: e.tensor_copy(out=vpad[:, 1 + b, :, 128:192], in_=src),
                         reads=[bpb[bk]], writes=[bvp[1 + b]])

        def attention(p, gain_n, next_n):
            b_load(WoS, scr["Wo"], 8)
            gain_load(gain_n)
            ev = 0
            for j in range(4):
                slot = a_next()
                for h in range(2):
                    ch = 2 * j + h
                    for g in range(NG):
                        bk = ctr["tp"] % 4
                        ctr["tp"] += 1

                        def mm(e, slot=slot, h=h, g=g, bk=bk):
                            for k in range(KC):
                                i_ = e.matmul(bank(bk), lhsT=AR[:, slot, h, k, :], rhs=xnT[:, k, g * GT:(g + 1) * GT],
                                              start=(k == 0), stop=(k == KC - 1))
                            return i_
                        P.op(P.pe, mm, reads=[bA[slot], bxn[g]], writes=[bpb[bk]])
                        flush_pending()
                        dst = hT[:, ch, g * GT:(g + 1) * GT]
                        if ev % 2 == 0:
                            P.op(P.act, lambda e, dst=dst, bk=bk: e.activation(out=dst, in_=bank(bk), func=AF.Copy),
                                 reads=[bpb[bk]], writes=[bh[ch]])
                        else:
                            P.op(P.dve, lambda e, dst=dst, bk=bk: e.tensor_copy(out=dst, in_=bank(bk)),
                                 reads=[bpb[bk]], writes=[bh[ch]])
                        ev += 1

            otp = PS[3]
            otv = otp[:].rearrange("p a b -> p (a b)").rearrange("p (c t) -> p c t", c=8)

            def quad_ops(b, kvh, q):
                par = q % 2
                rs = q % 3
                gb = p * NB + b
                mi = 1 if gb == 0 else 0
                sp_ = PS[par]
                sbk = [bpb[2 * par], bpb[2 * par + 1]]
                sv = sp_[:].rearrange("p a b -> p (a b)").rearrange("p (j k) -> p j k", j=4)
                r0 = 8 + rs * 4
                sm = hT[:, r0:r0 + 2, :].rearrange("p a b -> p (a b)").bitcast(F32).rearrange("p (j k) -> p j k", j=4)
                pe_ = sm
                pn = row_ap(r0 + 2).rearrange("p (j k) -> p j k", j=4)
                pts = row_ap(r0 + 3)
                b_sm = [bh[r0], bh[r0 + 1]]
                b_pe = b_sm
                b_pn = [row_buf(r0 + 2)]
                b_pt = [row_buf(r0 + 3)]
                st = astat[:, rs]
                bst = bastat[rs]
                ptb = 4 + par
                ptv = bank_bf(ptb)
                ops = []

                def mms(e):
                    for jj in range(4):
                        ch = 2 * kvh + jj // 2
                        ph = jj % 2
                        i_ = e.matmul(sv[:, SLOT[jj], :], lhsT=hT[ph * 64:(ph + 1) * 64, ch, b * 128:(b + 1) * 128],
                                      rhs=kT[ph * 64:(ph + 1) * 64, kvh, b * 128:b * 128 + 256], start=True, stop=True)
                    return i_
                ops.append(lambda: P.op(P.pe, mms, reads=[bh[2 * kvh], bh[2 * kvh + 1], bkT[kvh]], writes=sbk))
                ops.append(lambda: P.op(P.dve, lambda e: e.scalar_tensor_tensor(
                    out=sm, in0=sv, scalar=0.125, in1=mask[:, mi, :].unsqueeze(1).broadcast_to([128, 4, 256]),
                    op0=ALU.mult, op1=ALU.add), reads=sbk + [bconst], writes=b_sm))
                ops.append(lambda: P.op(P.dve, lambda e: e.tensor_reduce(out=st[:, 0, :], in_=sm, axis=AX.X, op=ALU.max),
                                        reads=b_sm, writes=[bst]))
                ops.append(lambda: P.op(P.dve, lambda e: e.tensor_tensor(out=st[:, 1, :], in0=st[:, 0, :],
                                                                         in1=sinks[:, 4 * kvh:4 * kvh + 4], op=ALU.max),
                                        reads=[bst, bconst], writes=[bst]))
                ops.append(lambda: P.op(P.dve, lambda e: e.tensor_scalar(out=st[:, 2, :], in0=st[:, 1, :], scalar1=-1.0,
                                                                         scalar2=None, op0=ALU.mult), reads=[bst], writes=[bst]))
                ops.append(lambda: P.op(P.dve, lambda e: e.tensor_tensor(out=st[:, 3, :], in0=sinks[:, 4 * kvh:4 * kvh + 4],
                                                                         in1=st[:, 1, :], op=ALU.subtract),
                                        reads=[bst, bconst], writes=[bst]))
                for jj in range(4):
                    ops.append(lambda jj=jj: P.op(P.act, lambda e: e.activation(
                        out=pe_[:, jj, :], in_=sm[:, jj, :], func=AF.Exp, bias=st[:, 2, jj:jj + 1],
                        accum_out=st[:, 4, jj:jj + 1]), reads=b_sm + [bst], writes=b_pe + [bst]))
                ops.append(lambda: P.op(P.act, lambda e: e.activation(out=st[:, 5, :], in_=st[:, 3, :], func=AF.Exp),
                                        reads=[bst], writes=[bst]))
                ops.append(lambda: P.op(P.dve, lambda e: e.tensor_tensor(out=st[:, 6, :], in0=st[:, 4, :], in1=st[:, 5, :],
                                                                         op=ALU.add), reads=[bst], writes=[bst]))
                ops.append(lambda: P.op(P.dve, lambda e: e.reciprocal(out=st[:, 7, :], in_=st[:, 6, :]),
                                        reads=[bst], writes=[bst]))
                ops.append(lambda: P.op(P.dve, lambda e: e.tensor_tensor(
                    out=pn, in0=pe_, in1=st[:, 7, :].unsqueeze(2).broadcast_to([128, 4, 256]), op=ALU.mult),
                    reads=b_pe + [bst], writes=b_pn))

                def trp(e):
                    for jj in range(4):
                        for kb in range(2):
                            idx = jj * 2 + kb
                            i_ = e.transpose(out=ptv[:, idx * 128:(idx + 1) * 128], in_=pn[:, jj, kb * 128:(kb + 1) * 128],
                                             identity=ident[:])
                    return i_
                ops.append(lambda: P.op(P.pe, trp, reads=b_pn + [bident], writes=[bpb[ptb]]))
                ops.append(lambda: P.op(P.act, lambda e: e.activation(out=pts, in_=ptv, func=AF.Copy),
                                        reads=[bpb[ptb]], writes=b_pt))

                def mmo(e):
                    for cc in range(2):
                        ch = 2 * kvh + cc
                        n = 0
                        for hl in range(2):
                            jj = 2 * cc + hl
                            for kb in range(2):
                                idx = SLOT[jj] * 2 + kb
                                i_ = e.matmul(otv[:, ch, :], lhsT=vpad[:, b + kb, kvh, hl * 64:hl * 64 + 128],
                                              rhs=pts[:, idx * 128:(idx + 1) * 128], start=(n == 0), stop=(n == 3))
                                n += 1
                    return i_
                ops.append(lambda: P.op(P.pe, mmo, reads=b_pt + [bvp[b], bvp[b + 1]], writes=[bpb[6], bpb[7]]))
                if kvh == 3:
                    ops.append(lambda: P.op(P.dve, lambda e: e.tensor_copy(out=BT[:, 8:16, b * 128:(b + 1) * 128], in_=otv),
                                            reads=[bpb[6], bpb[7]], writes=bB[8:16]))
                return ops

            quads = []
            q = 0
            for b in range(NB):
                for kvh in range(4):
                    quads.append(quad_ops(b, kvh, q))
                    q += 1
            run_pipelined(quads, depth=3)
            return

        def attention_tail():
            P.op(P.act, lambda e: e.activation(out=kT[:, :, 0:128], in_=kT[:, :, T:T + 128], func=AF.Copy),
                 reads=bkT, writes=bkT)
            P.op(P.act, lambda e: e.activation(out=vpad[:, 0], in_=vpad[:, NB], func=AF.Copy),
                 reads=[bvp[NB]], writes=[bvp[0]])

        def load_x(p, b):
            P.dma(P.sp, lambda e: e.dma_start(out=X[:, b, :], in_=x_d[p * T + b * 128:p * T + (b + 1) * 128, :]),
                  dXl[b], writes=[bX[b]])

        for b in range(NB):
            load_x(0, b)
        P.wait(P.pool, [b_.w for b_ in list(scr.values()) + list(wg00_buf.values()) if b_.w is not None] + [bbd.w, bident.w])

        last_store = [None] * NB
        for p in range(npass):
            new_engine_sems(f"p{p}")

            def store(b, p=p):
                last_store[b] = P.dma(P.sp, lambda e: e.dma_start(
                    out=out_d[p * T + b * 128:p * T + (b + 1) * 128, :], in_=X[:, b, :]), dXs[b], reads=[bX[b]])
                if p + 1 < npass and b >= 1:
                    load_x(p + 1, b - 1)
            ns = 9 if stop_after is None else stop_after
            if ns >= 1:
                prenorm_full(0)
            if ns >= 2:
                ffn(0, 0, 0, 1)
            if ns >= 3:
                recurrent(1, 2, p)
            if ns >= 4:
                ffn(0, 1, 2, 6)
            if ns >= 5:
                kv_stage()
            if ns >= 6:
                prenorm_full(3)
            if ns >= 7:
                ffn(1, 0, 3, 4)
            if ns >= 8:
                attention(p, 4, 5)
                project_postnorm(lambda k, b: BT[:, 8 + k, b * 128:(b + 1) * 128], bB[8:16], 8,
                                 lambda k, half: BT[:, k, half * GT:(half + 1) * GT], bB[0:8], 1.0, 5, None)
                attention_tail()
            if ns >= 9:
                ffn(1, 1, 5, None, final_store=store)
            else:
                for b in range(NB):
                    store(b)
            if p + 1 < npass:
                load_x(p + 1, NB - 1)
        P.wait(P.sp, [t for t in last_store if t is not None])
        P.wait(P.pool, [t for t in last_store if t is not None])
        with nc.Block() as block:
            P.finish(block)
    return nc


def _fm(v):
    return np.ascontiguousarray(np.asarray(v, np.float32).reshape(8, 128).T)


def make_consts():
    ident = np.eye(128, dtype=np.float32)
    qi = np.arange(128)[:, None]
    kj = np.arange(256)[None, :]
    delta = qi + 128 - kj
    valid = (delta >= 0) & (delta < 128)
    m0 = np.where(valid, 0.0, -1e30).astype(np.float32)
    valid1 = valid & (kj >= 128)
    m1 = np.where(valid1, 0.0, -1e30).astype(np.float32)
    mask = np.concatenate([m0, m1], axis=1)
    return ident, np.ascontiguousarray(mask)


def shared_inputs(norms, ffn_w_gate, ffn_w_up, ffn_w_down, a_w_in, a_conv_w, a_conv_b, a_gate_a_w, a_gate_a_b,
                  a_gate_x_w, a_gate_x_b, a_lambda, a_w_out, kv_norm, w_kv, b_w_q, b_sinks, b_w_o):
    f = lambda a: np.ascontiguousarray(np.asarray(a, np.float32))
    norms = f(norms)
    pre = [norms[0, 0], norms[0, 2], norms[0, 4], norms[1, 0], norms[1, 2], norms[1, 4], f(kv_norm)]
    prenorm_fm = np.concatenate([_fm(v) for v in pre], axis=1)
    postnorm = np.stack([norms[0, 1], norms[0, 3], norms[0, 5], norms[1, 1], norms[1, 3], norms[1, 5]])
    cw = f(a_conv_w)[0]
    rec = [cw[0], cw[1], cw[2], cw[3], f(a_conv_b)[0], f(a_gate_a_b)[0], f(a_gate_x_b)[0], f(a_lambda)[0]]
    rec_fm = np.concatenate([_fm(v) for v in rec], axis=1)

    def gate_layout(w):
        w = f(w)[0]
        w = w.reshape(8, 2, 64, 64)
        return np.ascontiguousarray(w.transpose(1, 2, 0, 3))
    ident, mask = make_consts()
    return {
        "ffn_w_gate": f(ffn_w_gate), "ffn_w_up": f(ffn_w_up), "ffn_w_down": f(ffn_w_down),
        "a_w_in": f(a_w_in)[0], "a_w_out": f(a_w_out)[0], "w_kv": f(w_kv), "b_w_q": f(b_w_q)[0], "b_w_o": f(b_w_o)[0],
        "a_gate_a_w": gate_layout(a_gate_a_w), "a_gate_x_w": gate_layout(a_gate_x_w),
        "prenorm_fm": np.ascontiguousarray(prenorm_fm), "postnorm": np.ascontiguousarray(postnorm),
        "rec_fm": np.ascontiguousarray(rec_fm),
        "sinks": np.ascontiguousarray(f(b_sinks)[0].reshape(4, 4)[:, [0, 2, 1, 3]].reshape(16)), "ident": ident, "mask": mask,
    }


_NC_CACHE = {}


def kernel(x, **params):
    x = np.asarray(x, np.float32)
    bsz, seq, _ = x.shape
    shared = shared_inputs(**params)
    if seq not in _NC_CACHE:
        _NC_CACHE[seq] = build_program(seq)
    nc = _NC_CACHE[seq]
    in_maps = []
    for c in range(bsz):
        m = dict(shared)
        m["x"] = np.ascontiguousarray(x[c])
        in_maps.append(m)
    res = run_bass_kernel_spmd(nc, in_maps, core_ids=list(range(bsz)))
    return np.stack([np.asarray(r["out"], np.float32) for r in res.results], axis=0)
```

```python
import contextlib
import os
KDBG = set(os.environ.get('KDBG', '').split(','))
import numpy as np
import concourse.bass as bass
import concourse.mybir as mybir
from concourse.bass_utils import run_bass_kernel_spmd

F32 = mybir.dt.float32
BF16 = mybir.dt.bfloat16
ALU = mybir.AluOpType
AF = mybir.ActivationFunctionType
AX = mybir.AxisListType

D = 1024
KC = 8
DFF = 2816
FC = 22
T = 1024
NB = 8
GT = 512
NG = 2
EPS = 1e-6
NCORES = 8
SLOT = [0, 2, 1, 3]
SEQ = 8192


class Tok:
    __slots__ = ("sem", "val")

    def __init__(self, sem, val):
        self.sem = sem
        self.val = val


class Buf:
    __slots__ = ("name", "w", "r", "excl")

    def __init__(self, name, excl=False):
        self.name = name
        self.w = None
        self.r = []
        self.excl = excl


class DSem:
    __slots__ = ("sem", "cnt")

    def __init__(self, sem):
        self.sem = sem
        self.cnt = 0


class Eng:
    def __init__(self, name):
        self.name = name
        self.ops = []
        self.sem = None
        self.cnt = 0
        self.waited = {}

    def set_sem(self, sem):
        self.sem = sem
        self.cnt = 0


class Prog:
    def __init__(self, nc):
        self.nc = nc
        self.pe = Eng("tensor")
        self.act = Eng("scalar")
        self.dve = Eng("vector")
        self.pool = Eng("gpsimd")
        self.sp = Eng("sync")
        self.engs = [self.pe, self.act, self.dve, self.pool, self.sp]

    def _deps(self, eng, reads, writes, extra=()):
        best = {}

        def add(t):
            k = id(t.sem)
            if k not in best or best[k].val < t.val:
                best[k] = t
        for b in reads:
            if b.w is not None:
                add(b.w)
            if b.excl:
                for t in b.r:
                    if t.sem is not eng.sem:
                        add(t)
        for b in writes:
            if b.w is not None:
                add(b.w)
            for t in b.r:
                add(t)
        for t in extra:
            add(t)
        for k, t in best.items():
            if eng.waited.get(k, 0) < t.val:
                eng.ops.append(("wait", t.sem, t.val))
                eng.waited[k] = t.val

    @staticmethod
    def _mark(tok, reads, writes):
        for b in reads:
            b.r = [t for t in b.r if t.sem is not tok.sem] + [tok]
        for b in writes:
            b.w = tok
            b.r = []

    def op(self, eng, fn, reads=(), writes=(), extra=()):
        self._deps(eng, reads, writes, extra)
        eng.cnt += 1
        tok = Tok(eng.sem, eng.cnt)
        eng.ops.append(("op", fn, eng.sem, 1))
        self._mark(tok, reads, writes)
        return tok

    def dma(self, eng, fn, dsem, reads=(), writes=(), extra=(), deps=True):
        if deps:
            self._deps(eng, reads, writes, extra)
        dsem.cnt += 16
        tok = Tok(dsem.sem, dsem.cnt)
        eng.ops.append(("op", fn, dsem.sem, 16))
        self._mark(tok, reads, writes)
        return tok

    def wait(self, eng, toks):
        self._deps(eng, (), (), toks)

    def finish(self, block):
        def run(e, ops):
            for o in ops:
                if o[0] == "wait":
                    e.wait_ge(o[1], o[2])
                else:
                    ins = o[1](e)
                    ins.then_inc(o[2], o[3])

        pe, act, dve, pool, sp = self.pe, self.act, self.dve, self.pool, self.sp

        @block.tensor
        def _(e):
            run(e, pe.ops)

        @block.scalar
        def _(e):
            run(e, act.ops)

        @block.vector
        def _(e):
            run(e, dve.ops)

        @block.gpsimd
        def _(e):
            run(e, pool.ops)

        @block.sync
        def _(e):
            run(e, sp.ops)


def build_program(S, stop_after=None):
    npass = S // T
    nc = bass.Bass("TRN2", target_bir_lowering=False)

    def din(name, shape, dt=F32):
        return nc.dram_tensor(name, list(shape), dt, kind="ExternalInput").ap()

    def dscr(name, shape, dt=BF16):
        return nc.dram_tensor(name, list(shape), dt, kind="Internal").ap()

    x_d = din("x", [S, D])
    out_d = nc.dram_tensor("out", [S, D], F32, kind="ExternalOutput").ap()
    wg_d = din("ffn_w_gate", [2, 2, D, DFF])
    wu_d = din("ffn_w_up", [2, 2, D, DFF])
    wd_d = din("ffn_w_down", [2, 2, DFF, D])
    win_d = din("a_w_in", [D, 2 * D])
    wout_d = din("a_w_out", [D, D])
    wkv_d = din("w_kv", [D, 512])
    wq_d = din("b_w_q", [D, D])
    wo_d = din("b_w_o", [D, D])
    gaw_d = din("a_gate_a_w", [2, 64, 8, 64])
    gxw_d = din("a_gate_x_w", [2, 64, 8, 64])
    prefm_d = din("prenorm_fm", [128, 7 * 8])
    postn_d = din("postnorm", [6, D])
    recfm_d = din("rec_fm", [128, 8 * 8])
    sinks_d = din("sinks", [16])
    ident_d = din("ident", [128, 128])
    mask_d = din("mask", [128, 2 * 256])

    WgS = [[dscr(f"WgS{l}{i}", [FC, 128, 2, KC, 128]) for i in range(2)] for l in range(2)]
    WdS = [[dscr(f"WdS{l}{i}", [DFF, D]) for i in range(2)] for l in range(2)]
    WinS = dscr("WinS", [8, 128, 2, KC, 128])
    WqS = dscr("WqS", [4, 128, 2, KC, 128])
    WkS = dscr("WkS", [2, 128, 2, KC, 128])
    WoutS = dscr("WoutS", [D, D])
    WoS = dscr("WoS", [D, D])
    WvS = dscr("WvS", [D, 256])

    with contextlib.ExitStack() as es:
        def sb(name, shape, dt):
            return es.enter_context(nc.sbuf_tensor("sb_" + name, list(shape), dt))

        def psum(name, shape, dt):
            return es.enter_context(nc.psum_tensor(name, list(shape), dt))

        def sem(name):
            return es.enter_context(nc.semaphore(name))

        X = sb("X", [128, NB, D], F32)
        xnT = sb("xnT", [128, KC, T], BF16)
        hT = sb("hT", [128, FC, T], BF16)
        BT = sb("BT", [128, FC, D], BF16)
        AR = sb("AR", [128, 3, 2, KC, 128], BF16)
        gain = sb("gain", [128, D], F32)
        kT = sb("kT", [128, 4, T + 128], BF16)
        vpad = sb("vpad", [128, NB + 1, 4, 192], BF16)
        wv = sb("wv", [128, KC, 256], BF16)
        bd = sb("bd", [128, 2, KC, 128], BF16)
        ident = sb("ident", [128, 128], BF16)
        mask = sb("mask", [128, 2, 256], F32)
        mask_bf = sb("mask_bf", [128, 2, 256], BF16)
        prefm = sb("prefm", [128, 7, 8], F32)
        recfm = sb("recfm", [128, 8, 8], F32)
        der = sb("der", [128, 6, 8], F32)
        sinks = sb("sinks", [128, 16], F32)
        half_c = sb("half_c", [128, GT], F32)
        mhalf_c = sb("mhalf_c", [128, 1], F32)
        hstate = sb("hstate", [128, 8], F32)
        cstate = sb("cstate", [128, 8, 3], F32)
        xr_sb = sb("xr_sb", [128, 4, GT + 3], F32)
        xs_bf = sb("xs_bf", [128, 2, D], BF16)
        junk = sb("junk", [128, D], BF16)
        ffn_t = sb("ffn_t", [128, 2, GT], F32)
        stat = sb("stat", [128, 4, 16], F32)
        astat = sb("astat", [128, 4, 8, 4], F32)

        PS = [psum(f"ps{i}", [128, 2, GT], F32) for i in range(4)]

        def bank(i):
            return PS[i // 2][:, i % 2, :]

        def bank_bf(i):
            return PS[i // 2][:, i % 2, :].bitcast(BF16)

        P = Prog(nc)
        bX = [Buf(f"X{b}") for b in range(NB)]
        bxn = [Buf(f"xn{g}") for g in range(NG)]
        bh = [Buf(f"h{r}") for r in range(FC)]
        bB = [Buf(f"B{r}") for r in range(FC)]
        bA = [Buf(f"A{r}") for r in range(3)]
        bgain = Buf("gain")
        bkT3 = [[Buf(f"kT{h}_{i}") for i in range(3)] for h in range(4)]
        bkT = [b_ for row in bkT3 for b_ in row]
        bvp = [Buf(f"vpad{i}") for i in range(NB + 1)]
        bwv = Buf("wv")
        bbd = Buf("bd")
        bconst = Buf("const")
        bpb = [Buf(f"pb{i}", excl=True) for i in range(8)]
        bxr = [Buf("xr0"), Buf("xr1"), Buf("xr2"), Buf("xr3")]
        bxs = [Buf("xs0"), Buf("xs1")]
        bjunk = Buf("junk")
        bft = [Buf("ft0"), Buf("ft1")]
        bstat = [Buf(f"st{i}") for i in range(16)]
        bastat = [Buf("as0"), Buf("as1"), Buf("as2"), Buf("as3")]
        bhst = Buf("hstate")
        bcst = Buf("cstate")

        def new_engine_sems(tag):
            for e in P.engs:
                e.set_sem(sem(f"e_{e.name}_{tag}"))
        new_engine_sems("pro")
        dA = [DSem(sem(f"dA{i}")) for i in range(3)]
        dB = [DSem(sem(f"dB{i}")) for i in range(3)]
        dgain = DSem(sem("dgain"))
        dXl = [DSem(sem(f"dXl{b}")) for b in range(NB)]
        dXs = [DSem(sem(f"dXs{b}")) for b in range(NB)]
        dconst = DSem(sem("dconst"))
        dwv = DSem(sem("dwv"))

        P.dma(P.sp, lambda e: e.dma_start(out=mask[:].rearrange("p a b -> p (a b)"), in_=mask_d), dconst, writes=[bconst])
        P.dma(P.sp, lambda e: e.dma_start(out=prefm[:].rearrange("p a b -> p (a b)"), in_=prefm_d), dconst, writes=[bconst], deps=False)
        P.dma(P.sp, lambda e: e.dma_start(out=recfm[:].rearrange("p a b -> p (a b)"), in_=recfm_d), dconst, writes=[bconst], deps=False)
        P.dma(P.sp, lambda e: e.dma_start(out=sinks[:], in_=sinks_d.partition_broadcast(128)), dconst, writes=[bconst], deps=False)
        dident = DSem(sem("dident"))
        bident = Buf("ident")
        P.dma(P.pool, lambda e: e.dma_start(out=ident[:], in_=ident_d), dident, writes=[bident], deps=False)
        P.op(P.dve, lambda e: e.tensor_copy(out=mask_bf[:].rearrange("p a b -> p (a b)"), in_=mask[:].rearrange("p a b -> p (a b)")),
             reads=[bconst], writes=[bconst])
        P.op(P.dve, lambda e: e.memset(vpad[:].rearrange("p a b c -> p (a b c)"), 0.0), writes=bvp)
        P.op(P.dve, lambda e: e.memset(kT[:].rearrange("p a b -> p (a b)"), 0.0), writes=bkT)
        P.op(P.dve, lambda e: e.memset(bd[:].rearrange("p a b c -> p (a b c)"), 0.0), writes=[bbd])
        P.op(P.dve, lambda e: e.memset(hstate[:], 0.0), writes=[bhst])
        P.op(P.dve, lambda e: e.memset(cstate[:].rearrange("p a b -> p (a b)"), 0.0), writes=[bcst])
        bexp = Buf("expc")
        P.op(P.dve, lambda e: e.memset(half_c[:], 0.5), writes=[bexp])
        P.op(P.dve, lambda e: e.memset(mhalf_c[:], -0.5), writes=[bexp])
        dbd = DSem(sem("dbd"))
        for gi, src in enumerate((gaw_d, gxw_d)):
            for half in range(2):
                P.dma(P.pool, lambda e, gi=gi, src=src, half=half: e.dma_start(
                    out=bd[half * 64:(half + 1) * 64, gi, :, half * 64:(half + 1) * 64], in_=src[half]),
                    dbd, writes=[bbd], deps=(gi == 0 and half == 0))
        bder = Buf("der")
        P.op(P.dve, lambda e: e.tensor_scalar(out=der[:, 0, :], in0=recfm[:, 5, :], scalar1=0.5, scalar2=None, op0=ALU.mult),
             reads=[bconst], writes=[bder])
        P.op(P.dve, lambda e: e.tensor_scalar(out=der[:, 1, :], in0=recfm[:, 6, :], scalar1=0.5, scalar2=None, op0=ALU.mult),
             reads=[bconst], writes=[bder])
        P.op(P.act, lambda e: e.activation(out=der[:, 4, :], in_=recfm[:, 7, :], func=AF.Exp, scale=-1.0),
             reads=[bconst], writes=[bder])
        P.op(P.act, lambda e: e.activation(out=der[:, 5, :], in_=der[:, 4, :], func=AF.Ln, bias=1.0),
             reads=[bder], writes=[bder])
        P.op(P.dve, lambda e: e.tensor_scalar(out=der[:, 2, :], in0=der[:, 5, :], scalar1=-8.0, scalar2=None, op0=ALU.mult),
             reads=[bder], writes=[bder])
        P.op(P.dve, lambda e: e.tensor_scalar(out=der[:, 3, :], in0=der[:, 5, :], scalar1=-4.0, scalar2=None, op0=ALU.mult),
             reads=[bder], writes=[bder])

        def cast_group(name, items):
            ds = DSem(sem("dc_" + name))
            b = Buf("scr_" + name)
            for (o, i) in items:
                if 'nocast' in KDBG:
                    continue
                P.dma(P.pool, lambda e, o=o, i=i: e.dma_start(out=o, in_=i), ds, writes=[b], deps=False)
            return b

        def a_items(dst, src, nchunk, col_of):
            it = []
            for c in range(nchunk):
                for h in range(2):
                    c0 = col_of(c, h)
                    it.append((dst[c, :, h], src[:, c0:c0 + 128].rearrange("(kc p) f -> p kc f", p=128)))
            return it

        def plain_items(dst, src, rows, piece):
            return [(dst[r0:r0 + piece, :], src[r0:r0 + piece, :]) for r0 in range(0, rows, piece)]

        scr = {}
        wg00_buf = {}

        def ffn_cast(l, i):
            it = []
            for c in range(FC):
                it.append((WgS[l][i][c, :, 0], wg_d[l, i][:, c * 128:(c + 1) * 128].rearrange("(kc p) f -> p kc f", p=128)))
                it.append((WgS[l][i][c, :, 1], wu_d[l, i][:, c * 128:(c + 1) * 128].rearrange("(kc p) f -> p kc f", p=128)))
            if (l, i) == (0, 0):
                bounds = [0, 3, 8, 15, FC]
                for gi in range(4):
                    bgrp = cast_group(f"Wg00_{gi}", it[2 * bounds[gi]:2 * bounds[gi + 1]])
                    for c in range(bounds[gi], bounds[gi + 1]):
                        wg00_buf[c] = bgrp
                    if gi == 0:
                        scr["Wd00"] = cast_group("Wd00", plain_items(WdS[l][i], wd_d[l, i], DFF, 352))
                scr["Wg00"] = bgrp
                return
            scr[f"Wg{l}{i}"] = cast_group(f"Wg{l}{i}", it)
            scr[f"Wd{l}{i}"] = cast_group(f"Wd{l}{i}", plain_items(WdS[l][i], wd_d[l, i], DFF, 352))

        ffn_cast(0, 0)
        scr["Win"] = cast_group("Win", a_items(WinS, win_d, 8, lambda c, h: h * D + c * 128))
        scr["Wout"] = cast_group("Wout", plain_items(WoutS, wout_d, D, 256))
        ffn_cast(0, 1)
        kit = []
        for j in range(2):
            for h in range(2):
                for d2 in range(2):
                    c0 = (2 * j + h) * 64
                    kit.append((WkS[j, :, h, :, d2 * 64:(d2 + 1) * 64],
                                wkv_d[:, c0:c0 + 64].rearrange("(kc p) f -> p kc f", p=128)))
        scr["Wk"] = cast_group("Wk", kit)
        scr["Wv"] = cast_group("Wv", plain_items(WvS, wkv_d[:, 256:512], D, 256))
        ffn_cast(1, 0)
        scr["Wq"] = cast_group("Wq", a_items(WqS, wq_d, 4, lambda c, h: (2 * c + h) * 128))
        scr["Wo"] = cast_group("Wo", plain_items(WoS, wo_d, D, 256))
        ffn_cast(1, 1)


        a_list = []
        ns_ = 9 if stop_after is None else stop_after
        for p in range(npass):
            if ns_ >= 2:
                for c in range(FC):
                    a_list.append((WgS[0][0][c], wg00_buf[c]))
            if ns_ >= 3:
                for c in range(8):
                    a_list.append((WinS[c], scr["Win"]))
            if ns_ >= 4:
                for c in range(FC):
                    a_list.append((WgS[0][1][c], scr["Wg01"]))
            if ns_ >= 5:
                for c in range(2):
                    a_list.append((WkS[c], scr["Wk"]))
            if ns_ >= 7:
                for c in range(FC):
                    a_list.append((WgS[1][0][c], scr["Wg10"]))
            if ns_ >= 8:
                for c in range(4):
                    a_list.append((WqS[c], scr["Wq"]))
            if ns_ >= 9:
                for c in range(FC):
                    a_list.append((WgS[1][1][c], scr["Wg11"]))
        a_state = {"issued": 0, "next": 0}

        def a_issue_upto(j):
            while a_state["issued"] <= min(j, len(a_list) - 1):
                k = a_state["issued"]
                src, sbuf_ = a_list[k]
                slot = k % 3
                P.dma(P.sp, lambda e, src=src, slot=slot: e.dma_start(out=AR[:, slot], in_=src),
                      dA[slot], reads=[sbuf_], writes=[bA[slot]])
                a_state["issued"] += 1

        def a_next(pref=2):
            j = a_state["next"]
            a_issue_upto(j + pref)
            a_state["next"] += 1
            return j % 3

        def b_load(src, scrbuf, nrows_chunks):
            for g0 in range(0, nrows_chunks, 8):
                n = min(8, nrows_chunks - g0)
                gi = g0 // 8
                P.dma(P.sp, lambda e, g0=g0, n=n: e.dma_start(
                    out=BT[:, g0:g0 + n, :],
                    in_=src[g0 * 128:(g0 + n) * 128, :].rearrange("(fc p) m -> p fc m", p=128)),
                    dB[gi], reads=[scrbuf], writes=bB[g0:g0 + n])

        def gain_load(n):
            P.dma(P.sp, lambda e: e.dma_start(out=gain[:], in_=postn_d[n].partition_broadcast(128)),
                  dgain, writes=[bgain])

        ctr = {"stat": 0, "xs": 0, "tp": 0, "ft": 0, "fp": 0, "ev": 0}

        def stat_slot():
            s = ctr["stat"] % 16
            ctr["stat"] += 1
            return s

        def prenorm_a(b):
            s = stat_slot()
            par = ctr["xs"] % 2
            ctr["xs"] += 1
            P.op(P.act, lambda e: e.activation(out=junk[:], in_=X[:, b, :], func=AF.Square, accum_out=stat[:, 0, s:s + 1]),
                 reads=[bX[b]], writes=[bjunk, bstat[s]])
            P.op(P.act, lambda e: e.activation(out=stat[:, 1, s:s + 1], in_=stat[:, 0, s:s + 1], func=AF.Ln,
                                               scale=1.0 / D, bias=EPS), reads=[bstat[s]], writes=[bstat[s]])
            P.op(P.act, lambda e: e.activation(out=stat[:, 2, s:s + 1], in_=stat[:, 1, s:s + 1], func=AF.Exp, scale=-0.5),
                 reads=[bstat[s]], writes=[bstat[s]])
            P.op(P.act, lambda e: e.activation(out=xs_bf[:, par, :], in_=X[:, b, :], func=AF.Copy, scale=stat[:, 2, s:s + 1]),
                 reads=[bX[b], bstat[s]], writes=[bxs[par]])
            return par

        def prenorm_b(b, par, n):
            if 'nopre_b' in KDBG:
                return
            bk = ctr["tp"] % 4
            ctr["tp"] += 1
            tpv = bank_bf(bk).rearrange("p (k t) -> p k t", k=KC)

            def tr(e):
                for k in range(KC):
                    i = e.transpose(out=tpv[:, k, :], in_=xs_bf[:, par, k * 128:(k + 1) * 128], identity=ident[:])
                return i
            P.op(P.pe, tr, reads=[bxs[par], bident], writes=[bpb[bk]])
            g = b // 4
            P.op(P.dve, lambda e: e.tensor_tensor(out=xnT[:, :, b * 128:(b + 1) * 128], in0=tpv,
                                                  in1=prefm[:, n, :].unsqueeze(2).broadcast_to([128, KC, 128]), op=ALU.mult),
                 reads=[bpb[bk], bconst], writes=[bxn[g]])

        def prenorm_full(n):
            pend = None
            for b in range(NB):
                par = prenorm_a(b)
                if pend is not None:
                    prenorm_b(pend[0], pend[1], n)
                pend = (b, par)
            prenorm_b(pend[0], pend[1], n)

        def project_postnorm(lhs_of, lhs_bufs, nk, rhs_of, rhs_bufs, coef, next_n, final_store):
            pend = None
            for b in range(NB):
                fp = ctr["fp"] % 2
                ctr["fp"] += 1
                pst = PS[2 + fp]
                banks = [bpb[4 + 2 * fp], bpb[5 + 2 * fp]]

                def mm(e, b=b, pst=pst):
                    for half in range(2):
                        for k in range(nk):
                            i = e.matmul(pst[:, half, :], lhsT=lhs_of(k, b), rhs=rhs_of(k, half),
                                         start=(k == 0), stop=(k == nk - 1))
                    return i
                P.op(P.pe, mm, reads=list(lhs_bufs) + list(rhs_bufs), writes=banks)
                s = stat_slot()
                fv = pst[:].rearrange("p a b -> p (a b)")
                P.op(P.act, lambda e, fv=fv, s=s: e.activation(out=junk[:], in_=fv, func=AF.Square, accum_out=stat[:, 0, s:s + 1]),
                     reads=banks, writes=[bjunk, bstat[s]])
                c2 = 1.0 / (coef * coef)
                P.op(P.act, lambda e, s=s, c2=c2: e.activation(out=stat[:, 1, s:s + 1], in_=stat[:, 0, s:s + 1], func=AF.Ln,
                                                               scale=c2 / D, bias=EPS * c2), reads=[bstat[s]], writes=[bstat[s]])
                P.op(P.act, lambda e, s=s: e.activation(out=stat[:, 2, s:s + 1], in_=stat[:, 1, s:s + 1], func=AF.Exp, scale=-0.5),
                     reads=[bstat[s]], writes=[bstat[s]])
                P.op(P.dve, lambda e, fv=fv: e.tensor_tensor(out=fv, in0=fv, in1=gain[:], op=ALU.mult),
                     reads=banks + [bgain], writes=banks)
                P.op(P.dve, lambda e, fv=fv, s=s, b=b: e.scalar_tensor_tensor(out=X[:, b, :], in0=fv, scalar=stat[:, 2, s:s + 1],
                                                                              in1=X[:, b, :], op0=ALU.mult, op1=ALU.add),
                     reads=banks + [bstat[s], bX[b]], writes=[bX[b]])
                if final_store is not None:
                    final_store(b)
                if next_n is not None:
                    par = prenorm_a(b)
                    if pend is not None:
                        prenorm_b(pend[0], pend[1], next_n)
                    pend = (b, par)
            if pend is not None:
                pending_tr.append((pend[0], pend[1], next_n))

        pending_tr = []

        def flush_pending():
            while pending_tr:
                b_, par_, n_ = pending_tr.pop(0)
                prenorm_b(b_, par_, n_)

        def ffn(l, i, gain_n, next_n, final_store=None):
            b_load(WdS[l][i], scr[f"Wd{l}{i}"], FC)
            gain_load(gain_n)
            slot_of = {}

            def group(c, g):
                if c not in slot_of:
                    slot_of[c] = a_next(1 if c == 1 else 2)
                slot = slot_of[c]
                pp = ctr["ft"] % 2
                ctr["ft"] += 1
                pst = PS[pp]
                banks = [bpb[2 * pp], bpb[2 * pp + 1]]

                def mm(e):
                    for h in range(2):
                        for k in range(KC):
                            i_ = e.matmul(pst[:, h, :], lhsT=AR[:, slot, h, k, :], rhs=xnT[:, k, g * GT:(g + 1) * GT],
                                          start=(k == 0), stop=(k == KC - 1))
                    return i_
                P.op(P.pe, mm, reads=[bA[slot], bxn[g]], writes=banks)
                P.op(P.act, lambda e: e.activation(out=ffn_t[:, pp, :], in_=pst[:, 0, :], func=AF.Silu),
                     reads=[banks[0]], writes=[bft[pp]])
                P.op(P.dve, lambda e: e.tensor_tensor(
                    out=hT[:, c, g * GT:(g + 1) * GT], in0=ffn_t[:, pp, :], in1=pst[:, 1, :], op=ALU.mult),
                    reads=[bft[pp], banks[1]], writes=[bh[c]])

            group(0, 0)
            group(1, 0)
            flush_pending()
            group(0, 1)
            group(1, 1)
            for c in range(2, FC):
                for g in range(NG):
                    group(c, g)
            project_postnorm(lambda k, b: hT[:, k, b * 128:(b + 1) * 128], bh, FC,
                             lambda k, half: BT[:, k, half * GT:(half + 1) * GT], bB, 0.5, next_n, final_store)

        def row_ap(r):
            return hT[:, r, :] if r < FC else BT[:, 16 + (r - FC), :]

        def row_buf(r):
            return bh[r] if r < FC else bB[16 + (r - FC)]

        def hrow_f32(r):
            return row_ap(r).bitcast(F32)

        def run_pipelined(unit_lists, depth=2):
            items = []
            maxlen = max(len(o) for o in unit_lists)
            step = -(-maxlen // depth)
            for u, ops in enumerate(unit_lists):
                for k, th in enumerate(ops):
                    items.append((u * step + k, u, k, th))
            items.sort(key=lambda t: (t[0], t[1], t[2]))
            for _, _, _, th in items:
                th()

        def recurrent(gain_n, next_n, p=1):
            peng = P.dve if p == 0 else P.pool
            b_load(WoutS, scr["Wout"], 8)
            gain_load(gain_n)
            slots = {}

            def unit_ops(c, g, u):
                par = u % 2
                rs = u % 4
                prs = (u - 1) % 4
                R = [rs * 7 + r for r in range(7)]
                r_xc, r_xcb, r_tr, r_a2, r_ti, r_gt, r_gs = R
                r_a = r_tr
                r_h = r_ti
                xc, tr_, a2, ti, gt, gs = (hrow_f32(r) for r in (r_xc, r_tr, r_a2, r_ti, r_gt, r_gs))
                a_ = tr_
                h_ = ti
                xcb = row_ap(r_xcb)[:, 0:GT]
                pst = PS[par]
                bk = [bpb[2 * par], bpb[2 * par + 1]]
                pst2 = PS[2 + par]
                bk2 = [bpb[4 + 2 * par], bpb[5 + 2 * par]]
                xr = xr_sb[:, rs, :]
                bxr_c = bxr[rs]
                bxr_p = bxr[prs]
                tsl = slice(g * GT, (g + 1) * GT)
                ops = []
                if g == 0:
                    ops.append(lambda: slots.__setitem__(c, a_next()))

                def mm(e):
                    slot = slots[c]
                    for h in range(2):
                        for k in range(KC):
                            i_ = e.matmul(pst[:, h, :], lhsT=AR[:, slot, h, k, :], rhs=xnT[:, k, tsl],
                                          start=(k == 0), stop=(k == KC - 1))
                    return i_
                ops.append(lambda: P.op(P.pe, mm, reads=[bA[slots[c]], bxn[g]], writes=bk))
                ops.append(lambda: P.op(P.act, lambda e: e.activation(out=xr[:, 3:GT + 3], in_=pst[:, 0, :], func=AF.Copy),
                                        reads=[bk[0]], writes=[bxr_c]))
                ops.append(lambda: P.op(P.dve, lambda e: e.tensor_copy(out=gs, in_=pst[:, 1, :]),
                                        reads=[bk[1]], writes=[row_buf(r_gs)]))
                if u == 0:
                    ops.append(flush_pending)
                if g == 0:
                    ops.append(lambda: P.op(P.dve, lambda e: e.tensor_copy(out=xr[:, 0:3], in_=cstate[:, c, :]),
                                            reads=[bcst], writes=[bxr_c]))
                else:
                    def halo():
                        P.op(P.dve, lambda e: e.tensor_copy(out=xr[:, 0:3], in_=xr_sb[:, prs, GT:GT + 3]),
                             reads=[bxr_p], writes=[bxr_c])
                        P.op(P.dve, lambda e: e.tensor_copy(out=cstate[:, c, :], in_=xr[:, GT:GT + 3]),
                             reads=[bxr_c], writes=[bcst])
                    ops.append(halo)
                ops.append(lambda: P.op(P.dve, lambda e: e.tensor_scalar(
                    out=xc, in0=xr[:, 0:GT], scalar1=recfm[:, 0, c:c + 1], scalar2=recfm[:, 4, c:c + 1],
                    op0=ALU.mult, op1=ALU.add), reads=[bxr_c, bconst], writes=[row_buf(r_xc)]))
                for k in range(1, 4):
                    ops.append(lambda k=k: P.op(P.dve, lambda e: e.scalar_tensor_tensor(
                        out=xc, in0=xr[:, k:k + GT], scalar=recfm[:, k, c:c + 1], in1=xc, op0=ALU.mult, op1=ALU.add),
                        reads=[bxr_c, bconst, row_buf(r_xc)], writes=[row_buf(r_xc)]))
                ops.append(lambda: P.op(P.act, lambda e: e.activation(out=xcb, in_=xc, func=AF.Copy),
                                        reads=[row_buf(r_xc)], writes=[row_buf(r_xcb)]))

                def mmg(e):
                    e.matmul(pst2[:, 0, :], lhsT=bd[:, 0, c, :], rhs=xcb, start=True, stop=True)
                    return e.matmul(pst2[:, 1, :], lhsT=bd[:, 1, c, :], rhs=xcb, start=True, stop=True)
                ops.append(lambda: P.op(P.pe, mmg, reads=[bbd, row_buf(r_xcb)], writes=bk2))
                ops.append(lambda: P.op(P.act, lambda e: e.activation(
                    out=tr_, in_=pst2[:, 0, :], func=AF.Tanh, scale=0.5, bias=der[:, 0, c:c + 1]),
                    reads=[bk2[0], bder], writes=[row_buf(r_tr)]))
                ops.append(lambda: P.op(P.act, lambda e: e.activation(
                    out=ti, in_=pst2[:, 1, :], func=AF.Tanh, scale=0.5, bias=der[:, 1, c:c + 1]),
                    reads=[bk2[1], bder], writes=[row_buf(r_ti)]))
                ops.append(lambda: P.op(peng, lambda e: e.tensor_tensor(out=gt, in0=gs, in1=gs, op=ALU.mult),
                                        reads=[row_buf(r_gs)], writes=[row_buf(r_gt)]))
                ops.append(lambda: P.op(peng, lambda e: e.tensor_scalar(out=gt, in0=gt, scalar1=0.044715, scalar2=1.0,
                                                                          op0=ALU.mult, op1=ALU.add),
                                        reads=[row_buf(r_gt)], writes=[row_buf(r_gt)]))
                ops.append(lambda: P.op(peng, lambda e: e.tensor_tensor(out=gt, in0=gt, in1=gs, op=ALU.mult),
                                        reads=[row_buf(r_gt), row_buf(r_gs)], writes=[row_buf(r_gt)]))
                ops.append(lambda: P.op(P.act, lambda e: e.activation(out=gt, in_=gt, func=AF.Tanh, scale=0.7978845608028654),
                                        reads=[row_buf(r_gt)], writes=[row_buf(r_gt)]))
                ops.append(lambda: P.op(P.act, lambda e: e.activation(
                    out=a2, in_=tr_, func=AF.Exp, scale=der[:, 2, c:c + 1], bias=der[:, 2, c:c + 1]),
                    reads=[row_buf(r_tr), bder], writes=[row_buf(r_a2)]))
                ops.append(lambda: P.op(P.act, lambda e: e.activation(
                    out=a_, in_=tr_, func=AF.Exp, scale=der[:, 3, c:c + 1], bias=der[:, 3, c:c + 1]),
                    reads=[row_buf(r_tr), bder], writes=[row_buf(r_a)]))
                ops.append(lambda: P.op(P.act, lambda e: e.activation(out=a2, in_=a2, func=AF.Ln, scale=-1.0, bias=1.000001),
                                        reads=[row_buf(r_a2)], writes=[row_buf(r_a2)]))
                ops.append(lambda: P.op(P.act, lambda e: e.activation(out=a2, in_=a2, func=AF.Exp, scale=0.5),
                                        reads=[row_buf(r_a2)], writes=[row_buf(r_a2)]))
                ops.append(lambda: P.op(P.dve, lambda e: e.scalar_tensor_tensor(out=ti, in0=ti, scalar=1.0, in1=xc,
                                                                                op0=ALU.add, op1=ALU.mult),
                                        reads=[row_buf(r_ti), row_buf(r_xc)], writes=[row_buf(r_ti)]))
                ops.append(lambda: P.op(P.dve, lambda e: e.tensor_tensor(out=a2, in0=a2, in1=ti, op=ALU.mult),
                                        reads=[row_buf(r_a2), row_buf(r_ti)], writes=[row_buf(r_a2)]))
                if g == 0:
                    init = hstate[:, c:c + 1]
                    init_b = bhst
                else:
                    init = hrow_f32(prs * 7 + 4)[:, GT - 1:GT]
                    init_b = row_buf(prs * 7 + 4)
                ops.append(lambda: P.op(P.dve, lambda e: e.tensor_tensor_scan(
                    out=h_, data0=a_, data1=a2, initial=init, op0=ALU.mult, op1=ALU.add),
                    reads=[row_buf(r_a), row_buf(r_a2), init_b], writes=[row_buf(r_h)]))
                if g == 1:
                    ops.append(lambda: P.op(P.dve, lambda e: e.tensor_copy(out=hstate[:, c:c + 1], in_=h_[:, GT - 1:GT]),
                                            reads=[row_buf(r_h)], writes=[bhst]))
                ops.append(lambda: P.op(P.dve, lambda e: e.scalar_tensor_tensor(out=gt, in0=gt, scalar=1.0, in1=gs,
                                                                                op0=ALU.add, op1=ALU.mult),
                                        reads=[row_buf(r_gt), row_buf(r_gs)], writes=[row_buf(r_gt)]))
                ops.append(lambda: P.op(P.dve, lambda e: e.scalar_tensor_tensor(
                    out=BT[:, 8 + c, tsl], in0=h_, scalar=0.25, in1=gt, op0=ALU.mult, op1=ALU.mult),
                    reads=[row_buf(r_h), row_buf(r_gt)], writes=[bB[8 + c]]))
                return ops

            units = []
            u = 0
            for c in range(8):
                for g in range(NG):
                    units.append(unit_ops(c, g, u))
                    u += 1
            run_pipelined(units, depth=4)
            project_postnorm(lambda k, b: BT[:, 8 + k, b * 128:(b + 1) * 128], bB[8:16], 8,
                             lambda k, half: BT[:, k, half * GT:(half + 1) * GT], bB[0:8], 1.0, next_n, None)

        wv_loaded = [False]

        def kv_stage():
            ev = 0
            if not wv_loaded[0]:
                P.dma(P.sp, lambda e: e.dma_start(out=wv[:], in_=WvS.rearrange("(kc p) n -> p kc n", p=128)), dwv,
                      reads=[scr["Wv"]], writes=[bwv])
                wv_loaded[0] = True
            for j in range(2):
                slot = a_next()
                for h in range(2):
                    kvh = 2 * j + h
                    for g in range(NG):
                        bk = ctr["tp"] % 4
                        ctr["tp"] += 1

                        def mm(e, slot=slot, h=h, g=g, bk=bk):
                            for k in range(KC):
                                i_ = e.matmul(bank(bk), lhsT=AR[:, slot, h, k, :], rhs=xnT[:, k, g * GT:(g + 1) * GT],
                                              start=(k == 0), stop=(k == KC - 1))
                            return i_
                        P.op(P.pe, mm, reads=[bA[slot], bxn[g]], writes=[bpb[bk]])
                        flush_pending()
                        dst = kT[:, kvh, 128 + g * GT:128 + (g + 1) * GT]
                        if ev % 2 == 0:
                            P.op(P.act, lambda e, dst=dst, bk=bk: e.activation(out=dst, in_=bank(bk), func=AF.Copy),
                                 reads=[bpb[bk]], writes=[bkT3[kvh][1 + g]])
                        else:
                            P.op(P.dve, lambda e, dst=dst, bk=bk: e.tensor_copy(out=dst, in_=bank(bk)),
                                 reads=[bpb[bk]], writes=[bkT3[kvh][1 + g]])
                        ev += 1
            for b in range(NB):
                bk = 4 + (b % 4)

                def mmv(e, b=b, bk=bk):
                    for k in range(KC):
                        i_ = e.matmul(bank(bk)[:, 0:256], lhsT=xnT[:, k, b * 128:(b + 1) * 128], rhs=wv[:, k, :],
                                      start=(k == 0), stop=(k == KC - 1))
                    return i_
                P.op(P.pe, mmv, reads=[bxn[b // 4], bwv], writes=[bpb[bk]])
                src = bank(bk)[:, 0:256].rearrange("p (h d) -> p h d", h=4)
                if b % 2 == 0:
                    P.op(P.act, lambda e, b=b, src=src: e.activation(out=vpad[:, 1 + b, :, 0:64], in_=src, func=AF.Copy),
                         reads=[bpb[bk]], writes=[bvp[1 + b]])
                    P.op(P.act, lambda e, b=b, src=src: e.activation(out=vpad[:, 1 + b, :, 128:192], in_=src, func=AF.Copy),
                         reads=[bpb[bk]], writes=[bvp[1 + b]])
                else:
                    P.op(P.dve, lambda e, b=b, src=src: e.tensor_copy(out=vpad[:, 1 + b, :, 0:64], in_=src),
                         reads=[bpb[bk]], writes=[bvp[1 + b]])
                    P.op(P.dve, lambda e, b=b, src=src: e.tensor_copy(out=vpad[:, 1 + b, :, 128:192], in_=src),
                         reads=[bpb[bk]], writes=[bvp[1 + b]])

        def attention(p, gain_n, next_n):
            b_load(WoS, scr["Wo"], 8)
            gain_load(gain_n)
            ev = 0
            for j in range(4):
                slot = a_next()
                for h in range(2):
                    ch = 2 * j + h
                    for g in range(NG):
                        bk = ctr["tp"] % 4
                        ctr["tp"] += 1

                        def mm(e, slot=slot, h=h, g=g, bk=bk):
                            for k in range(KC):
                                i_ = e.matmul(bank(bk), lhsT=AR[:, slot, h, k, :], rhs=xnT[:, k, g * GT:(g + 1) * GT],
                                              start=(k == 0), stop=(k == KC - 1))
                            return i_
                        P.op(P.pe, mm, reads=[bA[slot], bxn[g]], writes=[bpb[bk]])
                        flush_pending()
                        dst = hT[:, ch, g * GT:(g + 1) * GT]
                        if ev % 2 == 0:
                            P.op(P.act, lambda e, dst=dst, bk=bk: e.activation(out=dst, in_=bank(bk), func=AF.Copy),
                                 reads=[bpb[bk]], writes=[bh[ch]])
                        else:
                            P.op(P.dve, lambda e, dst=dst, bk=bk: e.tensor_copy(out=dst, in_=bank(bk)),
                                 reads=[bpb[bk]], writes=[bh[ch]])
                        ev += 1

            otp = PS[3]
            otv = otp[:].rearrange("p a b -> p (a b)").rearrange("p (c t) -> p c t", c=8)

            def quad_ops(b, kvh, q):
                par = q % 2
                rs = q % 3
                gb = p * NB + b
                mi = 1 if gb == 0 else 0
                sp_ = PS[par]
                sbk = [bpb[2 * par], bpb[2 * par + 1]]
                sv = sp_[:].rearrange("p a b -> p (a b)").rearrange("p (j k) -> p j k", j=4)
                r0 = 8 + rs * 4
                sm = hT[:, r0:r0 + 2, :].rearrange("p a b -> p (a b)").bitcast(F32).rearrange("p (j k) -> p j k", j=4)
                pe_ = sm
                pn = row_ap(r0 + 2).rearrange("p (j k) -> p j k", j=4)
                pts = row_ap(r0 + 3)
                b_sm = [bh[r0], bh[r0 + 1]]
                b_pe = b_sm
                b_pn = [row_buf(r0 + 2)]
                b_pt = [row_buf(r0 + 3)]
                st = astat[:, rs]
                bst = bastat[rs]
                ptb = 4 + par
                ptv = bank_bf(ptb)
                ops = []

                def mms(e):
                    for jj in range(4):
                        ch = 2 * kvh + jj // 2
                        ph = jj % 2
                        i_ = e.matmul(sv[:, SLOT[jj], :], lhsT=hT[ph * 64:(ph + 1) * 64, ch, b * 128:(b + 1) * 128],
                                      rhs=kT[ph * 64:(ph + 1) * 64, kvh, b * 128:b * 128 + 256], start=True, stop=True)
                    return i_
                ops.append(lambda: P.op(P.pe, mms, reads=[bh[2 * kvh], bh[2 * kvh + 1]] + bkT3[kvh], writes=sbk))
                ops.append(lambda: P.op(P.dve, lambda e: e.scalar_tensor_tensor(
                    out=sm, in0=sv, scalar=0.125, in1=mask[:, mi, :].unsqueeze(1).broadcast_to([128, 4, 256]),
                    op0=ALU.mult, op1=ALU.add), reads=sbk + [bconst], writes=b_sm))
                ops.append(lambda: P.op(P.dve, lambda e: e.tensor_reduce(out=st[:, 0, :], in_=sm, axis=AX.X, op=ALU.max),
                                        reads=b_sm, writes=[bst]))
                ops.append(lambda: P.op(P.dve, lambda e: e.tensor_tensor(out=st[:, 1, :], in0=st[:, 0, :],
                                                                         in1=sinks[:, 4 * kvh:4 * kvh + 4], op=ALU.max),
                                        reads=[bst, bconst], writes=[bst]))
                ops.append(lambda: P.op(P.dve, lambda e: e.tensor_scalar(out=st[:, 2, :], in0=st[:, 1, :], scalar1=-1.0,
                                                                         scalar2=None, op0=ALU.mult), reads=[bst], writes=[bst]))
                ops.append(lambda: P.op(P.dve, lambda e: e.tensor_tensor(out=st[:, 3, :], in0=sinks[:, 4 * kvh:4 * kvh + 4],
                                                                         in1=st[:, 1, :], op=ALU.subtract),
                                        reads=[bst, bconst], writes=[bst]))
                for jj in range(4):
                    ops.append(lambda jj=jj: P.op(P.act, lambda e: e.activation(
                        out=pe_[:, jj, :], in_=sm[:, jj, :], func=AF.Exp, bias=st[:, 2, jj:jj + 1],
                        accum_out=st[:, 4, jj:jj + 1]), reads=b_sm + [bst], writes=b_pe + [bst]))
                ops.append(lambda: P.op(P.act, lambda e: e.activation(out=st[:, 5, :], in_=st[:, 3, :], func=AF.Exp),
                                        reads=[bst], writes=[bst]))
                ops.append(lambda: P.op(P.dve, lambda e: e.tensor_tensor(out=st[:, 6, :], in0=st[:, 4, :], in1=st[:, 5, :],
                                                                         op=ALU.add), reads=[bst], writes=[bst]))
                ops.append(lambda: P.op(P.dve, lambda e: e.reciprocal(out=st[:, 7, :], in_=st[:, 6, :]),
                                        reads=[bst], writes=[bst]))
                ops.append(lambda: P.op(P.dve, lambda e: e.tensor_tensor(
                    out=pn, in0=pe_, in1=st[:, 7, :].unsqueeze(2).broadcast_to([128, 4, 256]), op=ALU.mult),
                    reads=b_pe + [bst], writes=b_pn))

                def trp(e):
                    for jj in range(4):
                        for kb in range(2):
                            idx = jj * 2 + kb
                            i_ = e.transpose(out=ptv[:, idx * 128:(idx + 1) * 128], in_=pn[:, jj, kb * 128:(kb + 1) * 128],
                                             identity=ident[:])
                    return i_
                ops.append(lambda: P.op(P.pe, trp, reads=b_pn + [bident], writes=[bpb[ptb]]))
                ops.append(lambda: P.op(P.act, lambda e: e.activation(out=pts, in_=ptv, func=AF.Copy),
                                        reads=[bpb[ptb]], writes=b_pt))

                def mmo(e):
                    for cc in range(2):
                        ch = 2 * kvh + cc
                        n = 0
                        for hl in range(2):
                            jj = 2 * cc + hl
                            for kb in range(2):
                                idx = SLOT[jj] * 2 + kb
                                i_ = e.matmul(otv[:, ch, :], lhsT=vpad[:, b + kb, kvh, hl * 64:hl * 64 + 128],
                                              rhs=pts[:, idx * 128:(idx + 1) * 128], start=(n == 0), stop=(n == 3))
                                n += 1
                    return i_
                ops.append(lambda: P.op(P.pe, mmo, reads=b_pt + [bvp[b], bvp[b + 1]], writes=[bpb[6], bpb[7]]))
                if kvh == 3:
                    ops.append(lambda: P.op(P.dve, lambda e: e.tensor_copy(out=BT[:, 8:16, b * 128:(b + 1) * 128], in_=otv),
                                            reads=[bpb[6], bpb[7]], writes=bB[8:16]))
                return ops

            quads = []
            q = 0
            for b in range(NB):
                for kvh in range(4):
                    quads.append(quad_ops(b, kvh, q))
                    q += 1
            run_pipelined(quads, depth=3)
            return

        def attention_tail():
            P.op(P.act, lambda e: e.activation(out=kT[:, :, 0:128], in_=kT[:, :, T:T + 128], func=AF.Copy),
                 reads=bkT, writes=bkT)
            P.op(P.act, lambda e: e.activation(out=vpad[:, 0], in_=vpad[:, NB], func=AF.Copy),
                 reads=[bvp[NB]], writes=[bvp[0]])

        def load_x(p, b):
            P.dma(P.sp, lambda e: e.dma_start(out=X[:, b, :], in_=x_d[p * T + b * 128:p * T + (b + 1) * 128, :]),
                  dXl[b], writes=[bX[b]])

        for b in range(NB):
            load_x(0, b)
        P.wait(P.pool, [b_.w for b_ in list(scr.values()) + list(wg00_buf.values()) if b_.w is not None] + [bbd.w, bident.w])

        last_store = [None] * NB
        for p in range(npass):
            new_engine_sems(f"p{p}")

            def store(b, p=p):
                last_store[b] = P.dma(P.sp, lambda e: e.dma_start(
                    out=out_d[p * T + b * 128:p * T + (b + 1) * 128, :], in_=X[:, b, :]), dXs[b], reads=[bX[b]])
                if p + 1 < npass and b >= 1:
                    load_x(p + 1, b - 1)
            ns = 9 if stop_after is None else stop_after
            if ns >= 1:
                prenorm_full(0)
            if ns >= 2:
                ffn(0, 0, 0, 1)
            if ns >= 3:
                recurrent(1, 2, p)
            if ns >= 4:
                ffn(0, 1, 2, 6)
            if ns >= 5:
                kv_stage()
            if ns >= 6:
                prenorm_full(3)
            if ns >= 7:
                ffn(1, 0, 3, 4)
            if ns >= 8:
                attention(p, 4, 5)
                project_postnorm(lambda k, b: BT[:, 8 + k, b * 128:(b + 1) * 128], bB[8:16], 8,
                                 lambda k, half: BT[:, k, half * GT:(half + 1) * GT], bB[0:8], 1.0, 5, None)
                attention_tail()
            if ns >= 9:
                ffn(1, 1, 5, None, final_store=store)
            else:
                for b in range(NB):
                    store(b)
            if p + 1 < npass:
                load_x(p + 1, NB - 1)
        P.wait(P.sp, [t for t in last_store if t is not None])
        P.wait(P.pool, [t for t in last_store if t is not None])
        with nc.Block() as block:
            P.finish(block)
    return nc


def _fm(v):
    return np.ascontiguousarray(np.asarray(v, np.float32).reshape(8, 128).T)


def make_consts():
    ident = np.eye(128, dtype=np.float32)
    qi = np.arange(128)[:, None]
    kj = np.arange(256)[None, :]
    delta = qi + 128 - kj
    valid = (delta >= 0) & (delta < 128)
    m0 = np.where(valid, 0.0, -1e30).astype(np.float32)
    valid1 = valid & (kj >= 128)
    m1 = np.where(valid1, 0.0, -1e30).astype(np.float32)
    mask = np.concatenate([m0, m1], axis=1)
    return ident, np.ascontiguousarray(mask)


def shared_inputs(norms, ffn_w_gate, ffn_w_up, ffn_w_down, a_w_in, a_conv_w, a_conv_b, a_gate_a_w, a_gate_a_b,
                  a_gate_x_w, a_gate_x_b, a_lambda, a_w_out, kv_norm, w_kv, b_w_q, b_sinks, b_w_o):
    f = lambda a: np.ascontiguousarray(np.asarray(a, np.float32))
    norms = f(norms)
    pre = [norms[0, 0], norms[0, 2], norms[0, 4], norms[1, 0], norms[1, 2], norms[1, 4], f(kv_norm)]
    prenorm_fm = np.concatenate([_fm(v) for v in pre], axis=1)
    postnorm = np.stack([norms[0, 1], norms[0, 3], norms[0, 5], norms[1, 1], norms[1, 3], norms[1, 5]])
    cw = f(a_conv_w)[0]
    rec = [cw[0], cw[1], cw[2], cw[3], f(a_conv_b)[0], f(a_gate_a_b)[0], f(a_gate_x_b)[0], f(a_lambda)[0]]
    rec_fm = np.concatenate([_fm(v) for v in rec], axis=1)

    def gate_layout(w):
        w = f(w)[0]
        w = w.reshape(8, 2, 64, 64)
        return np.ascontiguousarray(w.transpose(1, 2, 0, 3))
    ident, mask = make_consts()
    return {
        "ffn_w_gate": f(ffn_w_gate), "ffn_w_up": f(ffn_w_up), "ffn_w_down": f(ffn_w_down),
        "a_w_in": f(a_w_in)[0], "a_w_out": f(a_w_out)[0], "w_kv": f(w_kv), "b_w_q": f(b_w_q)[0], "b_w_o": f(b_w_o)[0],
        "a_gate_a_w": gate_layout(a_gate_a_w), "a_gate_x_w": gate_layout(a_gate_x_w),
        "prenorm_fm": np.ascontiguousarray(prenorm_fm), "postnorm": np.ascontiguousarray(postnorm),
        "rec_fm": np.ascontiguousarray(rec_fm),
        "sinks": np.ascontiguousarray(f(b_sinks)[0].reshape(4, 4)[:, [0, 2, 1, 3]].reshape(16)), "ident": ident, "mask": mask,
    }


_NC_CACHE = {}


def kernel(x, **params):
    x = np.asarray(x, np.float32)
    bsz, seq, _ = x.shape
    shared = shared_inputs(**params)
    if seq not in _NC_CACHE:
        _NC_CACHE[seq] = build_program(seq)
    nc = _NC_CACHE[seq]
    in_maps = []
    for c in range(bsz):
        m = dict(shared)
        m["x"] = np.ascontiguousarray(x[c])
        in_maps.append(m)
    res = run_bass_kernel_spmd(nc, in_maps, core_ids=list(range(bsz)))
    return np.stack([np.asarray(r["out"], np.float32) for r in res.results], axis=0)
```

```python
import contextlib
import os
KDBG = set(os.environ.get('KDBG', '').split(','))
import numpy as np
import concourse.bass as bass
import concourse.mybir as mybir
from concourse.bass_utils import run_bass_kernel_spmd

F32 = mybir.dt.float32
BF16 = mybir.dt.bfloat16
ALU = mybir.AluOpType
AF = mybir.ActivationFunctionType
AX = mybir.AxisListType

D = 1024
KC = 8
DFF = 2816
FC = 22
T = 1024
NB = 8
GT = 512
NG = 2
EPS = 1e-6
NCORES = 8
SLOT = [0, 2, 1, 3]
SEQ = 8192


class Tok:
    __slots__ = ("sem", "val")

    def __init__(self, sem, val):
        self.sem = sem
        self.val = val


class Buf:
    __slots__ = ("name", "w", "r", "excl")

    def __init__(self, name, excl=False):
        self.name = name
        self.w = None
        self.r = []
        self.excl = excl


class DSem:
    __slots__ = ("sem", "cnt")

    def __init__(self, sem):
        self.sem = sem
        self.cnt = 0


class Eng:
    def __init__(self, name):
        self.name = name
        self.ops = []
        self.sem = None
        self.cnt = 0
        self.waited = {}

    def set_sem(self, sem):
        self.sem = sem
        self.cnt = 0


class Prog:
    def __init__(self, nc):
        self.nc = nc
        self.pe = Eng("tensor")
        self.act = Eng("scalar")
        self.dve = Eng("vector")
        self.pool = Eng("gpsimd")
        self.sp = Eng("sync")
        self.engs = [self.pe, self.act, self.dve, self.pool, self.sp]

    def _deps(self, eng, reads, writes, extra=()):
        best = {}

        def add(t):
            k = id(t.sem)
            if k not in best or best[k].val < t.val:
                best[k] = t
        for b in reads:
            if b.w is not None:
                add(b.w)
            if b.excl:
                for t in b.r:
                    if t.sem is not eng.sem:
                        add(t)
        for b in writes:
            if b.w is not None:
                add(b.w)
            for t in b.r:
                add(t)
        for t in extra:
            add(t)
        for k, t in best.items():
            if eng.waited.get(k, 0) < t.val:
                eng.ops.append(("wait", t.sem, t.val))
                eng.waited[k] = t.val

    @staticmethod
    def _mark(tok, reads, writes):
        for b in reads:
            b.r = [t for t in b.r if t.sem is not tok.sem] + [tok]
        for b in writes:
            b.w = tok
            b.r = []

    def op(self, eng, fn, reads=(), writes=(), extra=()):
        self._deps(eng, reads, writes, extra)
        eng.cnt += 1
        tok = Tok(eng.sem, eng.cnt)
        eng.ops.append(("op", fn, eng.sem, 1))
        self._mark(tok, reads, writes)
        return tok

    def dma(self, eng, fn, dsem, reads=(), writes=(), extra=(), deps=True):
        if deps:
            self._deps(eng, reads, writes, extra)
        dsem.cnt += 16
        tok = Tok(dsem.sem, dsem.cnt)
        eng.ops.append(("op", fn, dsem.sem, 16))
        self._mark(tok, reads, writes)
        return tok

    def wait(self, eng, toks):
        self._deps(eng, (), (), toks)

    def finish(self, block):
        def run(e, ops):
            for o in ops:
                if o[0] == "wait":
                    e.wait_ge(o[1], o[2])
                else:
                    ins = o[1](e)
                    ins.then_inc(o[2], o[3])

        pe, act, dve, pool, sp = self.pe, self.act, self.dve, self.pool, self.sp

        @block.tensor
        def _(e):
            run(e, pe.ops)

        @block.scalar
        def _(e):
            run(e, act.ops)

        @block.vector
        def _(e):
            run(e, dve.ops)

        @block.gpsimd
        def _(e):
            run(e, pool.ops)

        @block.sync
        def _(e):
            run(e, sp.ops)


def build_program(S, stop_after=None):
    npass = S // T
    nc = bass.Bass("TRN2", target_bir_lowering=False)

    def din(name, shape, dt=F32):
        return nc.dram_tensor(name, list(shape), dt, kind="ExternalInput").ap()

    def dscr(name, shape, dt=BF16):
        return nc.dram_tensor(name, list(shape), dt, kind="Internal").ap()

    x_d = din("x", [S, D])
    out_d = nc.dram_tensor("out", [S, D], F32, kind="ExternalOutput").ap()
    wg_d = din("ffn_w_gate", [2, 2, D, DFF])
    wu_d = din("ffn_w_up", [2, 2, D, DFF])
    wd_d = din("ffn_w_down", [2, 2, DFF, D])
    win_d = din("a_w_in", [D, 2 * D])
    wout_d = din("a_w_out", [D, D])
    wkv_d = din("w_kv", [D, 512])
    wq_d = din("b_w_q", [D, D])
    wo_d = din("b_w_o", [D, D])
    gaw_d = din("a_gate_a_w", [2, 64, 8, 64])
    gxw_d = din("a_gate_x_w", [2, 64, 8, 64])
    prefm_d = din("prenorm_fm", [128, 7 * 8])
    postn_d = din("postnorm", [6, D])
    recfm_d = din("rec_fm", [128, 8 * 8])
    sinks_d = din("sinks", [16])
    ident_d = din("ident", [128, 128])
    mask_d = din("mask", [128, 2 * 256])

    WgS = [[dscr(f"WgS{l}{i}", [FC, 128, 2, KC, 128]) for i in range(2)] for l in range(2)]
    WdS = [[dscr(f"WdS{l}{i}", [DFF, D]) for i in range(2)] for l in range(2)]
    WinS = dscr("WinS", [8, 128, 2, KC, 128])
    WqS = dscr("WqS", [4, 128, 2, KC, 128])
    WkS = dscr("WkS", [2, 128, 2, KC, 128])
    WoutS = dscr("WoutS", [D, D])
    WoS = dscr("WoS", [D, D])
    WvS = dscr("WvS", [D, 256])

    with contextlib.ExitStack() as es:
        def sb(name, shape, dt):
            return es.enter_context(nc.sbuf_tensor("sb_" + name, list(shape), dt))

        def psum(name, shape, dt):
            return es.enter_context(nc.psum_tensor(name, list(shape), dt))

        def sem(name):
            return es.enter_context(nc.semaphore(name))

        X = sb("X", [128, NB, D], F32)
        xnT = sb("xnT", [128, KC, T], BF16)
        hT = sb("hT", [128, FC, T], BF16)
        BT = sb("BT", [128, FC, D], BF16)
        AR = sb("AR", [128, 3, 2, KC, 128], BF16)
        gain = sb("gain", [128, D], F32)
        kT = sb("kT", [128, 4, T + 128], BF16)
        vpad = sb("vpad", [128, NB + 1, 4, 192], BF16)
        wv = sb("wv", [128, KC, 256], BF16)
        bd = sb("bd", [128, 2, KC, 128], BF16)
        ident = sb("ident", [128, 128], BF16)
        mask = sb("mask", [128, 2, 256], F32)
        mask_bf = sb("mask_bf", [128, 2, 256], BF16)
        prefm = sb("prefm", [128, 7, 8], F32)
        recfm = sb("recfm", [128, 8, 8], F32)
        der = sb("der", [128, 6, 8], F32)
        sinks = sb("sinks", [128, 16], F32)
        half_c = sb("half_c", [128, GT], F32)
        mhalf_c = sb("mhalf_c", [128, 1], F32)
        hstate = sb("hstate", [128, 8], F32)
        cstate = sb("cstate", [128, 8, 3], F32)
        xr_sb = sb("xr_sb", [128, 4, GT + 3], F32)
        xs_bf = sb("xs_bf", [128, 2, D], BF16)
        junk = sb("junk", [128, D], BF16)
        ffn_t = sb("ffn_t", [128, 2, GT], F32)
        stat = sb("stat", [128, 4, 16], F32)
        astat = sb("astat", [128, 4, 8, 4], F32)

        PS = [psum(f"ps{i}", [128, 2, GT], F32) for i in range(4)]

        def bank(i):
            return PS[i // 2][:, i % 2, :]

        def bank_bf(i):
            return PS[i // 2][:, i % 2, :].bitcast(BF16)

        P = Prog(nc)
        bX = [Buf(f"X{b}") for b in range(NB)]
        bxn = [Buf(f"xn{g}") for g in range(NG)]
        bh = [Buf(f"h{r}") for r in range(FC)]
        bB = [Buf(f"B{r}") for r in range(FC)]
        bA = [Buf(f"A{r}") for r in range(3)]
        bgain = Buf("gain")
        bkT = [Buf(f"kT{h}") for h in range(4)]
        bvp = [Buf(f"vpad{i}") for i in range(NB + 1)]
        bwv = Buf("wv")
        bbd = Buf("bd")
        bconst = Buf("const")
        bpb = [Buf(f"pb{i}", excl=True) for i in range(8)]
        bxr = [Buf("xr0"), Buf("xr1"), Buf("xr2"), Buf("xr3")]
        bxs = [Buf("xs0"), Buf("xs1")]
        bjunk = Buf("junk")
        bft = [Buf("ft0"), Buf("ft1")]
        bstat = [Buf(f"st{i}") for i in range(16)]
        bastat = [Buf("as0"), Buf("as1"), Buf("as2"), Buf("as3")]
        bhst = Buf("hstate")
        bcst = Buf("cstate")

        def new_engine_sems(tag):
            for e in P.engs:
                e.set_sem(sem(f"e_{e.name}_{tag}"))
        new_engine_sems("pro")
        dA = [DSem(sem(f"dA{i}")) for i in range(3)]
        dB = [DSem(sem(f"dB{i}")) for i in range(3)]
        dgain = DSem(sem("dgain"))
        dXl = [DSem(sem(f"dXl{b}")) for b in range(NB)]
        dXs = [DSem(sem(f"dXs{b}")) for b in range(NB)]
        dconst = DSem(sem("dconst"))
        dwv = DSem(sem("dwv"))

        P.dma(P.sp, lambda e: e.dma_start(out=mask[:].rearrange("p a b -> p (a b)"), in_=mask_d), dconst, writes=[bconst])
        P.dma(P.sp, lambda e: e.dma_start(out=prefm[:].rearrange("p a b -> p (a b)"), in_=prefm_d), dconst, writes=[bconst], deps=False)
        P.dma(P.sp, lambda e: e.dma_start(out=recfm[:].rearrange("p a b -> p (a b)"), in_=recfm_d), dconst, writes=[bconst], deps=False)
        P.dma(P.sp, lambda e: e.dma_start(out=sinks[:], in_=sinks_d.partition_broadcast(128)), dconst, writes=[bconst], deps=False)
        dident = DSem(sem("dident"))
        bident = Buf("ident")
        P.dma(P.pool, lambda e: e.dma_start(out=ident[:], in_=ident_d), dident, writes=[bident], deps=False)
        P.op(P.dve, lambda e: e.tensor_copy(out=mask_bf[:].rearrange("p a b -> p (a b)"), in_=mask[:].rearrange("p a b -> p (a b)")),
             reads=[bconst], writes=[bconst])
        P.op(P.dve, lambda e: e.memset(vpad[:].rearrange("p a b c -> p (a b c)"), 0.0), writes=bvp)
        P.op(P.dve, lambda e: e.memset(kT[:].rearrange("p a b -> p (a b)"), 0.0), writes=bkT)
        P.op(P.dve, lambda e: e.memset(bd[:].rearrange("p a b c -> p (a b c)"), 0.0), writes=[bbd])
        P.op(P.dve, lambda e: e.memset(hstate[:], 0.0), writes=[bhst])
        P.op(P.dve, lambda e: e.memset(cstate[:].rearrange("p a b -> p (a b)"), 0.0), writes=[bcst])
        bexp = Buf("expc")
        P.op(P.dve, lambda e: e.memset(half_c[:], 0.5), writes=[bexp])
        P.op(P.dve, lambda e: e.memset(mhalf_c[:], -0.5), writes=[bexp])
        dbd = DSem(sem("dbd"))
        for gi, src in enumerate((gaw_d, gxw_d)):
            for half in range(2):
                P.dma(P.pool, lambda e, gi=gi, src=src, half=half: e.dma_start(
                    out=bd[half * 64:(half + 1) * 64, gi, :, half * 64:(half + 1) * 64], in_=src[half]),
                    dbd, writes=[bbd], deps=(gi == 0 and half == 0))
        bder = Buf("der")
        P.op(P.dve, lambda e: e.tensor_scalar(out=der[:, 0, :], in0=recfm[:, 5, :], scalar1=0.5, scalar2=None, op0=ALU.mult),
             reads=[bconst], writes=[bder])
        P.op(P.dve, lambda e: e.tensor_scalar(out=der[:, 1, :], in0=recfm[:, 6, :], scalar1=0.5, scalar2=None, op0=ALU.mult),
             reads=[bconst], writes=[bder])
        P.op(P.act, lambda e: e.activation(out=der[:, 4, :], in_=recfm[:, 7, :], func=AF.Exp, scale=-1.0),
             reads=[bconst], writes=[bder])
        P.op(P.act, lambda e: e.activation(out=der[:, 5, :], in_=der[:, 4, :], func=AF.Ln, bias=1.0),
             reads=[bder], writes=[bder])
        P.op(P.dve, lambda e: e.tensor_scalar(out=der[:, 2, :], in0=der[:, 5, :], scalar1=-8.0, scalar2=None, op0=ALU.mult),
             reads=[bder], writes=[bder])
        P.op(P.dve, lambda e: e.tensor_scalar(out=der[:, 3, :], in0=der[:, 5, :], scalar1=-4.0, scalar2=None, op0=ALU.mult),
             reads=[bder], writes=[bder])

        def cast_group(name, items):
            ds = DSem(sem("dc_" + name))
            b = Buf("scr_" + name)
            for (o, i) in items:
                if 'nocast' in KDBG:
                    continue
                P.dma(P.pool, lambda e, o=o, i=i: e.dma_start(out=o, in_=i), ds, writes=[b], deps=False)
            return b

        def a_items(dst, src, nchunk, col_of):
            it = []
            for c in range(nchunk):
                for h in range(2):
                    c0 = col_of(c, h)
                    it.append((dst[c, :, h], src[:, c0:c0 + 128].rearrange("(kc p) f -> p kc f", p=128)))
            return it

        def plain_items(dst, src, rows, piece):
            return [(dst[r0:r0 + piece, :], src[r0:r0 + piece, :]) for r0 in range(0, rows, piece)]

        scr = {}
        wg00_buf = {}

        def ffn_cast(l, i):
            it = []
            for c in range(FC):
                it.append((WgS[l][i][c, :, 0], wg_d[l, i][:, c * 128:(c + 1) * 128].rearrange("(kc p) f -> p kc f", p=128)))
                it.append((WgS[l][i][c, :, 1], wu_d[l, i][:, c * 128:(c + 1) * 128].rearrange("(kc p) f -> p kc f", p=128)))
            if (l, i) == (0, 0):
                bounds = [0, 3, 8, 15, FC]
                for gi in range(4):
                    bgrp = cast_group(f"Wg00_{gi}", it[2 * bounds[gi]:2 * bounds[gi + 1]])
                    for c in range(bounds[gi], bounds[gi + 1]):
                        wg00_buf[c] = bgrp
                    if gi == 0:
                        scr["Wd00"] = cast_group("Wd00", plain_items(WdS[l][i], wd_d[l, i], DFF, 352))
                scr["Wg00"] = bgrp
                return
            scr[f"Wg{l}{i}"] = cast_group(f"Wg{l}{i}", it)
            scr[f"Wd{l}{i}"] = cast_group(f"Wd{l}{i}", plain_items(WdS[l][i], wd_d[l, i], DFF, 352))

        ffn_cast(0, 0)
        scr["Win"] = cast_group("Win", a_items(WinS, win_d, 8, lambda c, h: h * D + c * 128))
        scr["Wout"] = cast_group("Wout", plain_items(WoutS, wout_d, D, 256))
        ffn_cast(0, 1)
        kit = []
        for j in range(2):
            for h in range(2):
                for d2 in range(2):
                    c0 = (2 * j + h) * 64
                    kit.append((WkS[j, :, h, :, d2 * 64:(d2 + 1) * 64],
                                wkv_d[:, c0:c0 + 64].rearrange("(kc p) f -> p kc f", p=128)))
        scr["Wk"] = cast_group("Wk", kit)
        scr["Wv"] = cast_group("Wv", plain_items(WvS, wkv_d[:, 256:512], D, 256))
        ffn_cast(1, 0)
        scr["Wq"] = cast_group("Wq", a_items(WqS, wq_d, 4, lambda c, h: (2 * c + h) * 128))
        scr["Wo"] = cast_group("Wo", plain_items(WoS, wo_d, D, 256))
        ffn_cast(1, 1)


        a_list = []
        ns_ = 9 if stop_after is None else stop_after
        for p in range(npass):
            if ns_ >= 2:
                for c in range(FC):
                    a_list.append((WgS[0][0][c], wg00_buf[c]))
            if ns_ >= 3:
                for c in range(8):
                    a_list.append((WinS[c], scr["Win"]))
            if ns_ >= 4:
                for c in range(FC):
                    a_list.append((WgS[0][1][c], scr["Wg01"]))
            if ns_ >= 5:
                for c in range(2):
                    a_list.append((WkS[c], scr["Wk"]))
            if ns_ >= 7:
                for c in range(FC):
                    a_list.append((WgS[1][0][c], scr["Wg10"]))
            if ns_ >= 8:
                for c in range(4):
                    a_list.append((WqS[c], scr["Wq"]))
            if ns_ >= 9:
                for c in range(FC):
                    a_list.append((WgS[1][1][c], scr["Wg11"]))
        a_state = {"issued": 0, "next": 0}

        def a_issue_upto(j):
            while a_state["issued"] <= min(j, len(a_list) - 1):
                k = a_state["issued"]
                src, sbuf_ = a_list[k]
                slot = k % 3
                P.dma(P.sp, lambda e, src=src, slot=slot: e.dma_start(out=AR[:, slot], in_=src),
                      dA[slot], reads=[sbuf_], writes=[bA[slot]])
                a_state["issued"] += 1

        def a_next(pref=2):
            j = a_state["next"]
            a_issue_upto(j + pref)
            a_state["next"] += 1
            return j % 3

        def b_load(src, scrbuf, nrows_chunks):
            for g0 in range(0, nrows_chunks, 8):
                n = min(8, nrows_chunks - g0)
                gi = g0 // 8
                P.dma(P.sp, lambda e, g0=g0, n=n: e.dma_start(
                    out=BT[:, g0:g0 + n, :],
                    in_=src[g0 * 128:(g0 + n) * 128, :].rearrange("(fc p) m -> p fc m", p=128)),
                    dB[gi], reads=[scrbuf], writes=bB[g0:g0 + n])

        def gain_load(n):
            P.dma(P.sp, lambda e: e.dma_start(out=gain[:], in_=postn_d[n].partition_broadcast(128)),
                  dgain, writes=[bgain])

        ctr = {"stat": 0, "xs": 0, "tp": 0, "ft": 0, "fp": 0, "ev": 0}

        def stat_slot():
            s = ctr["stat"] % 16
            ctr["stat"] += 1
            return s

        def prenorm_a(b):
            s = stat_slot()
            par = ctr["xs"] % 2
            ctr["xs"] += 1
            P.op(P.act, lambda e: e.activation(out=junk[:], in_=X[:, b, :], func=AF.Square, accum_out=stat[:, 0, s:s + 1]),
                 reads=[bX[b]], writes=[bjunk, bstat[s]])
            P.op(P.act, lambda e: e.activation(out=stat[:, 1, s:s + 1], in_=stat[:, 0, s:s + 1], func=AF.Ln,
                                               scale=1.0 / D, bias=EPS), reads=[bstat[s]], writes=[bstat[s]])
            P.op(P.act, lambda e: e.activation(out=stat[:, 2, s:s + 1], in_=stat[:, 1, s:s + 1], func=AF.Exp, scale=-0.5),
                 reads=[bstat[s]], writes=[bstat[s]])
            P.op(P.act, lambda e: e.activation(out=xs_bf[:, par, :], in_=X[:, b, :], func=AF.Copy, scale=stat[:, 2, s:s + 1]),
                 reads=[bX[b], bstat[s]], writes=[bxs[par]])
            return par

        def prenorm_b(b, par, n):
            if 'nopre_b' in KDBG:
                return
            bk = ctr["tp"] % 4
            ctr["tp"] += 1
            tpv = bank_bf(bk).rearrange("p (k t) -> p k t", k=KC)

            def tr(e):
                for k in range(KC):
                    i = e.transpose(out=tpv[:, k, :], in_=xs_bf[:, par, k * 128:(k + 1) * 128], identity=ident[:])
                return i
            P.op(P.pe, tr, reads=[bxs[par], bident], writes=[bpb[bk]])
            g = b // 4
            P.op(P.dve, lambda e: e.tensor_tensor(out=xnT[:, :, b * 128:(b + 1) * 128], in0=tpv,
                                                  in1=prefm[:, n, :].unsqueeze(2).broadcast_to([128, KC, 128]), op=ALU.mult),
                 reads=[bpb[bk], bconst], writes=[bxn[g]])

        def prenorm_full(n):
            pend = None
            for b in range(NB):
                par = prenorm_a(b)
                if pend is not None:
                    prenorm_b(pend[0], pend[1], n)
                pend = (b, par)
            prenorm_b(pend[0], pend[1], n)

        def project_postnorm(lhs_of, lhs_bufs, nk, rhs_of, rhs_bufs, coef, next_n, final_store):
            pend = None
            for b in range(NB):
                fp = ctr["fp"] % 2
                ctr["fp"] += 1
                pst = PS[2 + fp]
                banks = [bpb[4 + 2 * fp], bpb[5 + 2 * fp]]

                def mm(e, b=b, pst=pst):
                    for half in range(2):
                        for k in range(nk):
                            i = e.matmul(pst[:, half, :], lhsT=lhs_of(k, b), rhs=rhs_of(k, half),
                                         start=(k == 0), stop=(k == nk - 1))
                    return i
                P.op(P.pe, mm, reads=list(lhs_bufs) + list(rhs_bufs), writes=banks)
                s = stat_slot()
                fv = pst[:].rearrange("p a b -> p (a b)")
                P.op(P.act, lambda e, fv=fv, s=s: e.activation(out=junk[:], in_=fv, func=AF.Square, accum_out=stat[:, 0, s:s + 1]),
                     reads=banks, writes=[bjunk, bstat[s]])
                c2 = 1.0 / (coef * coef)
                P.op(P.act, lambda e, s=s, c2=c2: e.activation(out=stat[:, 1, s:s + 1], in_=stat[:, 0, s:s + 1], func=AF.Ln,
                                                               scale=c2 / D, bias=EPS * c2), reads=[bstat[s]], writes=[bstat[s]])
                P.op(P.act, lambda e, s=s: e.activation(out=stat[:, 2, s:s + 1], in_=stat[:, 1, s:s + 1], func=AF.Exp, scale=-0.5),
                     reads=[bstat[s]], writes=[bstat[s]])
                P.op(P.dve, lambda e, fv=fv: e.tensor_tensor(out=fv, in0=fv, in1=gain[:], op=ALU.mult),
                     reads=banks + [bgain], writes=banks)
                P.op(P.dve, lambda e, fv=fv, s=s, b=b: e.scalar_tensor_tensor(out=X[:, b, :], in0=fv, scalar=stat[:, 2, s:s + 1],
                                                                              in1=X[:, b, :], op0=ALU.mult, op1=ALU.add),
                     reads=banks + [bstat[s], bX[b]], writes=[bX[b]])
                if final_store is not None:
                    final_store(b)
                if next_n is not None:
                    par = prenorm_a(b)
                    if pend is not None:
                        prenorm_b(pend[0], pend[1], next_n)
                    pend = (b, par)
            if pend is not None:
                pending_tr.append((pend[0], pend[1], next_n))

        pending_tr = []

        def flush_pending():
            while pending_tr:
                b_, par_, n_ = pending_tr.pop(0)
                prenorm_b(b_, par_, n_)

        def ffn(l, i, gain_n, next_n, final_store=None):
            b_load(WdS[l][i], scr[f"Wd{l}{i}"], FC)
            gain_load(gain_n)
            slot_of = {}

            def group(c, g):
                if c not in slot_of:
                    slot_of[c] = a_next(1 if c == 1 else 2)
                slot = slot_of[c]
                pp = ctr["ft"] % 2
                ctr["ft"] += 1
                pst = PS[pp]
                banks = [bpb[2 * pp], bpb[2 * pp + 1]]

                def mm(e):
                    for h in range(2):
                        for k in range(KC):
                            i_ = e.matmul(pst[:, h, :], lhsT=AR[:, slot, h, k, :], rhs=xnT[:, k, g * GT:(g + 1) * GT],
                                          start=(k == 0), stop=(k == KC - 1))
                    return i_
                P.op(P.pe, mm, reads=[bA[slot], bxn[g]], writes=banks)
                P.op(P.act, lambda e: e.activation(out=ffn_t[:, pp, :], in_=pst[:, 0, :], func=AF.Silu),
                     reads=[banks[0]], writes=[bft[pp]])
                P.op(P.dve, lambda e: e.tensor_tensor(
                    out=hT[:, c, g * GT:(g + 1) * GT], in0=ffn_t[:, pp, :], in1=pst[:, 1, :], op=ALU.mult),
                    reads=[bft[pp], banks[1]], writes=[bh[c]])

            group(0, 0)
            group(1, 0)
            flush_pending()
            group(0, 1)
            group(1, 1)
            for c in range(2, FC):
                for g in range(NG):
                    group(c, g)
            project_postnorm(lambda k, b: hT[:, k, b * 128:(b + 1) * 128], bh, FC,
                             lambda k, half: BT[:, k, half * GT:(half + 1) * GT], bB, 0.5, next_n, final_store)

        def row_ap(r):
            return hT[:, r, :] if r < FC else BT[:, 16 + (r - FC), :]

        def row_buf(r):
            return bh[r] if r < FC else bB[16 + (r - FC)]

        def hrow_f32(r):
            return row_ap(r).bitcast(F32)

        def run_pipelined(unit_lists, depth=2):
            items = []
            maxlen = max(len(o) for o in unit_lists)
            step = -(-maxlen // depth)
            for u, ops in enumerate(unit_lists):
                for k, th in enumerate(ops):
                    items.append((u * step + k, u, k, th))
            items.sort(key=lambda t: (t[0], t[1], t[2]))
            for _, _, _, th in items:
                th()

        def recurrent(gain_n, next_n, p=1):
            peng = P.dve if p == 0 else P.pool
            b_load(WoutS, scr["Wout"], 8)
            gain_load(gain_n)
            slots = {}

            def unit_ops(c, g, u):
                par = u % 2
                rs = u % 4
                prs = (u - 1) % 4
                R = [rs * 7 + r for r in range(7)]
                r_xc, r_xcb, r_tr, r_a2, r_ti, r_gt, r_gs = R
                r_a = r_tr
                r_h = r_ti
                xc, tr_, a2, ti, gt, gs = (hrow_f32(r) for r in (r_xc, r_tr, r_a2, r_ti, r_gt, r_gs))
                a_ = tr_
                h_ = ti
                xcb = row_ap(r_xcb)[:, 0:GT]
                pst = PS[par]
                bk = [bpb[2 * par], bpb[2 * par + 1]]
                pst2 = PS[2 + par]
                bk2 = [bpb[4 + 2 * par], bpb[5 + 2 * par]]
                xr = xr_sb[:, rs, :]
                bxr_c = bxr[rs]
                bxr_p = bxr[prs]
                tsl = slice(g * GT, (g + 1) * GT)
                ops = []
                if g == 0:
                    ops.append(lambda: slots.__setitem__(c, a_next()))

                def mm(e):
                    slot = slots[c]
                    for h in range(2):
                        for k in range(KC):
                            i_ = e.matmul(pst[:, h, :], lhsT=AR[:, slot, h, k, :], rhs=xnT[:, k, tsl],
                                          start=(k == 0), stop=(k == KC - 1))
                    return i_
                ops.append(lambda: P.op(P.pe, mm, reads=[bA[slots[c]], bxn[g]], writes=bk))
                ops.append(lambda: P.op(P.act, lambda e: e.activation(out=xr[:, 3:GT + 3], in_=pst[:, 0, :], func=AF.Copy),
                                        reads=[bk[0]], writes=[bxr_c]))
                ops.append(lambda: P.op(P.dve, lambda e: e.tensor_copy(out=gs, in_=pst[:, 1, :]),
                                        reads=[bk[1]], writes=[row_buf(r_gs)]))
                if u == 0:
                    ops.append(flush_pending)
                if g == 0:
                    ops.append(lambda: P.op(P.dve, lambda e: e.tensor_copy(out=xr[:, 0:3], in_=cstate[:, c, :]),
                                            reads=[bcst], writes=[bxr_c]))
                else:
                    def halo():
                        P.op(P.dve, lambda e: e.tensor_copy(out=xr[:, 0:3], in_=xr_sb[:, prs, GT:GT + 3]),
                             reads=[bxr_p], writes=[bxr_c])
                        P.op(P.dve, lambda e: e.tensor_copy(out=cstate[:, c, :], in_=xr[:, GT:GT + 3]),
                             reads=[bxr_c], writes=[bcst])
                    ops.append(halo)
                ops.append(lambda: P.op(P.dve, lambda e: e.tensor_scalar(
                    out=xc, in0=xr[:, 0:GT], scalar1=recfm[:, 0, c:c + 1], scalar2=recfm[:, 4, c:c + 1],
                    op0=ALU.mult, op1=ALU.add), reads=[bxr_c, bconst], writes=[row_buf(r_xc)]))
                for k in range(1, 4):
                    ops.append(lambda k=k: P.op(P.dve, lambda e: e.scalar_tensor_tensor(
                        out=xc, in0=xr[:, k:k + GT], scalar=recfm[:, k, c:c + 1], in1=xc, op0=ALU.mult, op1=ALU.add),
                        reads=[bxr_c, bconst, row_buf(r_xc)], writes=[row_buf(r_xc)]))
                ops.append(lambda: P.op(P.act, lambda e: e.activation(out=xcb, in_=xc, func=AF.Copy),
                                        reads=[row_buf(r_xc)], writes=[row_buf(r_xcb)]))

                def mmg(e):
                    e.matmul(pst2[:, 0, :], lhsT=bd[:, 0, c, :], rhs=xcb, start=True, stop=True)
                    return e.matmul(pst2[:, 1, :], lhsT=bd[:, 1, c, :], rhs=xcb, start=True, stop=True)
                ops.append(lambda: P.op(P.pe, mmg, reads=[bbd, row_buf(r_xcb)], writes=bk2))
                ops.append(lambda: P.op(P.act, lambda e: e.activation(
                    out=tr_, in_=pst2[:, 0, :], func=AF.Tanh, scale=0.5, bias=der[:, 0, c:c + 1]),
                    reads=[bk2[0], bder], writes=[row_buf(r_tr)]))
                ops.append(lambda: P.op(P.act, lambda e: e.activation(
                    out=ti, in_=pst2[:, 1, :], func=AF.Tanh, scale=0.5, bias=der[:, 1, c:c + 1]),
                    reads=[bk2[1], bder], writes=[row_buf(r_ti)]))
                ops.append(lambda: P.op(peng, lambda e: e.tensor_tensor(out=gt, in0=gs, in1=gs, op=ALU.mult),
                                        reads=[row_buf(r_gs)], writes=[row_buf(r_gt)]))
                ops.append(lambda: P.op(peng, lambda e: e.tensor_scalar(out=gt, in0=gt, scalar1=0.044715, scalar2=1.0,
                                                                          op0=ALU.mult, op1=ALU.add),
                                        reads=[row_buf(r_gt)], writes=[row_buf(r_gt)]))
                ops.append(lambda: P.op(peng, lambda e: e.tensor_tensor(out=gt, in0=gt, in1=gs, op=ALU.mult),
                                        reads=[row_buf(r_gt), row_buf(r_gs)], writes=[row_buf(r_gt)]))
                ops.append(lambda: P.op(P.act, lambda e: e.activation(out=gt, in_=gt, func=AF.Tanh, scale=0.7978845608028654),
                                        reads=[row_buf(r_gt)], writes=[row_buf(r_gt)]))
                ops.append(lambda: P.op(P.act, lambda e: e.activation(
                    out=a2, in_=tr_, func=AF.Exp, scale=der[:, 2, c:c + 1], bias=der[:, 2, c:c + 1]),
                    reads=[row_buf(r_tr), bder], writes=[row_buf(r_a2)]))
                ops.append(lambda: P.op(P.act, lambda e: e.activation(
                    out=a_, in_=tr_, func=AF.Exp, scale=der[:, 3, c:c + 1], bias=der[:, 3, c:c + 1]),
                    reads=[row_buf(r_tr), bder], writes=[row_buf(r_a)]))
                ops.append(lambda: P.op(P.act, lambda e: e.activation(out=a2, in_=a2, func=AF.Ln, scale=-1.0, bias=1.000001),
                                        reads=[row_buf(r_a2)], writes=[row_buf(r_a2)]))
                ops.append(lambda: P.op(P.act, lambda e: e.activation(out=a2, in_=a2, func=AF.Exp, scale=0.5),
                                        reads=[row_buf(r_a2)], writes=[row_buf(r_a2)]))
                ops.append(lambda: P.op(P.dve, lambda e: e.scalar_tensor_tensor(out=ti, in0=ti, scalar=1.0, in1=xc,
                                                                                op0=ALU.add, op1=ALU.mult),
                                        reads=[row_buf(r_ti), row_buf(r_xc)], writes=[row_buf(r_ti)]))
                ops.append(lambda: P.op(P.dve, lambda e: e.tensor_tensor(out=a2, in0=a2, in1=ti, op=ALU.mult),
                                        reads=[row_buf(r_a2), row_buf(r_ti)], writes=[row_buf(r_a2)]))
                if g == 0:
                    init = hstate[:, c:c + 1]
                    init_b = bhst
                else:
                    init = hrow_f32(prs * 7 + 4)[:, GT - 1:GT]
                    init_b = row_buf(prs * 7 + 4)
                ops.append(lambda: P.op(P.dve, lambda e: e.tensor_tensor_scan(
                    out=h_, data0=a_, data1=a2, initial=init, op0=ALU.mult, op1=ALU.add),
                    reads=[row_buf(r_a), row_buf(r_a2), init_b], writes=[row_buf(r_h)]))
                if g == 1:
                    ops.append(lambda: P.op(P.dve, lambda e: e.tensor_copy(out=hstate[:, c:c + 1], in_=h_[:, GT - 1:GT]),
                                            reads=[row_buf(r_h)], writes=[bhst]))
                ops.append(lambda: P.op(P.dve, lambda e: e.scalar_tensor_tensor(out=gt, in0=gt, scalar=1.0, in1=gs,
                                                                                op0=ALU.add, op1=ALU.mult),
                                        reads=[row_buf(r_gt), row_buf(r_gs)], writes=[row_buf(r_gt)]))
                ops.append(lambda: P.op(P.dve, lambda e: e.scalar_tensor_tensor(
                    out=BT[:, 8 + c, tsl], in0=h_, scalar=0.25, in1=gt, op0=ALU.mult, op1=ALU.mult),
                    reads=[row_buf(r_h), row_buf(r_gt)], writes=[bB[8 + c]]))
                return ops

            units = []
            u = 0
            for c in range(8):
                for g in range(NG):
                    units.append(unit_ops(c, g, u))
                    u += 1
            run_pipelined(units, depth=4)
            project_postnorm(lambda k, b: BT[:, 8 + k, b * 128:(b + 1) * 128], bB[8:16], 8,
                             lambda k, half: BT[:, k, half * GT:(half + 1) * GT], bB[0:8], 1.0, next_n, None)

        wv_loaded = [False]

        def kv_stage():
            ev = 0
            if not wv_loaded[0]:
                P.dma(P.sp, lambda e: e.dma_start(out=wv[:], in_=WvS.rearrange("(kc p) n -> p kc n", p=128)), dwv,
                      reads=[scr["Wv"]], writes=[bwv])
                wv_loaded[0] = True
            for j in range(2):
                slot = a_next()
                for h in range(2):
                    kvh = 2 * j + h
                    for g in range(NG):
                        bk = ctr["tp"] % 4
                        ctr["tp"] += 1

                        def mm(e, slot=slot, h=h, g=g, bk=bk):
                            for k in range(KC):
                                i_ = e.matmul(bank(bk), lhsT=AR[:, slot, h, k, :], rhs=xnT[:, k, g * GT:(g + 1) * GT],
                                              start=(k == 0), stop=(k == KC - 1))
                            return i_
                        P.op(P.pe, mm, reads=[bA[slot], bxn[g]], writes=[bpb[bk]])
                        flush_pending()
                        dst = kT[:, kvh, 128 + g * GT:128 + (g + 1) * GT]
                        if ev % 2 == 0:
                            P.op(P.act, lambda e, dst=dst, bk=bk: e.activation(out=dst, in_=bank(bk), func=AF.Copy),
                                 reads=[bpb[bk]], writes=[bkT[kvh]])
                        else:
                            P.op(P.dve, lambda e, dst=dst, bk=bk: e.tensor_copy(out=dst, in_=bank(bk)),
                                 reads=[bpb[bk]], writes=[bkT[kvh]])
                        ev += 1
            for b in range(NB):
                bk = 4 + (b % 4)

                def mmv(e, b=b, bk=bk):
                    for k in range(KC):
                        i_ = e.matmul(bank(bk)[:, 0:256], lhsT=xnT[:, k, b * 128:(b + 1) * 128], rhs=wv[:, k, :],
                                      start=(k == 0), stop=(k == KC - 1))
                    return i_
                P.op(P.pe, mmv, reads=[bxn[b // 4], bwv], writes=[bpb[bk]])
                src = bank(bk)[:, 0:256].rearrange("p (h d) -> p h d", h=4)
                if b % 2 == 0:
                    P.op(P.act, lambda e, b=b, src=src: e.activation(out=vpad[:, 1 + b, :, 0:64], in_=src, func=AF.Copy),
                         reads=[bpb[bk]], writes=[bvp[1 + b]])
                    P.op(P.act, lambda e, b=b, src=src: e.activation(out=vpad[:, 1 + b, :, 128:192], in_=src, func=AF.Copy),
                         reads=[bpb[bk]], writes=[bvp[1 + b]])
                else:
                    P.op(P.dve, lambda e, b=b, src=src: e.tensor_copy(out=vpad[:, 1 + b, :, 0:64], in_=src),
                         reads=[bpb[bk]], writes=[bvp[1 + b]])
                    P.op(P.dve, lambda e, b=b, src=src: e.tensor_copy(out=vpad[:, 1 + b, :, 128:192], in_=src),
                         reads=[bpb[bk]], writes=[bvp[1 + b]])

        def attention(p, gain_n, next_n):
            b_load(WoS, scr["Wo"], 8)
            gain_load(gain_n)
            ev = 0
            for j in range(4):
                slot = a_next()
                for h in range(2):
                    ch = 2 * j + h
                    for g in range(NG):
                        bk = ctr["tp"] % 4
                        ctr["tp"] += 1

                        def mm(e, slot=slot, h=h, g=g, bk=bk):
                            for k in range(KC):
                                i_ = e.matmul(bank(bk), lhsT=AR[:, slot, h, k, :], rhs=xnT[:, k, g * GT:(g + 1) * GT],
                                              start=(k == 0), stop=(k == KC - 1))
                            return i_
                        P.op(P.pe, mm, reads=[bA[slot], bxn[g]], writes=[bpb[bk]])
                        flush_pending()
                        dst = hT[:, ch, g * GT:(g + 1) * GT]
                        if ev % 2 == 0:
                            P.op(P.act, lambda e, dst=dst, bk=bk: e.activation(out=dst, in_=bank(bk), func=AF.Copy),
                                 reads=[bpb[bk]], writes=[bh[ch]])
                        else:
                            P.op(P.dve, lambda e, dst=dst, bk=bk: e.tensor_copy(out=dst, in_=bank(bk)),
                                 reads=[bpb[bk]], writes=[bh[ch]])
                        ev += 1

            otp = PS[3]
            otv = otp[:].rearrange("p a b -> p (a b)").rearrange("p (c t) -> p c t", c=8)

            def quad_ops(b, kvh, q):
                par = q % 2
                rs = q % 3
                gb = p * NB + b
                mi = 1 if gb == 0 else 0
                sp_ = PS[par]
                sbk = [bpb[2 * par], bpb[2 * par + 1]]
                sv = sp_[:].rearrange("p a b -> p (a b)").rearrange("p (j k) -> p j k", j=4)
                r0 = 8 + rs * 4
                sm = hT[:, r0:r0 + 2, :].rearrange("p a b -> p (a b)").bitcast(F32).rearrange("p (j k) -> p j k", j=4)
                pe_ = sm
                pn = row_ap(r0 + 2).rearrange("p (j k) -> p j k", j=4)
                pts = row_ap(r0 + 3)
                b_sm = [bh[r0], bh[r0 + 1]]
                b_pe = b_sm
                b_pn = [row_buf(r0 + 2)]
                b_pt = [row_buf(r0 + 3)]
                st = astat[:, rs]
                bst = bastat[rs]
                ptb = 4 + par
                ptv = bank_bf(ptb)
                ops = []

                def mms(e):
                    for jj in range(4):
                        ch = 2 * kvh + jj // 2
                        ph = jj % 2
                        i_ = e.matmul(sv[:, SLOT[jj], :], lhsT=hT[ph * 64:(ph + 1) * 64, ch, b * 128:(b + 1) * 128],
                                      rhs=kT[ph * 64:(ph + 1) * 64, kvh, b * 128:b * 128 + 256], start=True, stop=True)
                    return i_
                ops.append(lambda: P.op(P.pe, mms, reads=[bh[2 * kvh], bh[2 * kvh + 1], bkT[kvh]], writes=sbk))
                ops.append(lambda: P.op(P.dve, lambda e: e.scalar_tensor_tensor(
                    out=sm, in0=sv, scalar=0.125, in1=mask[:, mi, :].unsqueeze(1).broadcast_to([128, 4, 256]),
                    op0=ALU.mult, op1=ALU.add), reads=sbk + [bconst], writes=b_sm))
                ops.append(lambda: P.op(P.dve, lambda e: e.tensor_reduce(out=st[:, 0, :], in_=sm, axis=AX.X, op=ALU.max),
                                        reads=b_sm, writes=[bst]))
                ops.append(lambda: P.op(P.dve, lambda e: e.tensor_tensor(out=st[:, 1, :], in0=st[:, 0, :],
                                                                         in1=sinks[:, 4 * kvh:4 * kvh + 4], op=ALU.max),
                                        reads=[bst, bconst], writes=[bst]))
                ops.append(lambda: P.op(P.dve, lambda e: e.tensor_scalar(out=st[:, 2, :], in0=st[:, 1, :], scalar1=-1.0,
                                                                         scalar2=None, op0=ALU.mult), reads=[bst], writes=[bst]))
                ops.append(lambda: P.op(P.dve, lambda e: e.tensor_tensor(out=st[:, 3, :], in0=sinks[:, 4 * kvh:4 * kvh + 4],
                                                                         in1=st[:, 1, :], op=ALU.subtract),
                                        reads=[bst, bconst], writes=[bst]))
                for jj in range(4):
                    ops.append(lambda jj=jj: P.op(P.act, lambda e: e.activation(
                        out=pe_[:, jj, :], in_=sm[:, jj, :], func=AF.Exp, bias=st[:, 2, jj:jj + 1],
                        accum_out=st[:, 4, jj:jj + 1]), reads=b_sm + [bst], writes=b_pe + [bst]))
                ops.append(lambda: P.op(P.act, lambda e: e.activation(out=st[:, 5, :], in_=st[:, 3, :], func=AF.Exp),
                                        reads=[bst], writes=[bst]))
                ops.append(lambda: P.op(P.dve, lambda e: e.tensor_tensor(out=st[:, 6, :], in0=st[:, 4, :], in1=st[:, 5, :],
                                                                         op=ALU.add), reads=[bst], writes=[bst]))
                ops.append(lambda: P.op(P.dve, lambda e: e.reciprocal(out=st[:, 7, :], in_=st[:, 6, :]),
                                        reads=[bst], writes=[bst]))
                ops.append(lambda: P.op(P.dve, lambda e: e.tensor_tensor(
                    out=pn, in0=pe_, in1=st[:, 7, :].unsqueeze(2).broadcast_to([128, 4, 256]), op=ALU.mult),
                    reads=b_pe + [bst], writes=b_pn))

                def trp(e):
                    for jj in range(4):
                        for kb in range(2):
                            idx = jj * 2 + kb
                            i_ = e.transpose(out=ptv[:, idx * 128:(idx + 1) * 128], in_=pn[:, jj, kb * 128:(kb + 1) * 128],
                                             identity=ident[:])
                    return i_
                ops.append(lambda: P.op(P.pe, trp, reads=b_pn + [bident], writes=[bpb[ptb]]))
                ops.append(lambda: P.op(P.act, lambda e: e.activation(out=pts, in_=ptv, func=AF.Copy),
                                        reads=[bpb[ptb]], writes=b_pt))

                def mmo(e):
                    for cc in range(2):
                        ch = 2 * kvh + cc
                        n = 0
                        for hl in range(2):
                            jj = 2 * cc + hl
                            for kb in range(2):
                                idx = SLOT[jj] * 2 + kb
                                i_ = e.matmul(otv[:, ch, :], lhsT=vpad[:, b + kb, kvh, hl * 64:hl * 64 + 128],
                                              rhs=pts[:, idx * 128:(idx + 1) * 128], start=(n == 0), stop=(n == 3))
                                n += 1
                    return i_
                ops.append(lambda: P.op(P.pe, mmo, reads=b_pt + [bvp[b], bvp[b + 1]], writes=[bpb[6], bpb[7]]))
                if kvh == 3:
                    ops.append(lambda: P.op(P.act, lambda e: e.activation(out=BT[:, 8:16, b * 128:(b + 1) * 128], in_=otv,
                                                                          func=AF.Copy),
                                            reads=[bpb[6], bpb[7]], writes=bB[8:16]))
                return ops

            quads = []
            q = 0
            for b in range(NB):
                for kvh in range(4):
                    quads.append(quad_ops(b, kvh, q))
                    q += 1
            run_pipelined(quads, depth=3)
            return

        def attention_tail():
            P.op(P.act, lambda e: e.activation(out=kT[:, :, 0:128], in_=kT[:, :, T:T + 128], func=AF.Copy),
                 reads=bkT, writes=bkT)
            P.op(P.act, lambda e: e.activation(out=vpad[:, 0], in_=vpad[:, NB], func=AF.Copy),
                 reads=[bvp[NB]], writes=[bvp[0]])

        def load_x(p, b):
            P.dma(P.sp, lambda e: e.dma_start(out=X[:, b, :], in_=x_d[p * T + b * 128:p * T + (b + 1) * 128, :]),
                  dXl[b], writes=[bX[b]])

        for b in range(NB):
            load_x(0, b)
        P.wait(P.pool, [b_.w for b_ in list(scr.values()) + list(wg00_buf.values()) if b_.w is not None] + [bbd.w, bident.w])

        last_store = [None] * NB
        for p in range(npass):
            new_engine_sems(f"p{p}")

            def store(b, p=p):
                last_store[b] = P.dma(P.sp, lambda e: e.dma_start(
                    out=out_d[p * T + b * 128:p * T + (b + 1) * 128, :], in_=X[:, b, :]), dXs[b], reads=[bX[b]])
                if p + 1 < npass and b >= 1:
                    load_x(p + 1, b - 1)
            ns = 9 if stop_after is None else stop_after
            if ns >= 1:
                prenorm_full(0)
            if ns >= 2:
                ffn(0, 0, 0, 1)
            if ns >= 3:
                recurrent(1, 2, p)
            if ns >= 4:
                ffn(0, 1, 2, 6)
            if ns >= 5:
                kv_stage()
            if ns >= 6:
                prenorm_full(3)
            if ns >= 7:
                ffn(1, 0, 3, 4)
            if ns >= 8:
                attention(p, 4, 5)
                project_postnorm(lambda k, b: BT[:, 8 + k, b * 128:(b + 1) * 128], bB[8:16], 8,
                                 lambda k, half: BT[:, k, half * GT:(half + 1) * GT], bB[0:8], 1.0, 5, None)
                attention_tail()
            if ns >= 9:
                ffn(1, 1, 5, None, final_store=store)
            else:
                for b in range(NB):
                    store(b)
            if p + 1 < npass:
                load_x(p + 1, NB - 1)
        P.wait(P.sp, [t for t in last_store if t is not None])
        P.wait(P.pool, [t for t in last_store if t is not None])
        with nc.Block() as block:
            P.finish(block)
    return nc


def _fm(v):
    return np.ascontiguousarray(np.asarray(v, np.float32).reshape(8, 128).T)


def make_consts():
    ident = np.eye(128, dtype=np.float32)
    qi = np.arange(128)[:, None]
    kj = np.arange(256)[None, :]
    delta = qi + 128 - kj
    valid = (delta >= 0) & (delta < 128)
    m0 = np.where(valid, 0.0, -1e30).astype(np.float32)
    valid1 = valid & (kj >= 128)
    m1 = np.where(valid1, 0.0, -1e30).astype(np.float32)
    mask = np.concatenate([m0, m1], axis=1)
    return ident, np.ascontiguousarray(mask)


def shared_inputs(norms, ffn_w_gate, ffn_w_up, ffn_w_down, a_w_in, a_conv_w, a_conv_b, a_gate_a_w, a_gate_a_b,
                  a_gate_x_w, a_gate_x_b, a_lambda, a_w_out, kv_norm, w_kv, b_w_q, b_sinks, b_w_o):
    f = lambda a: np.ascontiguousarray(np.asarray(a, np.float32))
    norms = f(norms)
    pre = [norms[0, 0], norms[0, 2], norms[0, 4], norms[1, 0], norms[1, 2], norms[1, 4], f(kv_norm)]
    prenorm_fm = np.concatenate([_fm(v) for v in pre], axis=1)
    postnorm = np.stack([norms[0, 1], norms[0, 3], norms[0, 5], norms[1, 1], norms[1, 3], norms[1, 5]])
    cw = f(a_conv_w)[0]
    rec = [cw[0], cw[1], cw[2], cw[3], f(a_conv_b)[0], f(a_gate_a_b)[0], f(a_gate_x_b)[0], f(a_lambda)[0]]
    rec_fm = np.concatenate([_fm(v) for v in rec], axis=1)

    def gate_layout(w):
        w = f(w)[0]
        w = w.reshape(8, 2, 64, 64)
        return np.ascontiguousarray(w.transpose(1, 2, 0, 3))
    ident, mask = make_consts()
    return {
        "ffn_w_gate": f(ffn_w_gate), "ffn_w_up": f(ffn_w_up), "ffn_w_down": f(ffn_w_down),
        "a_w_in": f(a_w_in)[0], "a_w_out": f(a_w_out)[0], "w_kv": f(w_kv), "b_w_q": f(b_w_q)[0], "b_w_o": f(b_w_o)[0],
        "a_gate_a_w": gate_layout(a_gate_a_w), "a_gate_x_w": gate_layout(a_gate_x_w),
        "prenorm_fm": np.ascontiguousarray(prenorm_fm), "postnorm": np.ascontiguousarray(postnorm),
        "rec_fm": np.ascontiguousarray(rec_fm),
        "sinks": np.ascontiguousarray(f(b_sinks)[0].reshape(4, 4)[:, [0, 2, 1, 3]].reshape(16)), "ident": ident, "mask": mask,
    }


_NC_CACHE = {}


def kernel(x, **params):
    x = np.asarray(x, np.float32)
    bsz, seq, _ = x.shape
    shared = shared_inputs(**params)
    if seq not in _NC_CACHE:
        _NC_CACHE[seq] = build_program(seq)
    nc = _NC_CACHE[seq]
    in_maps = []
    for c in range(bsz):
        m = dict(shared)
        m["x"] = np.ascontiguousarray(x[c])
        in_maps.append(m)
    res = run_bass_kernel_spmd(nc, in_maps, core_ids=list(range(bsz)))
    return np.stack([np.asarray(r["out"], np.float32) for r in res.results], axis=0)
```
